# Optimizing a Trainium2 kernel written in Bass

```python
import math
import jax, jax.numpy as jnp
from jax import lax
import numpy as np

D_MODEL = 2048
BATCH = 4
SEQ = 2048
DEPTH = 1

HEAD_DIM = 128
Q_BLOCK = 128
NSA_HEADS = 8
NSA_KV_GROUPS = 2
NSA_HPG = NSA_HEADS // NSA_KV_GROUPS
CMP_LEN = 32
CMP_STRIDE = 16
SEL_BLOCK = 64
N_SEL = 16
WINDOW = 512
SEL_Q_CHUNK = 64
N_NSA_BRANCH = 3
DIFF_HEADS = 4
DIFF_V_DIM = 2 * HEAD_DIM
NUM_BUCKETS = 32
MAX_DISTANCE = 128
N_BIAS_HEADS = NSA_HEADS + DIFF_HEADS
D_FF = 5632
CONV_WIDTH = 3
N_BRANCHES = 2
EPS = 1e-6
NEG = -1e30

NSA_Q_COLS = NSA_HEADS * HEAD_DIM
NSA_KV_COLS = N_NSA_BRANCH * 2 * NSA_KV_GROUPS * HEAD_DIM
NSA_GATE_COLS = NSA_HEADS * N_NSA_BRANCH
DIFF_Q_COLS = DIFF_HEADS * 2 * HEAD_DIM
DIFF_K_COLS = DIFF_HEADS * 2 * HEAD_DIM
DIFF_V_COLS = DIFF_HEADS * DIFF_V_DIM
MERGE_GATE_COLS = N_BRANCHES * D_MODEL
OFF_NSA_KV = NSA_Q_COLS
OFF_NSA_G = OFF_NSA_KV + NSA_KV_COLS
OFF_DQ = OFF_NSA_G + NSA_GATE_COLS
OFF_DK = OFF_DQ + DIFF_Q_COLS
OFF_DV = OFF_DK + DIFF_K_COLS
OFF_MG = OFF_DV + DIFF_V_COLS
IN_COLS = OFF_MG + MERGE_GATE_COLS

kernel_name = "hybrid_nsa_diffattn_convffn_block"


def rms_norm(x, gain):
    xf = x.astype(jnp.float32)
    y = xf * lax.rsqrt(jnp.mean(xf * xf, axis=-1, keepdims=True) + EPS)
    return (y * gain.astype(jnp.float32)).astype(x.dtype)


def t5_bucket(dist):
    n = jnp.maximum(jnp.asarray(dist, jnp.int32), 0)
    max_exact = NUM_BUCKETS // 2
    nf = jnp.maximum(n, max_exact).astype(jnp.float32)
    large = max_exact + (jnp.log(nf / max_exact) / math.log(MAX_DISTANCE / max_exact) * (NUM_BUCKETS - max_exact)).astype(jnp.int32)
    large = jnp.minimum(large, NUM_BUCKETS - 1)
    return jnp.where(n < max_exact, n, large)


def masked_softmax(logits, mask):
    p = jax.nn.softmax(jnp.where(mask, logits, NEG), axis=-1)
    return jnp.where(mask, p, 0.0)


def nsa_mixer(q, kv, gate_logits, cmp_pe, cmp_w1, cmp_w2, q_gain, k_gain, bias_tab):
    B, T = q.shape[0], q.shape[1]
    G, HPG, dk = NSA_KV_GROUPS, NSA_HPG, HEAD_DIM
    scale = dk ** -0.5
    q = rms_norm(q, q_gain)
    qg = q.reshape(B, T, G, HPG, dk).transpose(0, 2, 3, 1, 4)
    bt = bias_tab[:, :NSA_HEADS]
    t_idx = np.arange(T)

    nc = (T - CMP_LEN) // CMP_STRIDE + 1
    starts = np.arange(nc) * CMP_STRIDE
    tok = starts[:, None] + np.arange(CMP_LEN)[None, :]

    def compress(z, i):
        blk = z[:, tok] + cmp_pe[i][:, None, :]
        blk = blk.transpose(0, 1, 3, 2, 4).reshape(B, nc, G, CMP_LEN * dk)
        return jax.nn.gelu(blk @ cmp_w1[i]) @ cmp_w2[i]

    kc = rms_norm(compress(kv[:, :, 0, 0], 0), k_gain[0])
    vc = compress(kv[:, :, 0, 1], 1)
    blk_end = starts + CMP_LEN - 1
    cmask = blk_end[None, :] <= t_idx[:, None]
    cbias = bt[t5_bucket(t_idx[:, None] - blk_end[None, :])].transpose(2, 0, 1).reshape(G, HPG, T, nc)
    s_c = jnp.einsum('bghtd,bcgd->bghtc', qg, kc).astype(jnp.float32) * scale + cbias
    p_cmp = masked_softmax(s_c, cmask)
    o_cmp = jnp.einsum('bghtc,bcgd->bghtd', p_cmp.astype(vc.dtype), vc)

    ns = T // SEL_BLOCK
    n_sel = min(N_SEL, ns)
    sel_start = np.arange(ns) * SEL_BLOCK
    overlap = np.clip(np.minimum(starts[:, None] + CMP_LEN, sel_start[None, :] + SEL_BLOCK)
                      - np.maximum(starts[:, None], sel_start[None, :]), 0, None) / CMP_STRIDE
    imp = jnp.einsum('bghtc,cs->bgts', p_cmp, jnp.asarray(overlap, jnp.float32))
    cur = t_idx // SEL_BLOCK
    j = np.arange(ns)
    causal_blk = j[None, :] <= cur[:, None]
    forced = (j[None, :] == 0) | (j[None, :] == cur[:, None]) | (j[None, :] == cur[:, None] - 1)
    score = jnp.where(forced, 1e4, jnp.where(causal_blk, imp, -1e4))
    _, sel_idx = lax.top_k(score, n_sel)

    k_s = rms_norm(kv[:, :, 1, 0], k_gain[1])
    v_s = kv[:, :, 1, 1]
    kb = k_s.reshape(B, ns, SEL_BLOCK, G, dk).transpose(0, 3, 1, 2, 4)
    vb = v_s.reshape(B, ns, SEL_BLOCK, G, dk).transpose(0, 3, 1, 2, 4)
    nq = T // SEL_Q_CHUNK
    q_ch = qg.reshape(B, G, HPG, nq, SEL_Q_CHUNK, dk).transpose(3, 0, 1, 2, 4, 5)
    idx_ch = sel_idx.reshape(B, G, nq, SEL_Q_CHUNK, n_sel).transpose(2, 0, 1, 3, 4)
    t_ch = jnp.asarray(t_idx.reshape(nq, SEL_Q_CHUNK), jnp.int32)
    bias_grp = bt.reshape(NUM_BUCKETS, G, HPG)
    b_ar = jnp.arange(B)[:, None, None, None]
    g_ar = jnp.arange(G)[None, :, None, None]
    n_keys = n_sel * SEL_BLOCK

    def sel_chunk(args):
        qc, ic, tc = args
        kg = kb[b_ar, g_ar, ic]
        vg = vb[b_ar, g_ar, ic]
        pos = ic[..., None] * SEL_BLOCK + jnp.arange(SEL_BLOCK)
        dist = tc[None, None, :, None, None] - pos
        bias = bias_grp[t5_bucket(dist), g_ar[..., None]].transpose(0, 1, 5, 2, 3, 4)
        s = jnp.einsum('bghqd,bgqnkd->bghqnk', qc, kg).astype(jnp.float32) * scale + bias
        s = s.reshape(B, G, HPG, SEL_Q_CHUNK, n_keys)
        mask = (dist >= 0).reshape(B, G, 1, SEL_Q_CHUNK, n_keys)
        p = masked_softmax(s, mask)
        return jnp.einsum('bghqk,bgqkd->bghqd', p.astype(vg.dtype), vg.reshape(B, G, SEL_Q_CHUNK, n_keys, dk))

    o_slc = lax.map(sel_chunk, (q_ch, idx_ch, t_ch))
    o_slc = o_slc.transpose(1, 2, 3, 0, 4, 5).reshape(B, G, HPG, T, dk)

    k_w = rms_norm(kv[:, :, 2, 0], k_gain[2])
    v_w = kv[:, :, 2, 1]
    nb = T // Q_BLOCK
    nwb = WINDOW // Q_BLOCK
    slab_len = (nwb + 1) * Q_BLOCK

    def to_slab(z):
        zb = z.transpose(0, 2, 1, 3).reshape(B, G, nb, Q_BLOCK, dk)
        zp = jnp.pad(zb, ((0, 0), (0, 0), (nwb, 0), (0, 0), (0, 0)))
        return jnp.concatenate([zp[:, :, s:s + nb] for s in range(nwb + 1)], axis=3)

    k_slab, v_slab = to_slab(k_w), to_slab(v_w)
    rq = np.arange(Q_BLOCK)
    ks = np.arange(slab_len)
    wdist = nwb * Q_BLOCK + rq[:, None] - ks[None, :]
    kpos = (np.arange(nb)[:, None] - nwb) * Q_BLOCK + ks[None, :]
    wmask = ((wdist >= 0) & (wdist < WINDOW))[None] & (kpos >= 0)[:, None, :]
    wbias = bt[t5_bucket(wdist)].transpose(2, 0, 1).reshape(G, HPG, 1, Q_BLOCK, slab_len)
    qw = qg.reshape(B, G, HPG, nb, Q_BLOCK, dk)
    s_w = jnp.einsum('bghnqd,bgnkd->bghnqk', qw, k_slab).astype(jnp.float32) * scale + wbias
    p_w = masked_softmax(s_w, wmask)
    o_win = jnp.einsum('bghnqk,bgnkd->bghnqd', p_w.astype(v_slab.dtype), v_slab).reshape(B, G, HPG, T, dk)

    gt = jax.nn.sigmoid(gate_logits.astype(jnp.float32)).astype(o_win.dtype)
    gt = gt.reshape(B, T, G, HPG, N_NSA_BRANCH).transpose(0, 2, 3, 1, 4)[..., None, :]
    o = gt[..., 0] * o_cmp + gt[..., 1] * o_slc + gt[..., 2] * o_win
    return o.transpose(0, 3, 1, 2, 4).reshape(B, T, NSA_HEADS * dk)


def diff_attention(q, k, v, q_gain, k_gain, lam_q, lam_k, subln_gain, bias_tab, lam_init):
    B, T = q.shape[0], q.shape[1]
    scale = HEAD_DIM ** -0.5
    qh = rms_norm(q, q_gain).transpose(0, 2, 3, 1, 4)
    kh = rms_norm(k, k_gain).transpose(0, 2, 3, 1, 4)
    vh = v.transpose(0, 2, 1, 3)
    lq = lam_q.astype(jnp.float32)
    lk = lam_k.astype(jnp.float32)
    lam = jnp.exp(jnp.sum(lq[0] * lk[0])) - jnp.exp(jnp.sum(lq[1] * lk[1])) + lam_init
    dbt = bias_tab[:, NSA_HEADS:]
    outs = []
    for i in range(T // Q_BLOCK):
        L = (i + 1) * Q_BLOCK
        qi = qh[:, :, :, i * Q_BLOCK:L]
        dist = (i * Q_BLOCK + np.arange(Q_BLOCK))[:, None] - np.arange(L)[None, :]
        bias = dbt[t5_bucket(dist)].transpose(2, 0, 1)
        s = jnp.einsum('bhmqd,bhmkd->bhmqk', qi, kh[:, :, :, :L]).astype(jnp.float32) * scale + bias[None, :, None]
        p = masked_softmax(s, dist >= 0)
        a = p[:, :, 0] - lam * p[:, :, 1]
        outs.append(jnp.einsum('bhqk,bhkd->bhqd', a.astype(vh.dtype), vh[:, :, :L]))
    o = jnp.concatenate(outs, axis=2)
    o = rms_norm(o, subln_gain) * (1.0 - lam_init)
    return o.transpose(0, 2, 1, 3).reshape(B, T, DIFF_HEADS * DIFF_V_DIM)


def setup_inputs(seed: int = 0) -> dict:
    key = jax.random.key(seed)
    ks = jax.random.split(key, 26)
    f32 = jnp.float32

    def nrm(k, shape, s):
        return jax.random.normal(k, shape, f32) * s

    def gain(k, shape):
        return 1.0 + 0.05 * jax.random.normal(k, shape, f32)

    D, F, dk, L = D_MODEL, D_FF, HEAD_DIM, CMP_LEN
    return {
        "x": nrm(ks[0], (BATCH, SEQ, D), 1.0),
        "c": nrm(ks[1], (BATCH, D), 1.0),
        "w_ada": nrm(ks[2], (DEPTH, D, 6 * D), D ** -0.5),
        "b_ada": nrm(ks[3], (DEPTH, 6 * D), 0.02),
        "norm1_gain": gain(ks[4], (DEPTH, D)),
        "norm2_gain": gain(ks[5], (DEPTH, D)),
        "w_in": nrm(ks[6], (DEPTH, D, IN_COLS), D ** -0.5),
        "nsa_q_gain": gain(ks[7], (DEPTH, dk)),
        "nsa_k_gain": gain(ks[8], (DEPTH, N_NSA_BRANCH, dk)),
        "cmp_pe": nrm(ks[9], (DEPTH, 2, L, dk), 0.2),
        "cmp_w1": nrm(ks[10], (DEPTH, 2, L * dk, dk), (L * dk) ** -0.5),
        "cmp_w2": nrm(ks[11], (DEPTH, 2, dk, dk), dk ** -0.5),
        "diff_q_gain": gain(ks[12], (DEPTH, dk)),
        "diff_k_gain": gain(ks[13], (DEPTH, dk)),
        "diff_lambda_q": nrm(ks[14], (DEPTH, 2, dk), 0.1),
        "diff_lambda_k": nrm(ks[15], (DEPTH, 2, dk), 0.1),
        "diff_subln_gain": gain(ks[16], (DEPTH, DIFF_V_DIM)),
        "w_nsa_out": nrm(ks[17], (DEPTH, NSA_Q_COLS, D), NSA_Q_COLS ** -0.5),
        "w_diff_out": nrm(ks[18], (DEPTH, DIFF_V_COLS, D), DIFF_V_COLS ** -0.5),
        "w_o": nrm(ks[19], (DEPTH, D, D), D ** -0.5),
        "w_ffn_up": nrm(ks[20], (DEPTH, D, 2 * F), D ** -0.5),
        "ffn_conv_w": nrm(ks[21], (DEPTH, CONV_WIDTH, 2 * F), CONV_WIDTH ** -0.5),
        "ffn_conv_b": nrm(ks[22], (DEPTH, 2 * F), 0.02),
        "w_ffn_down": nrm(ks[23], (DEPTH, F, D), F ** -0.5),
        "rel_bias": nrm(ks[24], (NUM_BUCKETS, N_BIAS_HEADS), 0.5),
    }


def reference(x, c, w_ada, b_ada, norm1_gain, norm2_gain, w_in, nsa_q_gain, nsa_k_gain, cmp_pe, cmp_w1, cmp_w2,
              diff_q_gain, diff_k_gain, diff_lambda_q, diff_lambda_k, diff_subln_gain, w_nsa_out, w_diff_out, w_o,
              w_ffn_up, ffn_conv_w, ffn_conv_b, w_ffn_down, rel_bias):
    B, T, D = x.shape
    for l in range(DEPTH):
        lam_init = 0.8 - 0.6 * math.exp(-0.3 * l)
        mod = jax.nn.silu(c) @ w_ada[l] + b_ada[l]
        sh1, sc1, g1, sh2, sc2, g2 = [m[:, None, :] for m in jnp.split(mod, 6, axis=-1)]

        h = rms_norm(x, norm1_gain[l]) * (1.0 + sc1) + sh1
        proj = h @ w_in[l]
        nsa_q = proj[..., :OFF_NSA_KV].reshape(B, T, NSA_HEADS, HEAD_DIM)
        nsa_kv = proj[..., OFF_NSA_KV:OFF_NSA_G].reshape(B, T, N_NSA_BRANCH, 2, NSA_KV_GROUPS, HEAD_DIM)
        nsa_g = proj[..., OFF_NSA_G:OFF_DQ].reshape(B, T, NSA_HEADS, N_NSA_BRANCH)
        d_q = proj[..., OFF_DQ:OFF_DK].reshape(B, T, DIFF_HEADS, 2, HEAD_DIM)
        d_k = proj[..., OFF_DK:OFF_DV].reshape(B, T, DIFF_HEADS, 2, HEAD_DIM)
        d_v = proj[..., OFF_DV:OFF_MG].reshape(B, T, DIFF_HEADS, DIFF_V_DIM)
        merge_g = jax.nn.sigmoid(proj[..., OFF_MG:].astype(jnp.float32)).astype(x.dtype).reshape(B, T, N_BRANCHES, D)

        y_nsa = nsa_mixer(nsa_q, nsa_kv, nsa_g, cmp_pe[l], cmp_w1[l], cmp_w2[l], nsa_q_gain[l], nsa_k_gain[l], rel_bias) @ w_nsa_out[l]
        y_diff = diff_attention(d_q, d_k, d_v, diff_q_gain[l], diff_k_gain[l], diff_lambda_q[l], diff_lambda_k[l],
                                diff_subln_gain[l], rel_bias, lam_init) @ w_diff_out[l]
        mixed = (merge_g[:, :, 0] * y_nsa + merge_g[:, :, 1] * y_diff) @ w_o[l]
        x = x + g1 * mixed

        h2 = rms_norm(x, norm2_gain[l]) * (1.0 + sc2) + sh2
        u = h2 @ w_ffn_up[l]
        up = jnp.pad(u, ((0, 0), (CONV_WIDTH - 1, 0), (0, 0)))
        cw = ffn_conv_w[l]
        conv = ffn_conv_b[l] + sum(cw[k] * up[:, k:k + T] for k in range(CONV_WIDTH))
        a, val = jnp.split(conv, 2, axis=-1)
        x = x + g2 * ((jax.nn.silu(a) * val) @ w_ffn_down[l])
    return x
```

```python
import os
import math
import numpy as np
from contextlib import ExitStack
import concourse.bass as bass
import concourse.mybir as mybir
from concourse.bass_utils import run_bass_kernel_spmd

F32 = mybir.dt.float32
BF16 = mybir.dt.bfloat16
ALU = mybir.AluOpType
AF = mybir.ActivationFunctionType
AX = mybir.AxisListType

D = 2048
T = 2048
DFF = 5632
NQT = 9
QT0 = 7
NQ = NQT * 128
Q0 = QT0 * 128
OFF_NSA_KV = 1024
OFF_NSA_G = 1024 + 1536
OFF_DQ = OFF_NSA_G + 24
OFF_DK = OFF_DQ + 1024
OFF_DV = OFF_DK + 1024
OFF_MG = OFF_DV + 1024
IN_COLS = OFF_MG + 4096
NEGV = -30000.0
EPS = 1e-6
UL = 4096


class Res:
    __slots__ = ("name", "w", "r", "dsem", "dcount")

    def __init__(self, name):
        self.name = name
        self.w = None
        self.r = {}
        self.dsem = None
        self.dcount = 0


class Eng:
    def __init__(self, name, e, sem):
        self.name = name
        self.e = e
        self.sem = sem
        self.count = 0
        self.seen = {}


class Kern:
    def __init__(self, nc, stack):
        self.nc = nc
        self.stack = stack
        self.nsem = 0
        self.pe = Eng("pe", nc.tensor, self.newsem("s_pe"))
        self.act = Eng("act", nc.scalar, self.newsem("s_act"))
        self.dve = Eng("dve", nc.vector, self.newsem("s_dve"))
        self.pool = Eng("pool", nc.gpsimd, self.newsem("s_pool"))
        self.sp = Eng("sp", nc.sync, self.newsem("s_sp"))
        self.engines = [self.pe, self.act, self.dve, self.pool, self.sp]
        self.dres = []
        self.nwaits = 0
        self.ninst = 0
        self.freesems = []

    def newsem(self, name):
        self.nsem += 1
        return self.stack.enter_context(self.nc.semaphore(name))

    def res(self, name):
        return Res(name)

    def _wait(self, E, tok):
        sem, val = tok
        key = id(sem)
        if E.seen.get(key, 0) < val:
            E.e.wait_ge(sem, val)
            E.seen[key] = val
            self.nwaits += 1

    def _needs(self, E, reads, writes, skip_sem=None):
        for r in reads:
            if r.w is not None and r.w[0] is not skip_sem:
                self._wait(E, r.w)
        for w in writes:
            if w.w is not None and w.w[0] is not skip_sem:
                self._wait(E, w.w)
            for sem, val in list(w.r.values()):
                if sem is not skip_sem:
                    self._wait(E, (sem, val))

    def _mark(self, tok, reads, writes):
        sem, val = tok
        for r in reads:
            old = r.r.get(id(sem))
            if old is None or old[1] < val:
                r.r[id(sem)] = (sem, val)
        for w in writes:
            w.w = tok
            w.r = {}

    def op(self, E, fn, reads=(), writes=(), inc=True):
        skip = E.sem if E is self.pe else None
        self._needs(E, reads, writes, skip_sem=skip)
        inst = fn()
        self.ninst += 1
        if inc:
            E.count += 1
            inst.then_inc(E.sem, 1)
            tok = (E.sem, E.count)
        else:
            tok = (E.sem, E.count + 1)
        self._mark(tok, reads, writes)
        return inst

    def dma(self, E, out, in_, reads=(), writes=(), semres=None, **kw):
        if semres is None:
            semres = writes[0] if writes else reads[0]
        if semres.dsem is None:
            semres.dsem = self.newsem("d_" + semres.name)
            self.dres.append(semres)
        self._needs(E, reads, writes, skip_sem=semres.dsem)
        inst = E.e.dma_start(out=out, in_=in_, **kw)
        self.ninst += 1
        semres.dcount += 16
        inst.then_inc(semres.dsem, 16)
        tok = (semres.dsem, semres.dcount)
        self._mark(tok, reads, writes)
        return inst

    def barrier(self):
        toks = []
        for E in self.engines:
            if E.count > 0:
                toks.append((E.sem, E.count))
        for r in self.dres:
            if r.dcount > 0:
                toks.append((r.dsem, r.dcount))
        for E in self.engines:
            for t in toks:
                if t[0] is E.sem and E is self.pe:
                    continue
                self._wait(E, t)


def _dram_ap(t, offset, ap):
    return bass.AP(tensor=t.tensor, offset=offset, ap=ap)


def build_program(stage=99, debug=False):
    nc = bass.Bass("TRN2", target_bir_lowering=False)
    dt_in = lambda name, shape: nc.dram_tensor(name, list(shape), F32, kind="ExternalInput").ap()
    xctx = dt_in("xctx", [T, D])
    c_l = dt_in("c_l", [128, 16])
    w_ada = dt_in("w_ada", [D, 6 * D])
    b_ada = dt_in("b_ada", [1, 6 * D])
    g1_l = dt_in("g1_l", [128, 16])
    g2_l = dt_in("g2_l", [128, 16])
    w_in = dt_in("w_in", [D, IN_COLS])
    qkg = dt_in("qkg", [128, 8])
    cmp_pe = dt_in("cmp_pe", [2, 128, 32])
    cmp_w1 = dt_in("cmp_w1", [2, 4096, 128])
    cmp_w2 = dt_in("cmp_w2", [2, 128, 128])
    lam_qk = dt_in("lam_qk", [1, 512])
    subln = dt_in("subln", [1, 256])
    w_nsa_out = dt_in("w_nsa_out", [1024, D])
    w_diff_out = dt_in("w_diff_out", [1024, D])
    w_o = dt_in("w_o", [D, D])
    w_up = dt_in("w_up", [D, 2 * DFF])
    conv_l = dt_in("conv_l", [128, 88, 4])
    w_down = dt_in("w_down", [DFF, D])
    rel_bias = dt_in("rel_bias", [32, 12])
    oh_tab = dt_in("oh_tab", [33, UL])
    cmasks = dt_in("cmasks", [128, 8])
    selmask = dt_in("selmask", [128, NQT, 64])
    e_mat = dt_in("e_mat", [32, T])
    ovl = dt_in("ovl", [127, 32])
    out = nc.dram_tensor("out", [1024, D], F32, kind="ExternalOutput").ap()
    dbg = {}

    def dbg_out(name, shape):
        dbg[name] = nc.dram_tensor("dbg_" + name, list(shape), F32, kind="ExternalOutput").ap()
        return dbg[name]

    x1_scr = nc.dram_tensor("x1_scr", [NQ, D], F32, kind="Internal").ap()
    u_scr = nc.dram_tensor("u_scr", [12, UL], BF16, kind="Internal").ap()
    g_scr = nc.dram_tensor("g_scr", [2, D], F32, kind="Internal").ap()

    with ExitStack() as top:
        K = Kern(nc, top)
        pe, act, dve, pool, sp = K.pe, K.act, K.dve, K.pool, K.sp
        V = nc.vector
        A = nc.scalar
        G = nc.gpsimd
        PE = nc.tensor

        def sbuf(st, name, shape, dt):
            t = st.enter_context(nc.sbuf_tensor(name, list(shape), dt))
            return t, K.res(name)

        def psum(st, name, shape, dt):
            t = st.enter_context(nc.psum_tensor(name, list(shape), dt))
            return t, K.res(name)

        r_out = K.res("out")
        r_x1scr = K.res("x1scr")
        r_uscr = K.res("uscr")

        ident_f, r_identf = sbuf(top, "ident_f", [128, 128], F32)
        ident, r_ident = sbuf(top, "ident", [128, 128], BF16)
        jmat, r_jmat = sbuf(top, "jmat", [128, 128], BF16)
        j127, r_j127 = sbuf(top, "j127", [128, 128], BF16)
        ones_bf, r_onesbf = sbuf(top, "ones_bf", [128, 128], BF16)
        ones_f, r_onesf = sbuf(top, "ones_f", [128, 128], F32)
        modcol, r_modcol = sbuf(top, "modcol", [128, 96], F32)
        gcol, r_gcol = sbuf(top, "gcol", [128, 32], F32)
        r_gscr = K.res("gscr")
        cm, r_cm = sbuf(top, "cm", [128, 8], F32)
        qkg_t, r_qkg = sbuf(top, "qkg_t", [128, 8], F32)
        tmpf, r_tmpf = sbuf(top, "tmpf", [128, 128], F32)
        scb, r_scb = sbuf(top, "scb", [128, 16], BF16)
        nlam, r_nlam = sbuf(top, "nlam", [128, 1], F32)
        sublbc, r_sublbc = sbuf(top, "sublbc", [128, 256], F32)
        gl, r_gl = sbuf(top, "gl", [128, 32], F32)

        K.dma(sp, cm[:], cmasks, writes=[r_cm])
        K.dma(sp, qkg_t[:], qkg, writes=[r_qkg])
        K.op(pool, lambda: G.memset(ident_f[:], 0.0), writes=[r_identf])
        K.op(pool, lambda: G.affine_select(out=ident_f[:], in_=ident_f[:], pattern=[[-1, 128]], compare_op=ALU.not_equal,
                                           fill=1.0, base=0, channel_multiplier=1), reads=[r_identf], writes=[r_identf])
        K.op(dve, lambda: V.tensor_copy(out=ident[:], in_=ident_f[:]), reads=[r_identf], writes=[r_ident])
        K.op(pool, lambda: G.memset(tmpf[:], 0.0), writes=[r_tmpf])
        K.op(pool, lambda: G.affine_select(out=tmpf[:], in_=tmpf[:], pattern=[[1, 128]], compare_op=ALU.not_equal,
                                           fill=1.0, base=-127, channel_multiplier=1), reads=[r_tmpf], writes=[r_tmpf])
        K.op(dve, lambda: V.tensor_copy(out=jmat[:], in_=tmpf[:]), reads=[r_tmpf], writes=[r_jmat])
        K.op(pool, lambda: G.memset(tmpf[:], 0.0), reads=[r_tmpf], writes=[r_tmpf])
        K.op(pool, lambda: G.affine_select(out=tmpf[:], in_=tmpf[:], pattern=[[1, 128]], compare_op=ALU.not_equal,
                                           fill=1.0, base=-126, channel_multiplier=1), reads=[r_tmpf], writes=[r_tmpf])
        K.op(dve, lambda: V.tensor_copy(out=j127[:], in_=tmpf[:]), reads=[r_tmpf], writes=[r_j127])
        K.op(dve, lambda: V.memset(ones_bf[:], 1.0), writes=[r_onesbf])
        K.op(dve, lambda: V.memset(ones_f[:], 1.0), writes=[r_onesf])
        sc_ = 128.0 ** -0.5
        K.op(dve, lambda: V.tensor_scalar(out=qkg_t[:, 0:1], in0=qkg_t[:, 0:1], scalar1=sc_, scalar2=None, op0=ALU.mult),
             reads=[r_qkg], writes=[r_qkg])
        K.op(dve, lambda: V.tensor_scalar(out=qkg_t[:, 4:5], in0=qkg_t[:, 4:5], scalar1=sc_, scalar2=None, op0=ALU.mult),
             reads=[r_qkg], writes=[r_qkg])

        with ExitStack() as st:
            cl, r_cl = sbuf(st, "cl", [128, 16], F32)
            wb = [sbuf(st, "wada%d" % i, [128, 16, 512], BF16) for i in range(2)]
            brow, r_brow = sbuf(st, "brow", [1, 512], F32)
            mrow = [sbuf(st, "mrow%d" % i, [1, 512], F32) for i in range(2)]
            pm = [psum(st, "pm%d" % i, [1, 512], F32) for i in range(2)]
            pc = [psum(st, "pc%d" % i, [128, 4], F32) for i in range(2)]
            K.dma(sp, cl[:], c_l, writes=[r_cl])
            K.dma(sp, gl[:, 0:16], g1_l, writes=[r_gl])
            K.dma(sp, gl[:, 16:32], g2_l, writes=[r_gl])
            K.op(act, lambda: A.activation(out=scb[:], in_=cl[:], func=AF.Silu), reads=[r_cl], writes=[r_scb])
            pu0 = [psum(st, "pu0_%d" % i, [128, 512], F32) for i in range(2)]
            pu0c = [0]

            def pu0_next():
                t_ = pu0[pu0c[0] % 2]
                pu0c[0] += 1
                return t_
            su = st
            rba, r_rba = sbuf(su, "rba", [33, 12], F32)
            b31, r_b31 = sbuf(su, "b31", [33, 12], F32)
            oht, r_oht = sbuf(su, "oht", [33, UL], F32)
            ub, r_ub = sbuf(su, "ub", [12, UL], BF16)
            K.op(dve, lambda: V.memset(rba[:], NEGV), writes=[r_rba])
            K.dma(sp, rba[0:32, :], rel_bias, writes=[r_rba])
            K.dma(sp, b31[0:32, :], _dram_ap(rel_bias, 31 * 12, [[0, 32], [1, 12]]), writes=[r_b31])
            K.dma(sp, oht[:], oh_tab, writes=[r_oht])
            K.op(dve, lambda: V.tensor_tensor(out=rba[0:32, :], in0=rba[0:32, :], in1=b31[0:32, :], op=ALU.subtract),
                 reads=[r_rba, r_b31], writes=[r_rba])
            for ch in range(UL // 512):
                pt_, r_pt_ = pu0_next()
                K.op(pe, lambda: PE.matmul(pt_[0:12, :], lhsT=rba[:, :], rhs=oht[:, ch * 512:(ch + 1) * 512], start=True,
                                           stop=True), reads=[r_rba, r_oht], writes=[r_pt_])
                K.op(dve, lambda: V.tensor_copy(out=ub[:, ch * 512:(ch + 1) * 512], in_=pt_[0:12, :]), reads=[r_pt_],
                     writes=[r_ub])
            K.dma(sp, u_scr, ub[:], reads=[r_ub], writes=[r_uscr], semres=r_ub)

            lamt, r_lamt = sbuf(st, "lamt", [1, 512], F32)
            lamw, r_lamw = sbuf(st, "lamw", [1, 8], F32)
            subl, r_subl = sbuf(st, "subl", [1, 256], F32)
            K.dma(sp, lamt[:], lam_qk, writes=[r_lamt])
            K.dma(sp, subl[:], subln, writes=[r_subl])
            K.op(dve, lambda: V.tensor_tensor(out=lamt[:, 0:256], in0=lamt[:, 0:256], in1=lamt[:, 256:512], op=ALU.mult),
                 reads=[r_lamt], writes=[r_lamt])
            K.op(dve, lambda: V.tensor_reduce(out=lamw[:, 0:2], in_=lamt[:, 0:256].rearrange("p (a b) -> p a b", b=128),
                                              axis=AX.X, op=ALU.add), reads=[r_lamt], writes=[r_lamw])
            K.op(act, lambda: A.activation(out=lamw[:, 2:4], in_=lamw[:, 0:2], func=AF.Exp), reads=[r_lamw], writes=[r_lamw])
            K.op(dve, lambda: V.tensor_tensor(out=lamw[:, 4:5], in0=lamw[:, 3:4], in1=lamw[:, 2:3], op=ALU.subtract),
                 reads=[r_lamw], writes=[r_lamw])
            K.op(dve, lambda: V.tensor_scalar(out=lamw[:, 4:5], in0=lamw[:, 4:5], scalar1=-0.2, scalar2=None, op0=ALU.add),
                 reads=[r_lamw], writes=[r_lamw])
            pt_, r_pt_ = pu0_next()
            K.op(pe, lambda: PE.matmul(pt_[:, 0:1], lhsT=ones_f[0:1, :], rhs=lamw[0:1, 4:5], start=True, stop=True),
                 reads=[r_onesf, r_lamw], writes=[r_pt_])
            K.op(dve, lambda: V.tensor_copy(out=nlam[:], in_=pt_[:, 0:1]), reads=[r_pt_], writes=[r_nlam])
            pt_, r_pt_ = pu0_next()
            K.op(pe, lambda: PE.matmul(pt_[:, 0:256], lhsT=ones_f[0:1, :], rhs=subl[0:1, :], start=True, stop=True),
                 reads=[r_onesf, r_subl], writes=[r_pt_])
            K.op(dve, lambda: V.tensor_scalar(out=sublbc[:], in0=pt_[:, 0:256], scalar1=0.8, scalar2=None, op0=ALU.mult),
                 reads=[r_pt_], writes=[r_sublbc])

            w_ada_v = w_ada.rearrange("(kc p) n -> p kc n", p=128)
            for ch in range(8):
                wt, r_wt = wb[ch % 2]
                K.dma(pool, wt[:], w_ada_v[:, :, ch * 512:(ch + 1) * 512], writes=[r_wt])
                K.dma(sp, brow[:], b_ada[:, ch * 512:(ch + 1) * 512], writes=[r_brow])
                pmt, r_pm = pm[ch % 2]
                for kc in range(16):
                    K.op(pe, lambda kc=kc: PE.matmul(pmt[:], lhsT=scb[:, kc:kc + 1], rhs=wt[:, kc, :], start=(kc == 0),
                                                     stop=(kc == 15)), reads=[r_scb, r_wt], writes=[r_pm], inc=(kc == 15))
                mr, r_mr = mrow[ch % 2]
                K.op(dve, lambda: V.tensor_tensor(out=mr[:], in0=pmt[:], in1=brow[:], op=ALU.add),
                     reads=[r_pm, r_brow], writes=[r_mr])
                pct, r_pc = pc[ch % 2]
                for j in range(4):
                    K.op(pe, lambda j=j: PE.matmul(pct[:, j:j + 1], lhsT=mr[0:1, j * 128:(j + 1) * 128], rhs=ones_f[0:1, 0:1],
                                                   start=True, stop=True), reads=[r_mr, r_onesf], writes=[r_pc], inc=(j == 3))
                K.op(dve, lambda: V.tensor_copy(out=modcol[:, ch * 4:(ch + 1) * 4], in_=pct[:]), reads=[r_pc], writes=[r_modcol])
            K.op(dve, lambda: V.scalar_tensor_tensor(out=gcol[:, 0:16], in0=modcol[:, 16:32], scalar=1.0, in1=gl[:, 0:16],
                                                     op0=ALU.add, op1=ALU.mult), reads=[r_modcol, r_gl], writes=[r_gcol])
            K.barrier()
        if debug:
            d = dbg_out("modcol", [128, 96])
            K.dma(sp, d, modcol[:], reads=[r_modcol], writes=[r_out], semres=r_modcol)

        def norm_to_featmajor(st, src_ap_fn, ntiles, dstT, r_dstT, gc0, shc0, tok0, tag, src_reads=()):
            xs = [sbuf(st, "%s_x%d" % (tag, i), [128, D], F32) for i in range(8)]
            xn = [sbuf(st, "%s_xn%d" % (tag, i), [128, D], BF16) for i in range(8)]
            junk, r_junk = sbuf(st, tag + "_junk", [128, D], BF16)
            sss = [sbuf(st, "%s_ss%d" % (tag, i), [128, 8], F32) for i in range(2)]
            pt = [psum(st, "%s_pt%d" % (tag, i), [128, 512], BF16) for i in range(4)]
            ngroups = (ntiles + 3) // 4
            ev = [0]

            def load_a(g):
                nt = min(4, ntiles - g * 4)
                for i in range(nt):
                    xt, r_xt = xs[(g % 2) * 4 + i]
                    K.dma(sp, xt[:], src_ap_fn(g * 4 + i), reads=list(src_reads), writes=[r_xt])

            def stage_a(g):
                nt = min(4, ntiles - g * 4)
                ss, r_ss = sss[g % 2]
                for i in range(nt):
                    xt, r_xt = xs[(g % 2) * 4 + i]
                    K.op(act, lambda: A.activation(out=junk[:], in_=xt[:], func=AF.Square, accum_out=ss[:, i:i + 1]),
                         reads=[r_xt], writes=[r_junk, r_ss])
                K.op(act, lambda: A.activation(out=ss[:, 4:4 + nt], in_=ss[:, 0:nt], func=AF.Sqrt, scale=1.0 / D, bias=EPS),
                     reads=[r_ss], writes=[r_ss])
                K.op(dve, lambda: V.reciprocal(out=ss[:, 4:4 + nt], in_=ss[:, 4:4 + nt]), reads=[r_ss], writes=[r_ss])
                for i in range(nt):
                    xt, r_xt = xs[(g % 2) * 4 + i]
                    xnt, r_xnt = xn[(g % 2) * 4 + i]
                    K.op(dve, lambda: V.tensor_scalar(out=xnt[:], in0=xt[:], scalar1=ss[:, 4 + i:5 + i], scalar2=None,
                                                      op0=ALU.mult), reads=[r_xt, r_ss], writes=[r_xnt])

            def stage_b(g):
                nt = min(4, ntiles - g * 4)
                for kc in range(16):
                    ptt, r_pt = pt[kc % 4]
                    for i in range(nt):
                        xnt, r_xnt = xn[(g % 2) * 4 + i]
                        K.op(pe, lambda: PE.transpose(out=ptt[:, i * 128:(i + 1) * 128], in_=xnt[:, kc * 128:(kc + 1) * 128],
                                                      identity=ident[:]), reads=[r_xnt, r_ident], writes=[r_pt],
                             inc=(i == nt - 1))
                    t0 = tok0 + g * 512
                    if ev[0] % 2 == 0:
                        K.op(dve, lambda: V.tensor_scalar(out=dstT[:, kc, t0:t0 + nt * 128], in0=ptt[:, 0:nt * 128],
                                                          scalar1=gcol[:, gc0 + kc:gc0 + kc + 1],
                                                          scalar2=modcol[:, shc0 + kc:shc0 + kc + 1], op0=ALU.mult, op1=ALU.add),
                             reads=[r_pt, r_gcol, r_modcol], writes=[r_dstT])
                    else:
                        K.op(act, lambda: A.activation(out=dstT[:, kc, t0:t0 + nt * 128], in_=ptt[:, 0:nt * 128], func=AF.Identity,
                                                       scale=gcol[:, gc0 + kc:gc0 + kc + 1],
                                                       bias=modcol[:, shc0 + kc:shc0 + kc + 1]),
                             reads=[r_pt, r_gcol, r_modcol], writes=[r_dstT])
                    ev[0] += 1

            load_a(0)
            if ngroups > 1:
                load_a(1)
            stage_a(0)
            for g in range(ngroups):
                stage_b(g)
                if g + 1 < ngroups:
                    stage_a(g + 1)
                if g + 2 < ngroups:
                    load_a(g + 2)

        with ExitStack() as stB:
            hT, r_hT = sbuf(stB, "hT", [128, 16, T], BF16)
            with ExitStack() as st:
                norm_to_featmajor(st, lambda i: xctx[i * 128:(i + 1) * 128, :], 16, hT, r_hT, 0, 0, 0, "n1")
                K.barrier()
            if debug and stage <= 1:
                d = dbg_out("hT", [128, 16 * T])
                hf, r_hf = sbuf(stB, "hf", [128, T], F32)
                for kc in range(16):
                    K.op(dve, lambda: V.tensor_copy(out=hf[:], in_=hT[:, kc, :]), reads=[r_hT], writes=[r_hf])
                    K.dma(sp, d[:, kc * T:(kc + 1) * T], hf[:], reads=[r_hf], writes=[r_out], semres=r_hf)
            if stage <= 1:
                K.barrier()
                return nc, dbg

            oT_nsa, r_oTn = sbuf(stB, "oT_nsa", [128, 8, NQ], BF16)
            oT_diff, r_oTd = sbuf(stB, "oT_diff", [128, 8, NQ], BF16)
            w_in_v = w_in.rearrange("(kc p) n -> p kc n", p=128)

            with ExitStack() as st:
                wbufs = [sbuf(st, "wb%d" % i, [128, 4096], BF16) for i in range(3)]
                wctr = [0]

                def wnext():
                    t = wbufs[wctr[0] % 3]
                    wctr[0] += 1
                    return t

                def wload_in(c0, ncols, dst3, r_w):
                    K.dma(pool, dst3, w_in_v[:, :, c0:c0 + ncols], writes=[r_w])

                def _kvcol(g_, br, kv):
                    return OFF_NSA_KV + ((br * 2 + kv) * 2 + g_) * 128
                wlist = []
                if stage >= 2:
                    for h_ in range(4):
                        wlist += [(OFF_DQ + h_ * 256, 256), (OFF_DK + h_ * 256, 256), (OFF_DV + h_ * 256, 256)]
                if stage >= 3:
                    wlist.append((OFF_NSA_G, 24))
                    for g_ in range(2):
                        wlist += [(_kvcol(g_, 0, 0), 128), (_kvcol(g_, 0, 1), 128)]
                        for br_ in (1, 2):
                            wlist += [(_kvcol(g_, br_, 0), 128), (_kvcol(g_, br_, 1), 128)]
                        wlist += [((4 * g_) * 128, 256), ((4 * g_ + 2) * 128, 256)]
                wstate = {"emitted": 0, "cur": 0, "released": -1}
                wloaded = {}

                def _ws_emit():
                    while wstate["emitted"] < len(wlist) and wstate["emitted"] - 3 <= wstate["released"]:
                        j = wstate["emitted"]
                        c0_, nc_ = wlist[j]
                        t_, r_ = wbufs[j % 3]
                        v3 = t_[:, 0:16 * nc_].rearrange("p (k c) -> p k c", c=nc_)
                        wload_in(c0_, nc_, v3, r_)
                        wloaded[j] = (v3, r_)
                        wstate["emitted"] += 1

                def ws_next(c0_expect):
                    i = wstate["cur"]
                    _ws_emit()
                    assert i in wloaded, (i, wstate)
                    assert wlist[i][0] == c0_expect, (i, wlist[i], c0_expect)
                    wstate["cur"] += 1
                    return wloaded.pop(i)

                def ws_release():
                    wstate["released"] = wstate["cur"] - 1
                    _ws_emit()

                sqb = [sbuf(st, "sqb%d" % i, [128, 512], BF16) for i in range(2)]
                rsb = [sbuf(st, "rsb%d" % i, [128, 512], F32) for i in range(2)]
                nctr = [0]
                ps_a = [psum(st, "ps_a%d" % i, [128, 512], F32) for i in range(2)]
                ps_b = [psum(st, "ps_b%d" % i, [128, 512], F32) for i in range(5)]
                ps_t, r_ps_t = psum(st, "ps_t", [128, 1024], BF16)
                pctr = [0]

                ps_rot = list(ps_a) + [ps_b[2], ps_b[3]]

                def ps_next():
                    t = ps_rot[pctr[0] % len(ps_rot)]
                    pctr[0] += 1
                    return t

                def qknorm(ps_ap, r_ps, n, gcol, dst_ap, r_dst, view=None):
                    i = nctr[0] % 2
                    nctr[0] += 1
                    sq, r_sq = sqb[i]
                    rs, r_rs = rsb[i]
                    pss, r_pss = ps_b[4]
                    K.op(act, lambda: A.activation(out=sq[:, 0:n], in_=ps_ap, func=AF.Square), reads=[r_ps], writes=[r_sq])
                    K.op(pe, lambda: PE.matmul(pss[:, 0:n], lhsT=ones_bf[:], rhs=sq[:, 0:n], start=True, stop=True),
                         reads=[r_sq, r_onesbf], writes=[r_pss])
                    K.op(act, lambda: A.activation(out=rs[:, 0:n], in_=pss[:, 0:n], func=AF.Ln, scale=1.0 / 128, bias=EPS),
                         reads=[r_pss], writes=[r_rs])
                    K.op(act, lambda: A.activation(out=rs[:, 0:n], in_=rs[:, 0:n], func=AF.Exp, scale=-0.5),
                         reads=[r_rs], writes=[r_rs])
                    vw = view if view is not None else (lambda a: a)
                    K.op(dve, lambda: V.scalar_tensor_tensor(out=dst_ap, in0=vw(ps_ap), scalar=gcol, in1=vw(rs[:, 0:n]),
                                                             op0=ALU.mult, op1=ALU.mult),
                         reads=[r_ps, r_rs, r_qkg], writes=[r_dst])

                def projT(ps_ap, r_ps, w3, r_w, c0, t0, n):
                    for kc in range(16):
                        K.op(pe, lambda: PE.matmul(ps_ap, lhsT=w3[:, kc, c0:c0 + 128], rhs=hT[:, kc, t0:t0 + n],
                                                   start=(kc == 0), stop=(kc == 15)), reads=[r_w, r_hT], writes=[r_ps],
                             inc=(kc == 15))

                QCH = [(Q0, 512), (Q0 + 512, 512), (Q0 + 1024, 128)]

                if stage >= 2:
                  with ExitStack() as sd:
                    kTd = [sbuf(sd, "kTd%d" % m, [128, T], BF16) for m in range(2)]
                    qTd = [sbuf(sd, "qTd%d" % m, [128, NQ], BF16) for m in range(2)]
                    vaug, r_vaug = sbuf(sd, "vaugd", [128, 16, 258], BF16)
                    bd, r_bd = sbuf(sd, "bd", [128, 256], BF16)
                    dslots = []
                    for s_ in range(2):
                        dslots.append({
                            "ptd": sbuf(sd, "ptd%d" % s_, [128, 16 * 256], BF16),
                            "od": sbuf(sd, "od%d" % s_, [128, 256], F32),
                            "odb": sbuf(sd, "odb%d" % s_, [128, 256], BF16),
                            "junk": sbuf(sd, "junkd%d" % s_, [128, 256], F32),
                            "sm": sbuf(sd, "smd%d" % s_, [128, 8], F32),
                            "po": [ps_b[2 * s_], ps_b[2 * s_ + 1]],
                            "r_pst": K.res("pst%d" % s_),
                            "tcol": s_ * 256,
                        })
                    s_tiles = [ps_a[0], ps_a[1], ps_b[4]]
                    sctr = [0]
                    bgw = [sbuf(sd, "bgw%d" % i, [128, 16, 256], BF16) for i in range(2)]
                    bgb = [sbuf(sd, "bgb%d" % i, [1, 256], F32) for i in range(2)]
                    bgm = [sbuf(sd, "bgm%d" % i, [1, 256], F32) for i in range(2)]
                    w_ada_v2 = w_ada.rearrange("(kc p) n -> p kc n", p=128)
                    bgst = {"loaded": 0, "done": 0}
                    bgps = [ps_a[0], ps_a[1]]
                    NBG = 32

                    def bg_load():
                        c = bgst["loaded"]
                        if c >= NBG:
                            return
                        col0 = 4096 + c * 256
                        K.dma(pool, bgw[c % 2][0][:], w_ada_v2[:, :, col0:col0 + 256], writes=[bgw[c % 2][1]])
                        K.dma(sp, bgb[c % 2][0][:], b_ada[:, col0:col0 + 256], writes=[bgb[c % 2][1]])
                        bgst["loaded"] += 1

                    def bg_finish(c):
                        mr_, r_mr_ = bgm[c % 2]
                        pS, r_pS = bgps[c % 2]
                        for j in range(2):
                            K.op(pe, lambda: PE.matmul(pS[:, 256 + j:257 + j], lhsT=mr_[0:1, j * 128:(j + 1) * 128], rhs=ones_f[0:1, 0:1],
                                                       start=True, stop=True), reads=[r_mr_, r_onesf], writes=[r_pS], inc=(j == 1))
                        K.op(dve, lambda: V.tensor_copy(out=modcol[:, 32 + 2 * c:34 + 2 * c], in_=pS[:, 256:258]), reads=[r_pS],
                             writes=[r_modcol])

                    def bg_step():
                        c = bgst["done"]
                        if c > NBG:
                            return
                        bgst["done"] += 1
                        if c >= 1:
                            bg_finish(c - 1)
                        if c == NBG:
                            return
                        wt_, r_wt_ = bgw[c % 2]
                        br_, r_br_ = bgb[c % 2]
                        mr_, r_mr_ = bgm[c % 2]
                        pS, r_pS = bgps[c % 2]
                        for kc in range(16):
                            K.op(pe, lambda: PE.matmul(pS[0:1, 0:256], lhsT=scb[:, kc:kc + 1], rhs=wt_[:, kc, :], start=(kc == 0),
                                                       stop=(kc == 15)), reads=[r_scb, r_wt_], writes=[r_pS], inc=(kc == 15))
                        K.op(dve, lambda: V.tensor_tensor(out=mr_[:], in0=pS[0:1, 0:256], in1=br_[:], op=ALU.add),
                             reads=[r_pS, r_br_], writes=[r_mr_])
                        if c < 8:
                            K.dma(sp, g_scr[0:1, c * 256:(c + 1) * 256], mr_[:], reads=[r_mr_], writes=[r_gscr], semres=r_mr_)
                        elif c >= 24:
                            K.dma(sp, g_scr[1:2, (c - 24) * 256:(c - 23) * 256], mr_[:], reads=[r_mr_], writes=[r_gscr], semres=r_mr_)
                        bg_load()

                    bg_load()
                    bg_load()
                    K.op(pool, lambda: G.memset(vaug[:, :, 256:258], 1.0), writes=[r_vaug])
                    for h in range(4):
                        wq3, r_wq = ws_next(OFF_DQ + h * 256)
                        wk3, r_wk = ws_next(OFF_DK + h * 256)
                        wv3, r_wv = ws_next(OFF_DV + h * 256)
                        K.dma(sp, bd[:], _dram_ap(u_scr, (8 + h) * UL + 2048 - 127, [[1, 128], [1, 256]]), reads=[r_uscr],
                              writes=[r_bd])
                        for m in range(2):
                            for tcn in range(4):
                                pj, r_pj = ps_next()
                                projT(pj[:], r_pj, wk3, r_wk, m * 128, tcn * 512, 512)
                                qknorm(pj[:], r_pj, 512, qkg_t[:, 5:6], kTd[m][0][:, tcn * 512:(tcn + 1) * 512], kTd[m][1])
                            for (t0, n) in QCH:
                                pj, r_pj = ps_next()
                                projT(pj[:, 0:n], r_pj, wq3, r_wq, m * 128, t0, n)
                                qknorm(pj[:, 0:n], r_pj, n, qkg_t[:, 4:5], qTd[m][0][:, t0 - Q0:t0 - Q0 + n], qTd[m][1])
                        for i in range(16):
                            pj, r_pj = ps_next()
                            for kc in range(16):
                                K.op(pe, lambda: PE.matmul(pj[:, 0:256], lhsT=hT[:, kc, i * 128:(i + 1) * 128], rhs=wv3[:, kc, :],
                                                           start=(kc == 0), stop=(kc == 15)), reads=[r_hT, r_wv], writes=[r_pj],
                                     inc=(kc == 15))
                            if i % 2 == 0:
                                K.op(dve, lambda: V.tensor_copy(out=vaug[:, i, 0:256], in_=pj[:, 0:256]), reads=[r_pj],
                                     writes=[r_vaug])
                            else:
                                K.op(act, lambda: A.copy(out=vaug[:, i, 0:256], in_=pj[:, 0:256]), reads=[r_pj], writes=[r_vaug])
                        ws_release()
                        def diff_unit(qi, sl):
                            S_ = dslots[sl]
                            ptd, r_ptd = S_["ptd"]
                            od, r_od = S_["od"]
                            odb, r_odb = S_["odb"]
                            junkd, r_junkd = S_["junk"]
                            sm, r_sm = S_["sm"]
                            po = S_["po"]
                            r_ptr = S_["r_pst"]
                            tc0 = S_["tcol"]
                            qb = QT0 + qi
                            nj = qb + 1
                            for jp in range(0, nj, 2):
                                pS, r_pS = s_tiles[sctr[0] % len(s_tiles)]
                                sctr[0] += 1
                                js = [j for j in (jp, jp + 1) if j < nj]
                                for jj, j in enumerate(js):
                                    near = j >= qb - 1
                                    for m in range(2):
                                        col = jj * 256 + m * 128
                                        K.op(pe, lambda: PE.matmul(pS[:, col:col + 128], lhsT=kTd[m][0][:, j * 128:(j + 1) * 128],
                                                                   rhs=qTd[m][0][:, qi * 128:(qi + 1) * 128], start=True,
                                                                   stop=not near), reads=[kTd[m][1], qTd[m][1]], writes=[r_pS],
                                             inc=not near)
                                        if near:
                                            off = 0 if j == qb else 128
                                            K.op(pe, lambda: PE.matmul(pS[:, col:col + 128], lhsT=jmat[:], rhs=bd[:, off:off + 128],
                                                                       start=False, stop=True), reads=[r_jmat, r_bd],
                                                 writes=[r_pS])
                                ncol = len(js) * 256
                                bias_ = cm[:, 0:1] if jp < 8 else 0.0
                                rd = [r_pS, r_cm] if jp < 8 else [r_pS]
                                K.op(act, lambda: A.activation(out=ptd[:, jp * 256:jp * 256 + ncol], in_=pS[:, 0:ncol], func=AF.Exp,
                                                               bias=bias_), reads=rd, writes=[r_ptd])
                                yield "s"
                            for m in range(2):
                                for j in range(nj):
                                    K.op(pe, lambda: PE.matmul(po[m][0][:, 0:257], lhsT=ptd[:, j * 256 + m * 128:j * 256 + m * 128 + 128],
                                                               rhs=vaug[:, j, 0:257], start=(j == 0), stop=(j == nj - 1)),
                                         reads=[r_ptd, r_vaug], writes=[po[m][1]], inc=(j == nj - 1))
                                    if j % 4 == 3:
                                        yield "av"
                                yield "av"
                            K.op(dve, lambda: V.tensor_scalar(out=sm[:, 0:1], in0=po[0][0][:, 256:257], scalar1=1e-20,
                                                              scalar2=None, op0=ALU.max), reads=[po[0][1]], writes=[r_sm])
                            K.op(dve, lambda: V.tensor_scalar(out=sm[:, 1:2], in0=po[1][0][:, 256:257], scalar1=1e-20,
                                                              scalar2=None, op0=ALU.max), reads=[po[1][1]], writes=[r_sm])
                            yield "e"
                            K.op(dve, lambda: V.reciprocal(out=sm[:, 0:2], in_=sm[:, 0:2]), reads=[r_sm], writes=[r_sm])
                            yield "e"
                            K.op(dve, lambda: V.tensor_tensor(out=sm[:, 2:3], in0=sm[:, 1:2], in1=nlam[:], op=ALU.mult),
                                 reads=[r_sm, r_nlam], writes=[r_sm])
                            yield "e"
                            K.op(dve, lambda: V.tensor_scalar(out=od[:], in0=po[0][0][:, 0:256], scalar1=sm[:, 0:1], scalar2=None,
                                                              op0=ALU.mult), reads=[po[0][1], r_sm], writes=[r_od])
                            yield "e"
                            K.op(dve, lambda: V.scalar_tensor_tensor(out=od[:], in0=po[1][0][:, 0:256], scalar=sm[:, 2:3], in1=od[:],
                                                                     op0=ALU.mult, op1=ALU.add), reads=[po[1][1], r_sm, r_od],
                                 writes=[r_od])
                            yield "e"
                            K.op(act, lambda: A.activation(out=junkd[:], in_=od[:], func=AF.Square, accum_out=sm[:, 3:4]),
                                 reads=[r_od], writes=[r_junkd, r_sm])
                            yield "e"
                            K.op(act, lambda: A.activation(out=sm[:, 4:5], in_=sm[:, 3:4], func=AF.Ln, scale=1.0 / 256, bias=EPS),
                                 reads=[r_sm], writes=[r_sm])
                            yield "e"
                            K.op(act, lambda: A.activation(out=sm[:, 4:5], in_=sm[:, 4:5], func=AF.Exp, scale=-0.5),
                                 reads=[r_sm], writes=[r_sm])
                            yield "e"
                            K.op(dve, lambda: V.scalar_tensor_tensor(out=odb[:], in0=od[:], scalar=sm[:, 4:5], in1=sublbc[:],
                                                                     op0=ALU.mult, op1=ALU.mult), reads=[r_od, r_sm, r_sublbc],
                                 writes=[r_odb])
                            yield "e"
                            for hf in range(2):
                                K.op(pe, lambda: PE.transpose(out=ps_t[:, tc0 + hf * 128:tc0 + (hf + 1) * 128],
                                                              in_=odb[:, hf * 128:(hf + 1) * 128], identity=ident[:]),
                                     reads=[r_odb, r_ident], writes=[r_ptr], inc=(hf == 1))
                            yield "e"
                            K.op(act, lambda: A.copy(out=oT_diff[:, 2 * h:2 * h + 2, qi * 128:(qi + 1) * 128],
                                                     in_=ps_t[:, tc0:tc0 + 256].rearrange("p (a b) -> p a b", b=128)),
                                 reads=[r_ptr], writes=[r_oTd])

                        active = []
                        free_slots = [0, 1]
                        nxt = 0
                        primed = False
                        while active or nxt < NQT:
                            while len(active) < 2 and nxt < NQT:
                                sl = free_slots.pop(0)
                                gnr = diff_unit(nxt, sl)
                                nxt += 1
                                if not primed and active == []:
                                    primed = True
                                    live = True
                                    try:
                                        while next(gnr) == "s":
                                            pass
                                    except StopIteration:
                                        live = False
                                    if live:
                                        active.append((gnr, sl))
                                    else:
                                        free_slots.append(sl)
                                else:
                                    active.append((gnr, sl))
                            for ent in list(active):
                                try:
                                    next(ent[0])
                                except StopIteration:
                                    active.remove(ent)
                                    free_slots.append(ent[1])
                                    bg_step()
                    while bgst["done"] <= NBG:
                        bg_step()
                    K.barrier()

                if stage >= 3:
                  with ExitStack() as sn:
                    arena, _ = sbuf(sn, "arena", [128, 12288], BF16)
                    w1t = [(arena[:, 4096 + i * 4096:8192 + i * 4096].rearrange("p (l j) -> p l j", j=128), K.res("w1t%d" % i))
                           for i in range(2)]
                    w2t = [sbuf(sn, "w2t%d" % i, [128, 128], BF16) for i in range(2)]
                    pet = [sbuf(sn, "pet%d" % i, [128, 32], BF16) for i in range(2)]
                    cst, r_cst = sbuf(sn, "cst", [128, 2], F32)
                    ebf, r_ebf = sbuf(sn, "ebf", [32, T], BF16)
                    vcaug, r_vcaug = sbuf(sn, "vcaug", [128, 162], BF16)
                    gates, r_gates = sbuf(sn, "gates", [128, NQT, 24], F32)
                    smk, r_smk = sbuf(sn, "smk", [128, NQT, 64], F32)
                    w4t, r_w4t = sbuf(sn, "w4t", [128, 128], BF16)
                    K.dma(sp, smk[:], selmask, writes=[r_smk])
                    K.dma(pool, ebf[:], e_mat, writes=[r_ebf])
                    K.op(dve, lambda: V.memset(vcaug[:], 1.0), writes=[r_vcaug])
                    K.dma(pool, vcaug[0:127, 129:161], ovl, writes=[r_vcaug])
                    K.op(pool, lambda: G.memset(tmpf[:], 0.0), reads=[r_tmpf], writes=[r_tmpf])
                    K.op(pool, lambda: G.affine_select(out=tmpf[:], in_=tmpf[:], pattern=[[-1, 128]], compare_op=ALU.is_ge,
                                                       fill=NEGV, base=126, channel_multiplier=-1), reads=[r_tmpf], writes=[r_tmpf])
                    K.op(dve, lambda: V.tensor_copy(out=w4t[:], in_=tmpf[:]), reads=[r_tmpf], writes=[r_w4t])
                    for i in range(2):
                        K.dma(pool, w2t[i][0][:], cmp_w2[i], writes=[w2t[i][1]])
                        K.dma(pool, pet[i][0][:], cmp_pe[i], writes=[pet[i][1]])
                    wg3, r_wg = ws_next(OFF_NSA_G)
                    for qi in range(NQT):
                        pj, r_pj = ps_next()
                        for kc in range(16):
                            K.op(pe, lambda: PE.matmul(pj[:, 0:24], lhsT=hT[:, kc, Q0 + qi * 128:Q0 + (qi + 1) * 128], rhs=wg3[:, kc, :],
                                                       start=(kc == 0), stop=(kc == 15)), reads=[r_hT, r_wg], writes=[r_pj],
                                 inc=(kc == 15))
                        K.op(act, lambda: A.activation(out=gates[:, qi, :], in_=pj[:, 0:24], func=AF.Exp, scale=-1.0),
                             reads=[r_pj], writes=[r_gates])
                    ws_release()
                    K.op(dve, lambda: V.tensor_scalar(out=gates[:], in0=gates[:], scalar1=1.0, scalar2=None, op0=ALU.add),
                         reads=[r_gates], writes=[r_gates])
                    K.op(dve, lambda: V.reciprocal(out=gates[:], in_=gates[:]), reads=[r_gates], writes=[r_gates])

                    zT = [(arena[:, i * 2048:(i + 1) * 2048], K.res("zT%d" % i)) for i in range(2)]
                    kTn = [sbuf(sn, "kTn%d" % i, [128, T], BF16) for i in range(2)]
                    vaugn = [sbuf(sn, "vaugn%d" % i, [128, 16, 130], BF16) for i in range(2)]
                    qTn, r_qTn = sbuf(sn, "qTn", [128, NQT, 4, 128], BF16)
                    kcT, r_kcT = sbuf(sn, "kcT", [128, 128], BF16)
                    glu = [sbuf(sn, "glu%d" % i, [128, 128], BF16) for i in range(2)]
                    xg, r_xg = sbuf(sn, "xg", [128, 128], F32)
                    tg, r_tg = sbuf(sn, "tg", [128, 128], F32)
                    bn, r_bn = sbuf(sn, "bn", [128, 4, 256], BF16)
                    cbt = [sbuf(sn, "cbt%d" % i, [128, 4, 128], BF16) for i in range(2)]
                    pts, r_pts = arena[:, 0:8192].rearrange("p (a b) -> p a b", b=512), K.res("pts")
                    ptw, r_ptw = arena[:, 8192:8192 + 2560].rearrange("p (a b) -> p a b", b=512), K.res("ptw")
                    ptc, r_ptc = arena[:, 11264:11776], K.res("ptc")
                    oaccs = [sbuf(sn, "oacc%d" % i, [128, 4, 128], F32) for i in range(2)]
                    obf, r_obf = sbuf(sn, "obf", [128, 4, 128], BF16)
                    imp, r_imp = sbuf(sn, "imp", [128, 32], F32)
                    scr_, r_scr_ = sbuf(sn, "scr_", [128, 32], F32)
                    wk_, r_wk_ = sbuf(sn, "wk_", [128, 32], F32)
                    m8, r_m8 = sbuf(sn, "m8", [128, 16], F32)
                    negb, r_negb = sbuf(sn, "negb", [128, 32], BF16)
                    negT, r_negT = sbuf(sn, "negT", [32, 128], BF16)
                    smh, r_smh = sbuf(sn, "smh", [128, 16], F32)
                    smt, r_smt = sbuf(sn, "smt", [128, 16], F32)
                    r_pst_lo = K.res("pst_lo")
                    r_pst_hi = K.res("pst_hi")
                    impt, r_impt = sbuf(sn, "impt", [128, 4, 32], F32)
                    thr, r_thr = sbuf(sn, "thr", [128, 1], F32)
                    tmp2h = [sbuf(sn, "tmp2h%d" % i, [128, 2, 128], F32) for i in range(1)] * 2
                    tmp2t = [sbuf(sn, "tmp2t%d" % i, [128, 2, 128], F32) for i in range(1)] * 2
                    for i in range(2):
                        K.op(pool, lambda: G.memset(vaugn[i][0][:, :, 128:130], 1.0), writes=[vaugn[i][1]])

                    for g in range(2):
                        def kvcol(br, kv):
                            return OFF_NSA_KV + ((br * 2 + kv) * 2 + g) * 128
                        K.dma(sp, bn[:], _dram_ap(u_scr, (4 * g) * UL + 2048 - 127, [[1, 128], [UL, 4], [1, 256]]), reads=[r_uscr],
                              writes=[r_bn])
                        if g == 1:
                            K.barrier()
                            ps_rot[:] = list(ps_a) + [ps_b[2], ps_b[3]]
                        for i in range(2):
                            K.dma(pool, w1t[i][0], cmp_w1[i].rearrange("(l d) j -> d l j", d=128), writes=[w1t[i][1]])
                        if g == 0:
                            for i in range(2):
                                pt_, r_pt_ = ps_next()
                                for l in range(32):
                                    K.op(pe, lambda: PE.matmul(pt_[:, 0:1], lhsT=w1t[i][0][:, l, :], rhs=pet[i][0][:, l:l + 1],
                                                               start=(l == 0), stop=(l == 31)), reads=[w1t[i][1], pet[i][1]],
                                         writes=[r_pt_], inc=(l == 31))
                                K.op(dve, lambda: V.tensor_copy(out=cst[:, i:i + 1], in_=pt_[:, 0:1]), reads=[r_pt_], writes=[r_cst])
                        for i in range(2):
                            wz3, r_wz = ws_next(kvcol(0, i))
                            for tcn in range(4):
                                pj, r_pj = ps_next()
                                projT(pj[:], r_pj, wz3, r_wz, 0, tcn * 512, 512)
                                if tcn % 2 == 0:
                                    K.op(dve, lambda: V.tensor_copy(out=zT[i][0][:, tcn * 512:(tcn + 1) * 512], in_=pj[:]), reads=[r_pj],
                                         writes=[zT[i][1]])
                                else:
                                    K.op(act, lambda: A.copy(out=zT[i][0][:, tcn * 512:(tcn + 1) * 512], in_=pj[:]), reads=[r_pj],
                                         writes=[zT[i][1]])
                            if tcn == 3:
                                ws_release()
                        for i in range(2):
                            pj, r_pj = ps_next()
                            for l in range(32):
                                K.op(pe, lambda: PE.matmul(pj[:, 0:127], lhsT=w1t[i][0][:, l, :], rhs=zT[i][0][:, l:l + 16 * 126 + 1:16],
                                                           start=(l == 0), stop=(l == 31)), reads=[w1t[i][1], zT[i][1]], writes=[r_pj],
                                     inc=(l == 31))
                            K.op(dve, lambda: V.tensor_scalar(out=xg[:, 0:127], in0=pj[:, 0:127], scalar1=cst[:, i:i + 1], scalar2=None,
                                                              op0=ALU.add), reads=[r_pj, r_cst], writes=[r_xg])
                            K.op(dve, lambda: V.tensor_tensor(out=tg[:, 0:127], in0=xg[:, 0:127], in1=xg[:, 0:127], op=ALU.mult),
                                 reads=[r_xg], writes=[r_tg])
                            K.op(dve, lambda: V.tensor_scalar(out=tg[:, 0:127], in0=tg[:, 0:127], scalar1=0.044715, scalar2=1.0,
                                                              op0=ALU.mult, op1=ALU.add), reads=[r_tg], writes=[r_tg])
                            K.op(dve, lambda: V.tensor_tensor(out=tg[:, 0:127], in0=tg[:, 0:127], in1=xg[:, 0:127], op=ALU.mult),
                                 reads=[r_tg, r_xg], writes=[r_tg])
                            K.op(act, lambda: A.activation(out=tg[:, 0:127], in_=tg[:, 0:127], func=AF.Exp, scale=-1.5957691216),
                                 reads=[r_tg], writes=[r_tg])
                            K.op(dve, lambda: V.tensor_scalar(out=tg[:, 0:127], in0=tg[:, 0:127], scalar1=1.0, scalar2=None,
                                                              op0=ALU.add), reads=[r_tg], writes=[r_tg])
                            K.op(dve, lambda: V.reciprocal(out=tg[:, 0:127], in_=tg[:, 0:127]), reads=[r_tg], writes=[r_tg])
                            K.op(dve, lambda: V.tensor_tensor(out=glu[i][0][:, 0:127], in0=tg[:, 0:127], in1=xg[:, 0:127], op=ALU.mult),
                                 reads=[r_tg, r_xg], writes=[glu[i][1]])
                        pj, r_pj = ps_next()
                        K.op(pe, lambda: PE.matmul(pj[:, 0:127], lhsT=w2t[0][0][:], rhs=glu[0][0][:, 0:127], start=True, stop=True),
                             reads=[w2t[0][1], glu[0][1]], writes=[r_pj])
                        qknorm(pj[:, 0:127], r_pj, 127, qkg_t[:, 1:2], kcT[:, 0:127], r_kcT)
                        pj, r_pj = ps_next()
                        K.op(pe, lambda: PE.matmul(pj[0:127, 0:128], lhsT=glu[1][0][:, 0:127], rhs=w2t[1][0][:], start=True, stop=True),
                             reads=[w2t[1][1], glu[1][1]], writes=[r_pj])
                        K.op(dve, lambda: V.tensor_copy(out=vcaug[0:127, 0:128], in_=pj[0:127, 0:128]), reads=[r_pj], writes=[r_vcaug])
                        for bi, br in enumerate((1, 2)):
                            wz3, r_wz = ws_next(kvcol(br, 0))
                            for tcn in range(4):
                                pj, r_pj = ps_next()
                                projT(pj[:], r_pj, wz3, r_wz, 0, tcn * 512, 512)
                                qknorm(pj[:], r_pj, 512, qkg_t[:, 1 + br:2 + br], kTn[bi][0][:, tcn * 512:(tcn + 1) * 512], kTn[bi][1])
                            ws_release()
                            wz3, r_wz = ws_next(kvcol(br, 1))
                            for i in range(16):
                                pj, r_pj = ps_next()
                                for kc in range(16):
                                    K.op(pe, lambda: PE.matmul(pj[:, 0:128], lhsT=hT[:, kc, i * 128:(i + 1) * 128], rhs=wz3[:, kc, :],
                                                               start=(kc == 0), stop=(kc == 15)), reads=[r_hT, r_wz], writes=[r_pj],
                                         inc=(kc == 15))
                                if i % 2 == 0:
                                    K.op(dve, lambda: V.tensor_copy(out=vaugn[bi][0][:, i, 0:128], in_=pj[:, 0:128]), reads=[r_pj],
                                         writes=[vaugn[bi][1]])
                                else:
                                    K.op(act, lambda: A.copy(out=vaugn[bi][0][:, i, 0:128], in_=pj[:, 0:128]), reads=[r_pj],
                                         writes=[vaugn[bi][1]])
                            ws_release()
                        for hp in range(2):
                            wq3, r_wq = ws_next((4 * g + 2 * hp) * 128)
                            for hh in range(2):
                                hl = 2 * hp + hh
                                for (t0, n) in QCH:
                                    pj, r_pj = ps_next()
                                    projT(pj[:, 0:n], r_pj, wq3, r_wq, hh * 128, t0, n)
                                    q0 = (t0 - Q0) // 128
                                    qknorm(pj[:, 0:n], r_pj, n, qkg_t[:, 0:1], qTn[:, q0:q0 + n // 128, hl, :], r_qTn,
                                           view=lambda a: a.rearrange("p (a b) -> p a b", b=128))
                            ws_release()
                        K.barrier()
                        ps_rot[:] = list(ps_a) + [ps_b[4]]
                        def nsa_head(qi):
                            oacc_, r_oacc_ = oaccs[qi % 2]
                            qb = QT0 + qi
                            qrhs = qTn[:, qi].rearrange("p a b -> p (a b)")
                            cb, r_cb = cbt[qi % 2]
                            K.dma(sp, cb[0:127], _dram_ap(u_scr, (4 * g) * UL + 128 * qb + 1, [[16, 127], [UL, 4], [1, 128]]),
                                  reads=[r_uscr], writes=[r_cb])
                            pS, r_pS = ps_next()
                            K.op(pe, lambda: PE.matmul(pS[0:127, :], lhsT=kcT[:, 0:127], rhs=qrhs, start=True, stop=False),
                                 reads=[r_kcT, r_qTn], writes=[r_pS], inc=False)
                            K.op(pe, lambda: PE.matmul(pS[0:127, :], lhsT=j127[0:127, 0:127],
                                                       rhs=cb[0:127].rearrange("p a b -> p (a b)"), start=False, stop=True),
                                 reads=[r_j127, r_cb], writes=[r_pS])
                            K.op(act, lambda: A.activation(out=ptc[0:127, :], in_=pS[0:127, :], func=AF.Exp, bias=cm[0:127, 2:3]),
                                 reads=[r_pS, r_cm], writes=[r_ptc])
                            yield
                            pc_ = [ps_b[0], ps_b[1]]
                            for hl in range(4):
                                pcb, r_pcb = pc_[hl // 2]
                                c0 = (hl % 2) * 161
                                K.op(pe, lambda: PE.matmul(pcb[:, c0:c0 + 161], lhsT=ptc[0:127, hl * 128:(hl + 1) * 128],
                                                           rhs=vcaug[0:127, 0:161], start=True, stop=True), reads=[r_ptc, r_vcaug],
                                     writes=[r_pcb])
                            for b_ in range(2):
                                pcb, r_pcb = pc_[b_]
                                K.op(dve, lambda: V.tensor_scalar(out=smh[:, 2 * b_:2 * b_ + 2], in0=pcb[:, 128:290:161], scalar1=1e-20,
                                                                  scalar2=None, op0=ALU.max), reads=[r_pcb], writes=[r_smh])
                            yield
                            K.op(dve, lambda: V.reciprocal(out=smh[:, 0:4], in_=smh[:, 0:4]), reads=[r_smh], writes=[r_smh])
                            K.op(dve, lambda: V.tensor_tensor(out=smh[:, 4:8], in0=smh[:, 0:4], in1=gates[:, qi, 12 * g:12 * g + 10:3],
                                                              op=ALU.mult), reads=[r_smh, r_gates], writes=[r_smh])
                            for b_ in range(2):
                                pcb, r_pcb = pc_[b_]
                                pv = pcb[:, 0:322].rearrange("p (h c) -> p h c", c=161)
                                K.op(dve, lambda: V.tensor_tensor(out=oacc_[:, 2 * b_:2 * b_ + 2, :], in0=pv[:, :, 0:128],
                                                                  in1=smh[:, 4 + 2 * b_:6 + 2 * b_].unsqueeze(2).to_broadcast([128, 2, 128]),
                                                                  op=ALU.mult), reads=[r_pcb, r_smh], writes=[r_oacc_])
                                K.op(dve, lambda: V.tensor_tensor(out=impt[:, 2 * b_:2 * b_ + 2, :], in0=pv[:, :, 129:161],
                                                                  in1=smh[:, 2 * b_:2 * b_ + 2].unsqueeze(2).to_broadcast([128, 2, 32]),
                                                                  op=ALU.mult), reads=[r_pcb, r_smh], writes=[r_impt])
                            yield
                            K.op(dve, lambda: V.tensor_reduce(out=imp[:], in_=impt[:].rearrange("p h s -> p s h"), axis=AX.X, op=ALU.add),
                                 reads=[r_impt], writes=[r_imp])
                            for jj in range(5):
                                j = qb - 4 + jj
                                pS, r_pS = ps_next()
                                hasb = jj in (0, 3, 4)
                                K.op(pe, lambda: PE.matmul(pS[:], lhsT=kTn[1][0][:, j * 128:(j + 1) * 128], rhs=qrhs, start=True,
                                                           stop=not hasb), reads=[kTn[1][1], r_qTn], writes=[r_pS], inc=not hasb)
                                if jj == 0:
                                    K.op(pe, lambda: PE.matmul(pS[:], lhsT=jmat[:], rhs=w4t[:, 0:128].unsqueeze(1).to_broadcast([128, 4, 128]),
                                                               start=False, stop=True), reads=[r_jmat, r_w4t], writes=[r_pS])
                                elif jj >= 3:
                                    off = 0 if jj == 4 else 128
                                    K.op(pe, lambda: PE.matmul(pS[:], lhsT=jmat[:], rhs=bn[:, :, off:off + 128], start=False, stop=True),
                                         reads=[r_jmat, r_bn], writes=[r_pS])
                                bias_ = cm[:, 0:1] if j < 8 else 0.0
                                rd = [r_pS, r_cm] if j < 8 else [r_pS]
                                K.op(act, lambda: A.activation(out=ptw[:, jj, :], in_=pS[:], func=AF.Exp, bias=bias_), reads=rd,
                                     writes=[r_ptw])
                                yield
                            pw_ = [ps_b[0], ps_b[1]]
                            for hl in range(4):
                                pwb, r_pwb = pw_[hl // 2]
                                c0 = (hl % 2) * 129
                                for jj in range(5):
                                    j = qb - 4 + jj
                                    K.op(pe, lambda: PE.matmul(pwb[:, c0:c0 + 129], lhsT=ptw[:, jj, hl * 128:(hl + 1) * 128],
                                                               rhs=vaugn[1][0][:, j, 0:129], start=(jj == 0), stop=(jj == 4)),
                                         reads=[r_ptw, vaugn[1][1]], writes=[r_pwb], inc=(jj == 4))
                                yield
                            yield
                            for b_ in range(2):
                                K.op(dve, lambda: V.tensor_scalar(out=smh[:, 8 + 2 * b_:10 + 2 * b_], in0=pw_[b_][0][:, 128:258:129],
                                                                  scalar1=1e-20, scalar2=None, op0=ALU.max), reads=[pw_[b_][1]],
                                     writes=[r_smh])
                            K.op(dve, lambda: V.reciprocal(out=smh[:, 8:12], in_=smh[:, 8:12]), reads=[r_smh], writes=[r_smh])
                            K.op(dve, lambda: V.tensor_tensor(out=smh[:, 8:12], in0=smh[:, 8:12], in1=gates[:, qi, 12 * g + 2:12 * g + 12:3],
                                                              op=ALU.mult), reads=[r_smh, r_gates], writes=[r_smh])
                            yield
                            for b_ in range(2):
                                pb_, r_pb_ = pw_[b_]
                                pv = pb_[:, 0:258].rearrange("p (h c) -> p h c", c=129)
                                t2, r_t2 = tmp2h[b_]
                                K.op(dve, lambda: V.tensor_tensor(out=t2[:], in0=pv[:, :, 0:128],
                                                                  in1=smh[:, 8 + 2 * b_:10 + 2 * b_].unsqueeze(2).to_broadcast([128, 2, 128]),
                                                                  op=ALU.mult), reads=[r_pb_, r_smh], writes=[r_t2])
                                K.op(dve, lambda: V.tensor_tensor(out=oacc_[:, 2 * b_:2 * b_ + 2, :], in0=oacc_[:, 2 * b_:2 * b_ + 2, :],
                                                                  in1=t2[:], op=ALU.add), reads=[r_oacc_, r_t2], writes=[r_oacc_])
                                yield
                            K.op(dve, lambda: V.tensor_tensor(out=scr_[:], in0=imp[:], in1=smk[:, qi, 0:32], op=ALU.mult),
                                 reads=[r_imp, r_smk], writes=[r_scr_])
                            K.op(dve, lambda: V.tensor_tensor(out=scr_[:], in0=scr_[:], in1=smk[:, qi, 32:64], op=ALU.add),
                                 reads=[r_scr_, r_smk], writes=[r_scr_])
                            yield
                            K.op(dve, lambda: V.max(out=m8[:, 0:8], in_=scr_[:]), reads=[r_scr_], writes=[r_m8])
                            K.op(dve, lambda: V.match_replace(out=wk_[:], in_to_replace=m8[:, 0:8], in_values=scr_[:], imm_value=-3e4),
                                 reads=[r_scr_, r_m8], writes=[r_wk_])
                            K.op(dve, lambda: V.max(out=m8[:, 8:16], in_=wk_[:]), reads=[r_wk_, r_m8], writes=[r_m8])
                            yield
                            K.op(dve, lambda: V.tensor_reduce(out=thr[:], in_=m8[:, 8:16], axis=AX.X, op=ALU.min), reads=[r_m8],
                                 writes=[r_thr])
                            K.op(dve, lambda: V.tensor_scalar(out=scr_[:], in0=scr_[:], scalar1=thr[:], scalar2=None, op0=ALU.is_ge),
                                 reads=[r_scr_, r_thr], writes=[r_scr_])
                            K.op(dve, lambda: V.tensor_scalar(out=negb[:], in0=scr_[:], scalar1=-NEGV, scalar2=NEGV, op0=ALU.mult,
                                                              op1=ALU.add), reads=[r_scr_], writes=[r_negb])
                            yield
                            ptnb, r_ptn = ps_t[:, 512:1024], r_pst_hi
                            K.op(pe, lambda: PE.transpose(out=ptnb[0:32, 0:128], in_=negb[:, 0:32], identity=ident[:]),
                                 reads=[r_negb, r_ident], writes=[r_ptn])
                            K.op(dve, lambda: V.tensor_copy(out=negT[:], in_=ptnb[0:32, 0:128]), reads=[r_ptn], writes=[r_negT])
                            nj = qb + 1
                            for j in range(nj):
                                pS, r_pS = ps_next()
                                near = j >= qb - 1
                                K.op(pe, lambda: PE.matmul(pS[:], lhsT=kTn[0][0][:, j * 128:(j + 1) * 128], rhs=qrhs, start=True,
                                                           stop=False), reads=[kTn[0][1], r_qTn], writes=[r_pS], inc=False)
                                if j != qb:
                                    K.op(pe, lambda: PE.matmul(pS[:], lhsT=ebf[:, j * 128:(j + 1) * 128],
                                                               rhs=negT[:, 0:128].unsqueeze(1).to_broadcast([32, 4, 128]), start=False,
                                                               stop=not near), reads=[r_ebf, r_negT], writes=[r_pS], inc=not near)
                                if near:
                                    off = 0 if j == qb else 128
                                    K.op(pe, lambda: PE.matmul(pS[:], lhsT=jmat[:], rhs=bn[:, :, off:off + 128], start=False, stop=True),
                                         reads=[r_jmat, r_bn], writes=[r_pS])
                                bias_ = cm[:, 0:1] if j < 8 else 0.0
                                rd = [r_pS, r_cm] if j < 8 else [r_pS]
                                K.op(act, lambda: A.activation(out=pts[:, j, :], in_=pS[:], func=AF.Exp, bias=bias_), reads=rd,
                                     writes=[r_pts])
                                yield
                            px_ = [ps_b[2], ps_b[3]]
                            for hl in range(4):
                                pxb, r_pxb = px_[hl // 2]
                                c0 = (hl % 2) * 129
                                for j in range(nj):
                                    K.op(pe, lambda: PE.matmul(pxb[:, c0:c0 + 129], lhsT=pts[:, j, hl * 128:(hl + 1) * 128],
                                                               rhs=vaugn[0][0][:, j, 0:129], start=(j == 0), stop=(j == nj - 1)),
                                         reads=[r_pts, vaugn[0][1]], writes=[r_pxb], inc=(j == nj - 1))
                        def nsa_tail(qi):
                            oacc_, r_oacc_ = oaccs[qi % 2]
                            qb = QT0 + qi
                            px_ = [ps_b[2], ps_b[3]]
                            for b_ in range(2):
                                K.op(dve, lambda: V.tensor_scalar(out=smt[:, 12 + 2 * b_:14 + 2 * b_], in0=px_[b_][0][:, 128:258:129],
                                                                  scalar1=1e-20, scalar2=None, op0=ALU.max), reads=[px_[b_][1]],
                                     writes=[r_smt])
                            K.op(dve, lambda: V.reciprocal(out=smt[:, 12:16], in_=smt[:, 12:16]), reads=[r_smt], writes=[r_smt])
                            yield
                            K.op(dve, lambda: V.tensor_tensor(out=smt[:, 12:16], in0=smt[:, 12:16], in1=gates[:, qi, 12 * g + 1:12 * g + 11:3],
                                                              op=ALU.mult), reads=[r_smt, r_gates], writes=[r_smt])
                            yield
                            for bi, pp_ in ((1, px_),):
                                for b_ in range(2):
                                    pb_, r_pb_ = pp_[b_]
                                    pv = pb_[:, 0:258].rearrange("p (h c) -> p h c", c=129)
                                    t2, r_t2 = tmp2t[b_]
                                    cc0 = 8 + 4 * bi + 2 * b_
                                    K.op(dve, lambda: V.tensor_tensor(out=t2[:], in0=pv[:, :, 0:128],
                                                                      in1=smt[:, cc0:cc0 + 2].unsqueeze(2).to_broadcast([128, 2, 128]),
                                                                      op=ALU.mult), reads=[r_pb_, r_smt], writes=[r_t2])
                                    if bi == 0:
                                        K.op(dve, lambda: V.tensor_tensor(out=oacc_[:, 2 * b_:2 * b_ + 2, :], in0=oacc_[:, 2 * b_:2 * b_ + 2, :],
                                                                          in1=t2[:], op=ALU.add), reads=[r_oacc_, r_t2], writes=[r_oacc_])
                                    else:
                                        K.op(dve, lambda: V.tensor_tensor(out=obf[:, 2 * b_:2 * b_ + 2, :], in0=oacc_[:, 2 * b_:2 * b_ + 2, :],
                                                                          in1=t2[:], op=ALU.add), reads=[r_oacc_, r_t2], writes=[r_obf])
                                    yield
                            yield
                            ptrb, r_ptr = ps_t[:, 0:512], r_pst_lo
                            for hl in range(4):
                                K.op(pe, lambda: PE.transpose(out=ptrb[:, hl * 128:(hl + 1) * 128], in_=obf[:, hl, :], identity=ident[:]),
                                     reads=[r_obf, r_ident], writes=[r_ptr], inc=(hl == 3))
                            yield
                            K.op(act, lambda: A.copy(out=oT_nsa[:, 4 * g:4 * g + 4, qi * 128:(qi + 1) * 128],
                                                     in_=ptrb[:, 0:512].rearrange("p (a b) -> p a b", b=128)),
                                 reads=[r_ptr], writes=[r_oTn])
                        prev_tail = None
                        for qi in range(NQT):
                            hg = nsa_head(qi)
                            if prev_tail is not None:
                                t_alive = True
                                h_alive = True
                                while t_alive:
                                    if h_alive:
                                        try:
                                            next(hg)
                                        except StopIteration:
                                            h_alive = False
                                    try:
                                        next(prev_tail)
                                    except StopIteration:
                                        t_alive = False
                                if h_alive:
                                    for _ in hg:
                                        pass
                            else:
                                for _ in hg:
                                    pass
                            prev_tail = nsa_tail(qi)
                        for _ in prev_tail:
                            pass

                    K.barrier()
                K.barrier()

            if debug:
                of, r_of = sbuf(stB, "of", [128, 16 * NQ], F32)
                d = dbg_out("oT", [128, 16 * NQ])
                K.op(dve, lambda: V.tensor_copy(out=of[:, 0:8 * NQ], in_=oT_nsa[:].rearrange("p a b -> p (a b)")), reads=[r_oTn],
                     writes=[r_of])
                K.op(dve, lambda: V.tensor_copy(out=of[:, 8 * NQ:16 * NQ], in_=oT_diff[:].rearrange("p a b -> p (a b)")),
                     reads=[r_oTd], writes=[r_of])
                K.dma(sp, d, of[:], reads=[r_of], writes=[r_out], semres=r_of)
            if stage <= 3:
                K.barrier()
                return nc, dbg

            with ExitStack() as st:
                mergedT, r_mT = sbuf(st, "mergedT", [128, 16, NQ], BF16)
                wbufs = [sbuf(st, "wc%d" % i, [128, 4096], BF16) for i in range(5)]
                g1bc, r_g1bc = sbuf(st, "g1bc", [128, D], F32)
                sg = [sbuf(st, "sg%d" % i, [128, 512], F32) for i in range(4)]
                xts = [sbuf(st, "xts%d" % i, [128, 256], F32) for i in range(3)]
                x1s = [sbuf(st, "x1s%d" % i, [128, 256], F32) for i in range(3)]
                pss_ = [psum(st, "pcc%d" % i, [128, 512], F32) for i in range(8)]
                K.dma(sp, g1bc[:], _dram_ap(g_scr, 0, [[0, 128], [1, D]]), reads=[r_gscr], writes=[r_g1bc])
                wno_v = w_nsa_out.rearrange("(c p) n -> p c n", p=128)
                wdo_v = w_diff_out.rearrange("(c p) n -> p c n", p=128)
                TCH = [(0, 512), (512, 512), (1024, 128)]
                unit = 0
                wi = 0
                cload = {}

                def c_load(nch_):
                    wA_, r_wA_ = wbufs[(2 * nch_) % 5]
                    wa3_ = wA_[:, 0:1024].rearrange("p (c n) -> p c n", n=128)
                    wd3_ = wA_[:, 1024:2048].rearrange("p (c n) -> p c n", n=128)
                    K.dma(pool, wa3_, wno_v[:, :, nch_ * 128:(nch_ + 1) * 128], writes=[r_wA_])
                    K.dma(pool, wd3_, wdo_v[:, :, nch_ * 128:(nch_ + 1) * 128], writes=[r_wA_])
                    wB_, r_wB_ = wbufs[(2 * nch_ + 1) % 5]
                    wm0_ = wB_[:, 0:2048].rearrange("p (c n) -> p c n", n=128)
                    wm1_ = wB_[:, 2048:4096].rearrange("p (c n) -> p c n", n=128)
                    K.dma(pool, wm0_, w_in_v[:, :, OFF_MG + nch_ * 128:OFF_MG + (nch_ + 1) * 128], writes=[r_wB_])
                    K.dma(pool, wm1_, w_in_v[:, :, OFF_MG + D + nch_ * 128:OFF_MG + D + (nch_ + 1) * 128], writes=[r_wB_])
                    cload[nch_] = (wa3_, wd3_, r_wA_, wm0_, wm1_, r_wB_)
                c_load(0)
                for nch in range(16):
                    if nch + 1 < 16:
                        c_load(nch + 1)
                    wa3, wd3, r_wA, wm0, wm1, r_wB = cload.pop(nch)
                    for (t0, n) in TCH:
                        pset = pss_[(unit % 2) * 4:(unit % 2) * 4 + 4]
                        (pn, r_pn), (pd, r_pd), (p0, r_p0), (p1, r_p1) = pset
                        s0, r_s0 = sg[(unit % 2) * 2]
                        s1, r_s1 = sg[(unit % 2) * 2 + 1]
                        unit += 1
                        for c in range(8):
                            K.op(pe, lambda: PE.matmul(pn[:, 0:n], lhsT=wa3[:, c, :], rhs=oT_nsa[:, c, t0:t0 + n], start=(c == 0),
                                                       stop=(c == 7)), reads=[r_wA, r_oTn], writes=[r_pn], inc=(c == 7))
                        for c in range(8):
                            K.op(pe, lambda: PE.matmul(pd[:, 0:n], lhsT=wd3[:, c, :], rhs=oT_diff[:, c, t0:t0 + n], start=(c == 0),
                                                       stop=(c == 7)), reads=[r_wA, r_oTd], writes=[r_pd], inc=(c == 7))
                        for kc in range(16):
                            K.op(pe, lambda: PE.matmul(p0[:, 0:n], lhsT=wm0[:, kc, :], rhs=hT[:, kc, Q0 + t0:Q0 + t0 + n],
                                                       start=(kc == 0), stop=(kc == 15)), reads=[r_wB, r_hT], writes=[r_p0],
                                 inc=(kc == 15))
                        for kc in range(16):
                            K.op(pe, lambda: PE.matmul(p1[:, 0:n], lhsT=wm1[:, kc, :], rhs=hT[:, kc, Q0 + t0:Q0 + t0 + n],
                                                       start=(kc == 0), stop=(kc == 15)), reads=[r_wB, r_hT], writes=[r_p1],
                                 inc=(kc == 15))
                        K.op(act, lambda: A.activation(out=s0[:, 0:n], in_=p0[:, 0:n], func=AF.Sigmoid), reads=[r_p0], writes=[r_s0])
                        K.op(act, lambda: A.activation(out=s1[:, 0:n], in_=p1[:, 0:n], func=AF.Sigmoid), reads=[r_p1], writes=[r_s1])
                        K.op(dve, lambda: V.tensor_tensor(out=s0[:, 0:n], in0=pn[:, 0:n], in1=s0[:, 0:n], op=ALU.mult),
                             reads=[r_pn, r_s0], writes=[r_s0])
                        K.op(dve, lambda: V.tensor_tensor(out=s1[:, 0:n], in0=pd[:, 0:n], in1=s1[:, 0:n], op=ALU.mult),
                             reads=[r_pd, r_s1], writes=[r_s1])
                        K.op(dve, lambda: V.tensor_tensor(out=mergedT[:, nch, t0:t0 + n], in0=s0[:, 0:n], in1=s1[:, 0:n], op=ALU.add),
                             reads=[r_s0, r_s1], writes=[r_mT])
                w_o_v = w_o.rearrange("(kc p) n -> p kc n", p=128)
                u2 = 0
                oload = {}

                def o_load(cc_):
                    wO_, r_wO_ = wbufs[(32 + cc_) % 5]
                    wo3_ = wO_[:].rearrange("p (c n) -> p c n", n=256)
                    K.dma(pool, wo3_, w_o_v[:, :, cc_ * 256:(cc_ + 1) * 256], writes=[r_wO_])
                    oload[cc_] = (wo3_, r_wO_)
                o_load(0)
                o_load(1)
                for cc in range(8):
                    if cc + 2 < 8:
                        o_load(cc + 2)
                    wo3, r_wO = oload.pop(cc)
                    for qi in range(NQT):
                        xt, r_xt = xts[u2 % 3]
                        x1t, r_x1t = x1s[u2 % 3]
                        pq, r_pq = pss_[u2 % 8]
                        u2 += 1
                        K.dma(act, xt[:], xctx[Q0 + qi * 128:Q0 + (qi + 1) * 128, cc * 256:(cc + 1) * 256], writes=[r_xt])
                        for nch in range(16):
                            K.op(pe, lambda: PE.matmul(pq[:, 0:256], lhsT=mergedT[:, nch, qi * 128:(qi + 1) * 128], rhs=wo3[:, nch, :],
                                                       start=(nch == 0), stop=(nch == 15)), reads=[r_mT, r_wO], writes=[r_pq],
                                 inc=(nch == 15))
                        K.op(dve, lambda: V.tensor_tensor(out=x1t[:], in0=pq[:, 0:256], in1=g1bc[:, cc * 256:(cc + 1) * 256], op=ALU.mult),
                             reads=[r_pq, r_g1bc], writes=[r_x1t])
                        K.op(dve, lambda: V.tensor_tensor(out=x1t[:], in0=x1t[:], in1=xt[:], op=ALU.add), reads=[r_x1t, r_xt],
                             writes=[r_x1t])
                        K.dma(sp, x1_scr[qi * 128:(qi + 1) * 128, cc * 256:(cc + 1) * 256], x1t[:], reads=[r_x1t], writes=[r_x1scr],
                              semres=r_x1t)
                K.barrier()
        if stage <= 4:
            K.barrier()
            return nc, dbg

        with ExitStack() as stE:
            h2T, r_h2T = sbuf(stE, "h2T", [128, 16, NQ], BF16)
            K.op(dve, lambda: V.scalar_tensor_tensor(out=gcol[:, 16:32], in0=modcol[:, 64:80], scalar=1.0, in1=gl[:, 16:32],
                                                     op0=ALU.add, op1=ALU.mult), reads=[r_modcol, r_gl], writes=[r_gcol])
            with ExitStack() as st:
                norm_to_featmajor(st, lambda i: x1_scr[i * 128:(i + 1) * 128, :], NQT, h2T, r_h2T, 16, 48, 0, "n2",
                                  src_reads=[r_x1scr])
                K.barrier()
            with ExitStack() as st:
                acc, r_acc = sbuf(st, "acc", [128, 8, D], F32)
                g2bc, r_g2bc = sbuf(st, "g2bc", [128, D], F32)
                cvt, r_cvt = sbuf(st, "cvt", [128, 88, 4], F32)
                gT, r_gT = sbuf(st, "gT", [128, 11, 1024], BF16)
                wbufs = [sbuf(st, "we%d" % i, [128, 4096], BF16) for i in range(3)]
                uu = [sbuf(st, "uu%d" % i, [128, 1026], F32) for i in range(2)]
                cc_ = [sbuf(st, "cc%d" % i, [128, 1024], F32) for i in range(2)]
                sa, r_sa = sbuf(st, "sa", [128, 1024], F32)
                tmpd = [sbuf(st, "tmpd%d" % i, [128, 256], F32) for i in range(2)]
                pss_ = [psum(st, "pee%d" % i, [128, 512], F32) for i in range(8)]
                racc = [K.res("acc%d" % i) for i in range(8)]
                for i in range(8):
                    K.dma(sp, acc[:, i, :], x1_scr[128 + i * 128:128 + (i + 1) * 128, :], reads=[r_x1scr], writes=[racc[i]])
                K.dma(sp, g2bc[:], _dram_ap(g_scr, D, [[0, 128], [1, D]]), reads=[r_gscr], writes=[r_g2bc])
                K.dma(sp, cvt[:], conv_l, writes=[r_cvt])
                w_up_v = w_up.rearrange("(kc p) n -> p kc n", p=128)
                w_down_v = w_down.rearrange("(fc p) n -> p fc n", p=128)
                UCH = [(126, 512, 0), (638, 512, 512), (1150, 2, 1024)]
                pi = [0]
                estages = []
                for fp_ in range(4):
                    for fl_ in range(11):
                        estages.append(("up", fp_, fl_))
                    for cx_ in range(8):
                        estages.append(("down", fp_, cx_))
                eload = {}

                def e_load(si):
                    kind, fp_, x_ = estages[si]
                    wt_, r_wt_ = wbufs[si % 3]
                    if kind == "up":
                        fc_ = fp_ * 11 + x_
                        v0 = wt_[:, 0:2048].rearrange("p (c n) -> p c n", n=128)
                        v1 = wt_[:, 2048:4096].rearrange("p (c n) -> p c n", n=128)
                        K.dma(pool, v0, w_up_v[:, :, fc_ * 128:(fc_ + 1) * 128], writes=[r_wt_])
                        K.dma(pool, v1, w_up_v[:, :, DFF + fc_ * 128:DFF + (fc_ + 1) * 128], writes=[r_wt_])
                        eload[si] = ([v0, v1], r_wt_)
                    else:
                        v0 = wt_[:, 0:11 * 256].rearrange("p (c n) -> p c n", n=256)
                        K.dma(pool, v0, w_down_v[:, fp_ * 11:(fp_ + 1) * 11, x_ * 256:(x_ + 1) * 256], writes=[r_wt_])
                        eload[si] = (v0, r_wt_)

                def e_up(fp, fl, wu3, r_wU):
                    fc = fp * 11 + fl
                    for part in range(2):
                        u, r_u = uu[part]
                        cv_, r_cv = cc_[part]
                        ch = fc + 44 * part
                        for (ti, n, dc) in UCH:
                            pu, r_pu = pss_[pi[0] % 8]; pi[0] += 1
                            for kc in range(16):
                                K.op(pe, lambda: PE.matmul(pu[:, 0:n], lhsT=wu3[part][:, kc, :], rhs=h2T[:, kc, ti:ti + n],
                                                           start=(kc == 0), stop=(kc == 15)), reads=[r_wU, r_h2T], writes=[r_pu],
                                     inc=(kc == 15))
                            K.op(act, lambda: A.copy(out=u[:, dc:dc + n], in_=pu[:, 0:n]), reads=[r_pu], writes=[r_u])
                        K.op(dve, lambda: V.tensor_scalar(out=u[:, 0:2], in0=u[:, 0:2], scalar1=cm[:, 1:2], scalar2=None,
                                                          op0=ALU.mult), reads=[r_u, r_cm], writes=[r_u])
                        K.op(dve, lambda: V.tensor_scalar(out=cv_[:], in0=u[:, 2:1026], scalar1=cvt[:, ch, 2:3],
                                                          scalar2=cvt[:, ch, 3:4], op0=ALU.mult, op1=ALU.add),
                             reads=[r_u, r_cvt], writes=[r_cv])
                        K.op(dve, lambda: V.scalar_tensor_tensor(out=cv_[:], in0=u[:, 1:1025], scalar=cvt[:, ch, 1:2], in1=cv_[:],
                                                                 op0=ALU.mult, op1=ALU.add), reads=[r_u, r_cvt, r_cv],
                             writes=[r_cv])
                        K.op(dve, lambda: V.scalar_tensor_tensor(out=cv_[:], in0=u[:, 0:1024], scalar=cvt[:, ch, 0:1], in1=cv_[:],
                                                                 op0=ALU.mult, op1=ALU.add), reads=[r_u, r_cvt, r_cv],
                             writes=[r_cv])
                    K.op(act, lambda: A.activation(out=sa[:], in_=cc_[0][0][:], func=AF.Silu), reads=[cc_[0][1]], writes=[r_sa])
                    K.op(pool, lambda: G.tensor_tensor(out=gT[:, fl, :], in0=sa[:], in1=cc_[1][0][:], op=ALU.mult),
                         reads=[r_sa, cc_[1][1]], writes=[r_gT])

                def e_down(fp, cc, wd3, r_wD):
                    for i in range(8):
                        pq, r_pq = pss_[pi[0] % 8]; pi[0] += 1
                        td, r_td = tmpd[pi[0] % 2]
                        for fl in range(11):
                            K.op(pe, lambda: PE.matmul(pq[:, 0:256], lhsT=gT[:, fl, i * 128:(i + 1) * 128], rhs=wd3[:, fl, :],
                                                       start=(fl == 0), stop=(fl == 10)), reads=[r_gT, r_wD], writes=[r_pq],
                                 inc=(fl == 10))
                        K.op(dve, lambda: V.tensor_tensor(out=td[:], in0=pq[:, 0:256], in1=g2bc[:, cc * 256:(cc + 1) * 256],
                                                          op=ALU.mult), reads=[r_pq, r_g2bc], writes=[r_td])
                        K.op(dve, lambda: V.tensor_tensor(out=acc[:, i, cc * 256:(cc + 1) * 256],
                                                          in0=acc[:, i, cc * 256:(cc + 1) * 256], in1=td[:], op=ALU.add),
                             reads=[r_td, racc[i]], writes=[racc[i]])

                e_load(0)
                e_load(1)
                for si in range(len(estages)):
                    if si + 2 < len(estages):
                        e_load(si + 2)
                    kind, fp, x_ = estages[si]
                    wv_, r_wv_ = eload.pop(si)
                    if kind == "up":
                        e_up(fp, x_, wv_, r_wv_)
                    else:
                        e_down(fp, x_, wv_, r_wv_)
                for i in range(8):
                    K.dma(sp, out[i * 128:(i + 1) * 128, :], acc[:, i, :], reads=[racc[i]], writes=[r_out], semres=racc[i])
                K.barrier()
        print('ninst', K.ninst, 'nwaits', K.nwaits, 'nsem', K.nsem)
    return nc, dbg


def _t5_bucket_np(n):
    n = np.maximum(np.asarray(n, np.int64), 0)
    nf = np.maximum(n, 16).astype(np.float32)
    large = 16 + (np.log(nf / np.float32(16)) / np.float32(math.log(128 / 16)) * np.float32(16)).astype(np.int32)
    large = np.minimum(large, 31)
    return np.where(n < 16, n, large)


def _consts():
    d = np.arange(UL) - 2048
    oh = np.zeros((33, UL), np.float32)
    b = _t5_bucket_np(d)
    for i in range(UL):
        if d[i] < 0:
            oh[32, i] = 1.0
        else:
            oh[b[i], i] = 1.0
    e_mat = np.zeros((32, T), np.float32)
    for s in range(32):
        e_mat[s, s * 64:(s + 1) * 64] = 1.0
    starts = np.arange(127) * 16
    sel_start = np.arange(32) * 64
    ovl = np.clip(np.minimum(starts[:, None] + 32, sel_start[None, :] + 64) - np.maximum(starts[:, None], sel_start[None, :]),
                  0, None).astype(np.float32) / 16.0
    return oh, e_mat, ovl


def _core_masks(half):
    cm = np.zeros((128, 8), np.float32)
    cm[:, 0] = 0.0 if half == 1 else NEGV
    cm[:, 1] = 1.0 if half == 1 else 0.0
    if half == 0:
        cm[:64, 2] = NEGV
    shift = 0 if half == 1 else 1024
    t_loc = Q0 + np.arange(NQ)
    t_real = t_loc - shift
    cur = np.floor_divide(t_real, 64)
    j_real = np.arange(32)[None, :] - shift // 64
    valid = (j_real >= 0) & (j_real <= cur[:, None])
    forced = valid & ((j_real == 0) | (j_real == cur[:, None]) | (j_real == cur[:, None] - 1))
    mul = (valid & ~forced).astype(np.float32)
    add = np.where(forced, 1e4, np.where(valid, 0.0, -1e4)).astype(np.float32)
    sm = np.concatenate([mul, add], axis=1).reshape(NQT, 128, 64).transpose(1, 0, 2)
    return cm, np.ascontiguousarray(sm)


def _col_layout(v):
    return np.ascontiguousarray(np.asarray(v, np.float32).reshape(16, 128).T)


def make_in_maps(inp):
    x = np.asarray(inp["x"], np.float32)
    oh, e_mat, ovl = _consts()
    f = lambda k: np.ascontiguousarray(np.asarray(inp[k], np.float32)[0])
    qkg = np.zeros((128, 8), np.float32)
    qkg[:, 0] = f("nsa_q_gain")
    qkg[:, 1:4] = f("nsa_k_gain").T
    qkg[:, 4] = f("diff_q_gain")
    qkg[:, 5] = f("diff_k_gain")
    lam_qk = np.concatenate([f("diff_lambda_q").reshape(-1), f("diff_lambda_k").reshape(-1)])[None, :]
    cw = f("ffn_conv_w")
    cb = f("ffn_conv_b")
    conv = np.concatenate([cw, cb[None, :]], axis=0)
    conv_l = np.ascontiguousarray(conv.reshape(4, 88, 128).transpose(2, 1, 0))
    shared = {
        "w_ada": f("w_ada"), "b_ada": np.asarray(inp["b_ada"], np.float32).reshape(1, -1),
        "g1_l": _col_layout(f("norm1_gain")), "g2_l": _col_layout(f("norm2_gain")),
        "w_in": f("w_in"), "qkg": qkg, "cmp_pe": np.ascontiguousarray(f("cmp_pe").transpose(0, 2, 1)), "cmp_w1": f("cmp_w1"), "cmp_w2": f("cmp_w2"),
        "lam_qk": np.ascontiguousarray(lam_qk), "subln": f("diff_subln_gain")[None, :],
        "w_nsa_out": f("w_nsa_out"), "w_diff_out": f("w_diff_out"), "w_o": f("w_o"), "w_up": f("w_ffn_up"),
        "conv_l": conv_l, "w_down": f("w_ffn_down"), "rel_bias": np.asarray(inp["rel_bias"], np.float32),
        "oh_tab": oh, "e_mat": e_mat, "ovl": ovl,
    }
    maps = []
    for core in range(8):
        b, half = core // 2, core % 2
        if half == 1:
            xc = np.ascontiguousarray(x[b])
        else:
            xc = np.concatenate([np.zeros((1024, D), np.float32), x[b, :1024]], axis=0)
        cm, sm = _core_masks(half)
        m = dict(shared)
        m.update({"xctx": xc, "c_l": _col_layout(np.asarray(inp["c"], np.float32)[b]), "cmasks": cm, "selmask": sm})
        maps.append(m)
    return maps


_PROG = {}


def kernel(**inputs):
    if "p" not in _PROG:
        _PROG["p"] = build_program()[0]
    nc = _PROG["p"]
    maps = make_in_maps(inputs)
    res = run_bass_kernel_spmd(nc, maps, core_ids=list(range(8)))
    outp = np.zeros((4, T, D), np.float32)
    for core in range(8):
        b, half = core // 2, core % 2
        outp[b, half * 1024:(half + 1) * 1024] = res.results[core]["out"]
    return outp
```

```python
import os
import math
import numpy as np
from contextlib import ExitStack
import concourse.bass as bass
import concourse.mybir as mybir
from concourse.bass_utils import run_bass_kernel_spmd

F32 = mybir.dt.float32
BF16 = mybir.dt.bfloat16
ALU = mybir.AluOpType
AF = mybir.ActivationFunctionType
AX = mybir.AxisListType

D = 2048
T = 2048
DFF = 5632
NQT = 9
QT0 = 7
NQ = NQT * 128
Q0 = QT0 * 128
OFF_NSA_KV = 1024
OFF_NSA_G = 1024 + 1536
OFF_DQ = OFF_NSA_G + 24
OFF_DK = OFF_DQ + 1024
OFF_DV = OFF_DK + 1024
OFF_MG = OFF_DV + 1024
IN_COLS = OFF_MG + 4096
NEGV = -30000.0
EPS = 1e-6
UL = 4096


class Res:
    __slots__ = ("name", "w", "r", "dsem", "dcount")

    def __init__(self, name):
        self.name = name
        self.w = None
        self.r = {}
        self.dsem = None
        self.dcount = 0


class Eng:
    def __init__(self, name, e, sem):
        self.name = name
        self.e = e
        self.sem = sem
        self.count = 0
        self.seen = {}


class Kern:
    def __init__(self, nc, stack):
        self.nc = nc
        self.stack = stack
        self.nsem = 0
        self.pe = Eng("pe", nc.tensor, self.newsem("s_pe"))
        self.act = Eng("act", nc.scalar, self.newsem("s_act"))
        self.dve = Eng("dve", nc.vector, self.newsem("s_dve"))
        self.pool = Eng("pool", nc.gpsimd, self.newsem("s_pool"))
        self.sp = Eng("sp", nc.sync, self.newsem("s_sp"))
        self.engines = [self.pe, self.act, self.dve, self.pool, self.sp]
        self.dres = []
        self.nwaits = 0
        self.ninst = 0
        self.freesems = []

    def newsem(self, name):
        self.nsem += 1
        return self.stack.enter_context(self.nc.semaphore(name))

    def res(self, name):
        return Res(name)

    def _wait(self, E, tok):
        sem, val = tok
        key = id(sem)
        if E.seen.get(key, 0) < val:
            E.e.wait_ge(sem, val)
            E.seen[key] = val
            self.nwaits += 1

    def _needs(self, E, reads, writes, skip_sem=None):
        for r in reads:
            if r.w is not None and r.w[0] is not skip_sem:
                self._wait(E, r.w)
        for w in writes:
            if w.w is not None and w.w[0] is not skip_sem:
                self._wait(E, w.w)
            for sem, val in list(w.r.values()):
                if sem is not skip_sem:
                    self._wait(E, (sem, val))

    def _mark(self, tok, reads, writes):
        sem, val = tok
        for r in reads:
            old = r.r.get(id(sem))
            if old is None or old[1] < val:
                r.r[id(sem)] = (sem, val)
        for w in writes:
            w.w = tok
            w.r = {}

    def op(self, E, fn, reads=(), writes=(), inc=True):
        skip = E.sem if E is self.pe else None
        self._needs(E, reads, writes, skip_sem=skip)
        inst = fn()
        self.ninst += 1
        if inc:
            E.count += 1
            inst.then_inc(E.sem, 1)
            tok = (E.sem, E.count)
        else:
            tok = (E.sem, E.count + 1)
        self._mark(tok, reads, writes)
        return inst

    def dma(self, E, out, in_, reads=(), writes=(), semres=None, **kw):
        if semres is None:
            semres = writes[0] if writes else reads[0]
        if semres.dsem is None:
            semres.dsem = self.newsem("d_" + semres.name)
            self.dres.append(semres)
        self._needs(E, reads, writes, skip_sem=semres.dsem)
        inst = E.e.dma_start(out=out, in_=in_, **kw)
        self.ninst += 1
        semres.dcount += 16
        inst.then_inc(semres.dsem, 16)
        tok = (semres.dsem, semres.dcount)
        self._mark(tok, reads, writes)
        return inst

    def barrier(self):
        toks = []
        for E in self.engines:
            if E.count > 0:
                toks.append((E.sem, E.count))
        for r in self.dres:
            if r.dcount > 0:
                toks.append((r.dsem, r.dcount))
        for E in self.engines:
            for t in toks:
                if t[0] is E.sem and E is self.pe:
                    continue
                self._wait(E, t)


def _dram_ap(t, offset, ap):
    return bass.AP(tensor=t.tensor, offset=offset, ap=ap)


def build_program(stage=99, debug=False):
    nc = bass.Bass("TRN2", target_bir_lowering=False)
    dt_in = lambda name, shape: nc.dram_tensor(name, list(shape), F32, kind="ExternalInput").ap()
    xctx = dt_in("xctx", [T, D])
    c_l = dt_in("c_l", [128, 16])
    w_ada = dt_in("w_ada", [D, 6 * D])
    b_ada = dt_in("b_ada", [1, 6 * D])
    g1_l = dt_in("g1_l", [128, 16])
    g2_l = dt_in("g2_l", [128, 16])
    w_in = dt_in("w_in", [D, IN_COLS])
    qkg = dt_in("qkg", [128, 8])
    cmp_pe = dt_in("cmp_pe", [2, 128, 32])
    cmp_w1 = dt_in("cmp_w1", [2, 4096, 128])
    cmp_w2 = dt_in("cmp_w2", [2, 128, 128])
    lam_qk = dt_in("lam_qk", [1, 512])
    subln = dt_in("subln", [1, 256])
    w_nsa_out = dt_in("w_nsa_out", [1024, D])
    w_diff_out = dt_in("w_diff_out", [1024, D])
    w_o = dt_in("w_o", [D, D])
    w_up = dt_in("w_up", [D, 2 * DFF])
    conv_l = dt_in("conv_l", [128, 88, 4])
    w_down = dt_in("w_down", [DFF, D])
    rel_bias = dt_in("rel_bias", [32, 12])
    oh_tab = dt_in("oh_tab", [33, UL])
    cmasks = dt_in("cmasks", [128, 8])
    selmask = dt_in("selmask", [128, NQT, 64])
    e_mat = dt_in("e_mat", [32, T])
    ovl = dt_in("ovl", [127, 32])
    out = nc.dram_tensor("out", [1024, D], F32, kind="ExternalOutput").ap()
    dbg = {}

    def dbg_out(name, shape):
        dbg[name] = nc.dram_tensor("dbg_" + name, list(shape), F32, kind="ExternalOutput").ap()
        return dbg[name]

    x1_scr = nc.dram_tensor("x1_scr", [NQ, D], F32, kind="Internal").ap()
    u_scr = nc.dram_tensor("u_scr", [12, UL], BF16, kind="Internal").ap()
    g_scr = nc.dram_tensor("g_scr", [2, D], F32, kind="Internal").ap()

    with ExitStack() as top:
        K = Kern(nc, top)
        pe, act, dve, pool, sp = K.pe, K.act, K.dve, K.pool, K.sp
        V = nc.vector
        A = nc.scalar
        G = nc.gpsimd
        PE = nc.tensor

        def sbuf(st, name, shape, dt):
            t = st.enter_context(nc.sbuf_tensor(name, list(shape), dt))
            return t, K.res(name)

        def psum(st, name, shape, dt):
            t = st.enter_context(nc.psum_tensor(name, list(shape), dt))
            return t, K.res(name)

        r_out = K.res("out")
        r_x1scr = K.res("x1scr")
        r_uscr = K.res("uscr")

        ident_f, r_identf = sbuf(top, "ident_f", [128, 128], F32)
        ident, r_ident = sbuf(top, "ident", [128, 128], BF16)
        jmat, r_jmat = sbuf(top, "jmat", [128, 128], BF16)
        j127, r_j127 = sbuf(top, "j127", [128, 128], BF16)
        ones_bf, r_onesbf = sbuf(top, "ones_bf", [128, 128], BF16)
        ones_f, r_onesf = sbuf(top, "ones_f", [128, 128], F32)
        modcol, r_modcol = sbuf(top, "modcol", [128, 96], F32)
        gcol, r_gcol = sbuf(top, "gcol", [128, 32], F32)
        r_gscr = K.res("gscr")
        cm, r_cm = sbuf(top, "cm", [128, 8], F32)
        qkg_t, r_qkg = sbuf(top, "qkg_t", [128, 8], F32)
        tmpf, r_tmpf = sbuf(top, "tmpf", [128, 128], F32)
        scb, r_scb = sbuf(top, "scb", [128, 16], BF16)
        nlam, r_nlam = sbuf(top, "nlam", [128, 1], F32)
        sublbc, r_sublbc = sbuf(top, "sublbc", [128, 256], F32)
        gl, r_gl = sbuf(top, "gl", [128, 32], F32)

        K.dma(sp, cm[:], cmasks, writes=[r_cm])
        K.dma(sp, qkg_t[:], qkg, writes=[r_qkg])
        K.op(pool, lambda: G.memset(ident_f[:], 0.0), writes=[r_identf])
        K.op(pool, lambda: G.affine_select(out=ident_f[:], in_=ident_f[:], pattern=[[-1, 128]], compare_op=ALU.not_equal,
                                           fill=1.0, base=0, channel_multiplier=1), reads=[r_identf], writes=[r_identf])
        K.op(dve, lambda: V.tensor_copy(out=ident[:], in_=ident_f[:]), reads=[r_identf], writes=[r_ident])
        K.op(pool, lambda: G.memset(tmpf[:], 0.0), writes=[r_tmpf])
        K.op(pool, lambda: G.affine_select(out=tmpf[:], in_=tmpf[:], pattern=[[1, 128]], compare_op=ALU.not_equal,
                                           fill=1.0, base=-127, channel_multiplier=1), reads=[r_tmpf], writes=[r_tmpf])
        K.op(dve, lambda: V.tensor_copy(out=jmat[:], in_=tmpf[:]), reads=[r_tmpf], writes=[r_jmat])
        K.op(pool, lambda: G.memset(tmpf[:], 0.0), reads=[r_tmpf], writes=[r_tmpf])
        K.op(pool, lambda: G.affine_select(out=tmpf[:], in_=tmpf[:], pattern=[[1, 128]], compare_op=ALU.not_equal,
                                           fill=1.0, base=-126, channel_multiplier=1), reads=[r_tmpf], writes=[r_tmpf])
        K.op(dve, lambda: V.tensor_copy(out=j127[:], in_=tmpf[:]), reads=[r_tmpf], writes=[r_j127])
        K.op(dve, lambda: V.memset(ones_bf[:], 1.0), writes=[r_onesbf])
        K.op(dve, lambda: V.memset(ones_f[:], 1.0), writes=[r_onesf])
        sc_ = 128.0 ** -0.5
        K.op(dve, lambda: V.tensor_scalar(out=qkg_t[:, 0:1], in0=qkg_t[:, 0:1], scalar1=sc_, scalar2=None, op0=ALU.mult),
             reads=[r_qkg], writes=[r_qkg])
        K.op(dve, lambda: V.tensor_scalar(out=qkg_t[:, 4:5], in0=qkg_t[:, 4:5], scalar1=sc_, scalar2=None, op0=ALU.mult),
             reads=[r_qkg], writes=[r_qkg])

        with ExitStack() as st:
            cl, r_cl = sbuf(st, "cl", [128, 16], F32)
            wb = [sbuf(st, "wada%d" % i, [128, 16, 512], BF16) for i in range(2)]
            brow, r_brow = sbuf(st, "brow", [1, 512], F32)
            mrow = [sbuf(st, "mrow%d" % i, [1, 512], F32) for i in range(2)]
            pm = [psum(st, "pm%d" % i, [1, 512], F32) for i in range(2)]
            pc = [psum(st, "pc%d" % i, [128, 4], F32) for i in range(2)]
            K.dma(sp, cl[:], c_l, writes=[r_cl])
            K.dma(sp, gl[:, 0:16], g1_l, writes=[r_gl])
            K.dma(sp, gl[:, 16:32], g2_l, writes=[r_gl])
            K.op(act, lambda: A.activation(out=scb[:], in_=cl[:], func=AF.Silu), reads=[r_cl], writes=[r_scb])
            pu0 = [psum(st, "pu0_%d" % i, [128, 512], F32) for i in range(2)]
            pu0c = [0]

            def pu0_next():
                t_ = pu0[pu0c[0] % 2]
                pu0c[0] += 1
                return t_
            su = st
            rba, r_rba = sbuf(su, "rba", [33, 12], F32)
            b31, r_b31 = sbuf(su, "b31", [33, 12], F32)
            oht, r_oht = sbuf(su, "oht", [33, UL], F32)
            ub, r_ub = sbuf(su, "ub", [12, UL], BF16)
            K.op(dve, lambda: V.memset(rba[:], NEGV), writes=[r_rba])
            K.dma(sp, rba[0:32, :], rel_bias, writes=[r_rba])
            K.dma(sp, b31[0:32, :], _dram_ap(rel_bias, 31 * 12, [[0, 32], [1, 12]]), writes=[r_b31])
            K.dma(sp, oht[:], oh_tab, writes=[r_oht])
            K.op(dve, lambda: V.tensor_tensor(out=rba[0:32, :], in0=rba[0:32, :], in1=b31[0:32, :], op=ALU.subtract),
                 reads=[r_rba, r_b31], writes=[r_rba])
            for ch in range(UL // 512):
                pt_, r_pt_ = pu0_next()
                K.op(pe, lambda: PE.matmul(pt_[0:12, :], lhsT=rba[:, :], rhs=oht[:, ch * 512:(ch + 1) * 512], start=True,
                                           stop=True), reads=[r_rba, r_oht], writes=[r_pt_])
                K.op(dve, lambda: V.tensor_copy(out=ub[:, ch * 512:(ch + 1) * 512], in_=pt_[0:12, :]), reads=[r_pt_],
                     writes=[r_ub])
            K.dma(sp, u_scr, ub[:], reads=[r_ub], writes=[r_uscr], semres=r_ub)

            lamt, r_lamt = sbuf(st, "lamt", [1, 512], F32)
            lamw, r_lamw = sbuf(st, "lamw", [1, 8], F32)
            subl, r_subl = sbuf(st, "subl", [1, 256], F32)
            K.dma(sp, lamt[:], lam_qk, writes=[r_lamt])
            K.dma(sp, subl[:], subln, writes=[r_subl])
            K.op(dve, lambda: V.tensor_tensor(out=lamt[:, 0:256], in0=lamt[:, 0:256], in1=lamt[:, 256:512], op=ALU.mult),
                 reads=[r_lamt], writes=[r_lamt])
            K.op(dve, lambda: V.tensor_reduce(out=lamw[:, 0:2], in_=lamt[:, 0:256].rearrange("p (a b) -> p a b", b=128),
                                              axis=AX.X, op=ALU.add), reads=[r_lamt], writes=[r_lamw])
            K.op(act, lambda: A.activation(out=lamw[:, 2:4], in_=lamw[:, 0:2], func=AF.Exp), reads=[r_lamw], writes=[r_lamw])
            K.op(dve, lambda: V.tensor_tensor(out=lamw[:, 4:5], in0=lamw[:, 3:4], in1=lamw[:, 2:3], op=ALU.subtract),
                 reads=[r_lamw], writes=[r_lamw])
            K.op(dve, lambda: V.tensor_scalar(out=lamw[:, 4:5], in0=lamw[:, 4:5], scalar1=-0.2, scalar2=None, op0=ALU.add),
                 reads=[r_lamw], writes=[r_lamw])
            pt_, r_pt_ = pu0_next()
            K.op(pe, lambda: PE.matmul(pt_[:, 0:1], lhsT=ones_f[0:1, :], rhs=lamw[0:1, 4:5], start=True, stop=True),
                 reads=[r_onesf, r_lamw], writes=[r_pt_])
            K.op(dve, lambda: V.tensor_copy(out=nlam[:], in_=pt_[:, 0:1]), reads=[r_pt_], writes=[r_nlam])
            pt_, r_pt_ = pu0_next()
            K.op(pe, lambda: PE.matmul(pt_[:, 0:256], lhsT=ones_f[0:1, :], rhs=subl[0:1, :], start=True, stop=True),
                 reads=[r_onesf, r_subl], writes=[r_pt_])
            K.op(dve, lambda: V.tensor_scalar(out=sublbc[:], in0=pt_[:, 0:256], scalar1=0.8, scalar2=None, op0=ALU.mult),
                 reads=[r_pt_], writes=[r_sublbc])

            w_ada_v = w_ada.rearrange("(kc p) n -> p kc n", p=128)
            for ch in range(8):
                wt, r_wt = wb[ch % 2]
                K.dma(pool, wt[:], w_ada_v[:, :, ch * 512:(ch + 1) * 512], writes=[r_wt])
                K.dma(sp, brow[:], b_ada[:, ch * 512:(ch + 1) * 512], writes=[r_brow])
                pmt, r_pm = pm[ch % 2]
                for kc in range(16):
                    K.op(pe, lambda kc=kc: PE.matmul(pmt[:], lhsT=scb[:, kc:kc + 1], rhs=wt[:, kc, :], start=(kc == 0),
                                                     stop=(kc == 15)), reads=[r_scb, r_wt], writes=[r_pm], inc=(kc == 15))
                mr, r_mr = mrow[ch % 2]
                K.op(dve, lambda: V.tensor_tensor(out=mr[:], in0=pmt[:], in1=brow[:], op=ALU.add),
                     reads=[r_pm, r_brow], writes=[r_mr])
                pct, r_pc = pc[ch % 2]
                for j in range(4):
                    K.op(pe, lambda j=j: PE.matmul(pct[:, j:j + 1], lhsT=mr[0:1, j * 128:(j + 1) * 128], rhs=ones_f[0:1, 0:1],
                                                   start=True, stop=True), reads=[r_mr, r_onesf], writes=[r_pc], inc=(j == 3))
                K.op(dve, lambda: V.tensor_copy(out=modcol[:, ch * 4:(ch + 1) * 4], in_=pct[:]), reads=[r_pc], writes=[r_modcol])
            K.op(dve, lambda: V.scalar_tensor_tensor(out=gcol[:, 0:16], in0=modcol[:, 16:32], scalar=1.0, in1=gl[:, 0:16],
                                                     op0=ALU.add, op1=ALU.mult), reads=[r_modcol, r_gl], writes=[r_gcol])
            K.barrier()
        if debug:
            d = dbg_out("modcol", [128, 96])
            K.dma(sp, d, modcol[:], reads=[r_modcol], writes=[r_out], semres=r_modcol)

        def norm_to_featmajor(st, src_ap_fn, ntiles, dstT, r_dstT, gc0, shc0, tok0, tag, src_reads=()):
            xs = [sbuf(st, "%s_x%d" % (tag, i), [128, D], F32) for i in range(8)]
            xn = [sbuf(st, "%s_xn%d" % (tag, i), [128, D], BF16) for i in range(8)]
            junk, r_junk = sbuf(st, tag + "_junk", [128, D], BF16)
            sss = [sbuf(st, "%s_ss%d" % (tag, i), [128, 8], F32) for i in range(2)]
            pt = [psum(st, "%s_pt%d" % (tag, i), [128, 512], BF16) for i in range(4)]
            ngroups = (ntiles + 3) // 4
            ev = [0]

            def load_a(g):
                nt = min(4, ntiles - g * 4)
                for i in range(nt):
                    xt, r_xt = xs[(g % 2) * 4 + i]
                    K.dma(sp, xt[:], src_ap_fn(g * 4 + i), reads=list(src_reads), writes=[r_xt])

            def stage_a(g):
                nt = min(4, ntiles - g * 4)
                ss, r_ss = sss[g % 2]
                for i in range(nt):
                    xt, r_xt = xs[(g % 2) * 4 + i]
                    K.op(act, lambda: A.activation(out=junk[:], in_=xt[:], func=AF.Square, accum_out=ss[:, i:i + 1]),
                         reads=[r_xt], writes=[r_junk, r_ss])
                K.op(act, lambda: A.activation(out=ss[:, 4:4 + nt], in_=ss[:, 0:nt], func=AF.Sqrt, scale=1.0 / D, bias=EPS),
                     reads=[r_ss], writes=[r_ss])
                K.op(dve, lambda: V.reciprocal(out=ss[:, 4:4 + nt], in_=ss[:, 4:4 + nt]), reads=[r_ss], writes=[r_ss])
                for i in range(nt):
                    xt, r_xt = xs[(g % 2) * 4 + i]
                    xnt, r_xnt = xn[(g % 2) * 4 + i]
                    K.op(dve, lambda: V.tensor_scalar(out=xnt[:], in0=xt[:], scalar1=ss[:, 4 + i:5 + i], scalar2=None,
                                                      op0=ALU.mult), reads=[r_xt, r_ss], writes=[r_xnt])

            def stage_b(g):
                nt = min(4, ntiles - g * 4)
                for kc in range(16):
                    ptt, r_pt = pt[kc % 4]
                    for i in range(nt):
                        xnt, r_xnt = xn[(g % 2) * 4 + i]
                        K.op(pe, lambda: PE.transpose(out=ptt[:, i * 128:(i + 1) * 128], in_=xnt[:, kc * 128:(kc + 1) * 128],
                                                      identity=ident[:]), reads=[r_xnt, r_ident], writes=[r_pt],
                             inc=(i == nt - 1))
                    t0 = tok0 + g * 512
                    if ev[0] % 2 == 0:
                        K.op(dve, lambda: V.tensor_scalar(out=dstT[:, kc, t0:t0 + nt * 128], in0=ptt[:, 0:nt * 128],
                                                          scalar1=gcol[:, gc0 + kc:gc0 + kc + 1],
                                                          scalar2=modcol[:, shc0 + kc:shc0 + kc + 1], op0=ALU.mult, op1=ALU.add),
                             reads=[r_pt, r_gcol, r_modcol], writes=[r_dstT])
                    else:
                        K.op(act, lambda: A.activation(out=dstT[:, kc, t0:t0 + nt * 128], in_=ptt[:, 0:nt * 128], func=AF.Identity,
                                                       scale=gcol[:, gc0 + kc:gc0 + kc + 1],
                                                       bias=modcol[:, shc0 + kc:shc0 + kc + 1]),
                             reads=[r_pt, r_gcol, r_modcol], writes=[r_dstT])
                    ev[0] += 1

            load_a(0)
            if ngroups > 1:
                load_a(1)
            stage_a(0)
            for g in range(ngroups):
                stage_b(g)
                if g + 1 < ngroups:
                    stage_a(g + 1)
                if g + 2 < ngroups:
                    load_a(g + 2)

        with ExitStack() as stB:
            hT, r_hT = sbuf(stB, "hT", [128, 16, T], BF16)
            with ExitStack() as st:
                norm_to_featmajor(st, lambda i: xctx[i * 128:(i + 1) * 128, :], 16, hT, r_hT, 0, 0, 0, "n1")
                K.barrier()
            if debug and stage <= 1:
                d = dbg_out("hT", [128, 16 * T])
                hf, r_hf = sbuf(stB, "hf", [128, T], F32)
                for kc in range(16):
                    K.op(dve, lambda: V.tensor_copy(out=hf[:], in_=hT[:, kc, :]), reads=[r_hT], writes=[r_hf])
                    K.dma(sp, d[:, kc * T:(kc + 1) * T], hf[:], reads=[r_hf], writes=[r_out], semres=r_hf)
            if stage <= 1:
                K.barrier()
                return nc, dbg

            oT_nsa, r_oTn = sbuf(stB, "oT_nsa", [128, 8, NQ], BF16)
            oT_diff, r_oTd = sbuf(stB, "oT_diff", [128, 8, NQ], BF16)
            w_in_v = w_in.rearrange("(kc p) n -> p kc n", p=128)

            with ExitStack() as st:
                wbufs = [sbuf(st, "wb%d" % i, [128, 4096], BF16) for i in range(3)]
                wctr = [0]

                def wnext():
                    t = wbufs[wctr[0] % 3]
                    wctr[0] += 1
                    return t

                def wload_in(c0, ncols, dst3, r_w):
                    K.dma(pool, dst3, w_in_v[:, :, c0:c0 + ncols], writes=[r_w])

                def _kvcol(g_, br, kv):
                    return OFF_NSA_KV + ((br * 2 + kv) * 2 + g_) * 128
                wlist = []
                if stage >= 2:
                    for h_ in range(4):
                        wlist += [(OFF_DQ + h_ * 256, 256), (OFF_DK + h_ * 256, 256), (OFF_DV + h_ * 256, 256)]
                if stage >= 3:
                    wlist.append((OFF_NSA_G, 24))
                    for g_ in range(2):
                        wlist += [(_kvcol(g_, 0, 0), 128), (_kvcol(g_, 0, 1), 128)]
                        for br_ in (1, 2):
                            wlist += [(_kvcol(g_, br_, 0), 128), (_kvcol(g_, br_, 1), 128)]
                        wlist += [((4 * g_) * 128, 256), ((4 * g_ + 2) * 128, 256)]
                wstate = {"emitted": 0, "cur": 0, "released": -1}
                wloaded = {}

                def _ws_emit():
                    while wstate["emitted"] < len(wlist) and wstate["emitted"] - 3 <= wstate["released"]:
                        j = wstate["emitted"]
                        c0_, nc_ = wlist[j]
                        t_, r_ = wbufs[j % 3]
                        v3 = t_[:, 0:16 * nc_].rearrange("p (k c) -> p k c", c=nc_)
                        wload_in(c0_, nc_, v3, r_)
                        wloaded[j] = (v3, r_)
                        wstate["emitted"] += 1

                def ws_next(c0_expect):
                    i = wstate["cur"]
                    _ws_emit()
                    assert i in wloaded, (i, wstate)
                    assert wlist[i][0] == c0_expect, (i, wlist[i], c0_expect)
                    wstate["cur"] += 1
                    return wloaded.pop(i)

                def ws_release():
                    wstate["released"] = wstate["cur"] - 1
                    _ws_emit()

                sqb = [sbuf(st, "sqb%d" % i, [128, 512], BF16) for i in range(2)]
                rsb = [sbuf(st, "rsb%d" % i, [128, 512], F32) for i in range(2)]
                nctr = [0]
                ps_a = [psum(st, "ps_a%d" % i, [128, 512], F32) for i in range(2)]
                ps_b = [psum(st, "ps_b%d" % i, [128, 512], F32) for i in range(5)]
                ps_t, r_ps_t = psum(st, "ps_t", [128, 1024], BF16)
                pctr = [0]

                ps_rot = list(ps_a) + [ps_b[2], ps_b[3]]

                def ps_next():
                    t = ps_rot[pctr[0] % len(ps_rot)]
                    pctr[0] += 1
                    return t

                def qknorm(ps_ap, r_ps, n, gcol, dst_ap, r_dst, view=None):
                    i = nctr[0] % 2
                    nctr[0] += 1
                    sq, r_sq = sqb[i]
                    rs, r_rs = rsb[i]
                    pss, r_pss = ps_b[4]
                    K.op(act, lambda: A.activation(out=sq[:, 0:n], in_=ps_ap, func=AF.Square), reads=[r_ps], writes=[r_sq])
                    K.op(pe, lambda: PE.matmul(pss[:, 0:n], lhsT=ones_bf[:], rhs=sq[:, 0:n], start=True, stop=True),
                         reads=[r_sq, r_onesbf], writes=[r_pss])
                    K.op(act, lambda: A.activation(out=rs[:, 0:n], in_=pss[:, 0:n], func=AF.Ln, scale=1.0 / 128, bias=EPS),
                         reads=[r_pss], writes=[r_rs])
                    K.op(act, lambda: A.activation(out=rs[:, 0:n], in_=rs[:, 0:n], func=AF.Exp, scale=-0.5),
                         reads=[r_rs], writes=[r_rs])
                    vw = view if view is not None else (lambda a: a)
                    K.op(dve, lambda: V.scalar_tensor_tensor(out=dst_ap, in0=vw(ps_ap), scalar=gcol, in1=vw(rs[:, 0:n]),
                                                             op0=ALU.mult, op1=ALU.mult),
                         reads=[r_ps, r_rs, r_qkg], writes=[r_dst])

                def projT(ps_ap, r_ps, w3, r_w, c0, t0, n):
                    for kc in range(16):
                        K.op(pe, lambda: PE.matmul(ps_ap, lhsT=w3[:, kc, c0:c0 + 128], rhs=hT[:, kc, t0:t0 + n],
                                                   start=(kc == 0), stop=(kc == 15)), reads=[r_w, r_hT], writes=[r_ps],
                             inc=(kc == 15))

                QCH = [(Q0, 512), (Q0 + 512, 512), (Q0 + 1024, 128)]

                if stage >= 2:
                  with ExitStack() as sd:
                    kTd = [sbuf(sd, "kTd%d" % m, [128, T], BF16) for m in range(2)]
                    qTd = [sbuf(sd, "qTd%d" % m, [128, NQ], BF16) for m in range(2)]
                    vaug, r_vaug = sbuf(sd, "vaugd", [128, 16, 258], BF16)
                    bd, r_bd = sbuf(sd, "bd", [128, 256], BF16)
                    dslots = []
                    for s_ in range(2):
                        dslots.append({
                            "ptd": sbuf(sd, "ptd%d" % s_, [128, 16 * 256], BF16),
                            "od": sbuf(sd, "od%d" % s_, [128, 256], F32),
                            "odb": sbuf(sd, "odb%d" % s_, [128, 256], BF16),
                            "junk": sbuf(sd, "junkd%d" % s_, [128, 256], F32),
                            "sm": sbuf(sd, "smd%d" % s_, [128, 8], F32),
                            "po": [ps_b[2 * s_], ps_b[2 * s_ + 1]],
                            "r_pst": K.res("pst%d" % s_),
                            "tcol": s_ * 256,
                        })
                    s_tiles = [ps_a[0], ps_a[1], ps_b[4]]
                    sctr = [0]
                    bgw = [sbuf(sd, "bgw%d" % i, [128, 16, 256], BF16) for i in range(2)]
                    bgb = [sbuf(sd, "bgb%d" % i, [1, 256], F32) for i in range(2)]
                    bgm = [sbuf(sd, "bgm%d" % i, [1, 256], F32) for i in range(2)]
                    w_ada_v2 = w_ada.rearrange("(kc p) n -> p kc n", p=128)
                    bgst = {"loaded": 0, "done": 0}
                    bgps = [ps_a[0], ps_a[1]]
                    NBG = 32

                    def bg_load():
                        c = bgst["loaded"]
                        if c >= NBG:
                            return
                        col0 = 4096 + c * 256
                        K.dma(pool, bgw[c % 2][0][:], w_ada_v2[:, :, col0:col0 + 256], writes=[bgw[c % 2][1]])
                        K.dma(sp, bgb[c % 2][0][:], b_ada[:, col0:col0 + 256], writes=[bgb[c % 2][1]])
                        bgst["loaded"] += 1

                    def bg_finish(c):
                        mr_, r_mr_ = bgm[c % 2]
                        pS, r_pS = bgps[c % 2]
                        for j in range(2):
                            K.op(pe, lambda: PE.matmul(pS[:, 256 + j:257 + j], lhsT=mr_[0:1, j * 128:(j + 1) * 128], rhs=ones_f[0:1, 0:1],
                                                       start=True, stop=True), reads=[r_mr_, r_onesf], writes=[r_pS], inc=(j == 1))
                        K.op(dve, lambda: V.tensor_copy(out=modcol[:, 32 + 2 * c:34 + 2 * c], in_=pS[:, 256:258]), reads=[r_pS],
                             writes=[r_modcol])

                    def bg_step():
                        c = bgst["done"]
                        if c > NBG:
                            return
                        bgst["done"] += 1
                        if c >= 1:
                            bg_finish(c - 1)
                        if c == NBG:
                            return
                        wt_, r_wt_ = bgw[c % 2]
                        br_, r_br_ = bgb[c % 2]
                        mr_, r_mr_ = bgm[c % 2]
                        pS, r_pS = bgps[c % 2]
                        for kc in range(16):
                            K.op(pe, lambda: PE.matmul(pS[0:1, 0:256], lhsT=scb[:, kc:kc + 1], rhs=wt_[:, kc, :], start=(kc == 0),
                                                       stop=(kc == 15)), reads=[r_scb, r_wt_], writes=[r_pS], inc=(kc == 15))
                        K.op(dve, lambda: V.tensor_tensor(out=mr_[:], in0=pS[0:1, 0:256], in1=br_[:], op=ALU.add),
                             reads=[r_pS, r_br_], writes=[r_mr_])
                        if c < 8:
                            K.dma(sp, g_scr[0:1, c * 256:(c + 1) * 256], mr_[:], reads=[r_mr_], writes=[r_gscr], semres=r_mr_)
                        elif c >= 24:
                            K.dma(sp, g_scr[1:2, (c - 24) * 256:(c - 23) * 256], mr_[:], reads=[r_mr_], writes=[r_gscr], semres=r_mr_)
                        bg_load()

                    bg_load()
                    bg_load()
                    K.op(pool, lambda: G.memset(vaug[:, :, 256:258], 1.0), writes=[r_vaug])
                    for h in range(4):
                        wq3, r_wq = ws_next(OFF_DQ + h * 256)
                        wk3, r_wk = ws_next(OFF_DK + h * 256)
                        wv3, r_wv = ws_next(OFF_DV + h * 256)
                        K.dma(sp, bd[:], _dram_ap(u_scr, (8 + h) * UL + 2048 - 127, [[1, 128], [1, 256]]), reads=[r_uscr],
                              writes=[r_bd])
                        for m in range(2):
                            for tcn in range(4):
                                pj, r_pj = ps_next()
                                projT(pj[:], r_pj, wk3, r_wk, m * 128, tcn * 512, 512)
                                qknorm(pj[:], r_pj, 512, qkg_t[:, 5:6], kTd[m][0][:, tcn * 512:(tcn + 1) * 512], kTd[m][1])
                            for (t0, n) in QCH:
                                pj, r_pj = ps_next()
                                projT(pj[:, 0:n], r_pj, wq3, r_wq, m * 128, t0, n)
                                qknorm(pj[:, 0:n], r_pj, n, qkg_t[:, 4:5], qTd[m][0][:, t0 - Q0:t0 - Q0 + n], qTd[m][1])
                        for i in range(16):
                            pj, r_pj = ps_next()
                            for kc in range(16):
                                K.op(pe, lambda: PE.matmul(pj[:, 0:256], lhsT=hT[:, kc, i * 128:(i + 1) * 128], rhs=wv3[:, kc, :],
                                                           start=(kc == 0), stop=(kc == 15)), reads=[r_hT, r_wv], writes=[r_pj],
                                     inc=(kc == 15))
                            if i % 2 == 0:
                                K.op(dve, lambda: V.tensor_copy(out=vaug[:, i, 0:256], in_=pj[:, 0:256]), reads=[r_pj],
                                     writes=[r_vaug])
                            else:
                                K.op(act, lambda: A.copy(out=vaug[:, i, 0:256], in_=pj[:, 0:256]), reads=[r_pj], writes=[r_vaug])
                        ws_release()
                        def diff_unit(qi, sl):
                            S_ = dslots[sl]
                            ptd, r_ptd = S_["ptd"]
                            od, r_od = S_["od"]
                            odb, r_odb = S_["odb"]
                            junkd, r_junkd = S_["junk"]
                            sm, r_sm = S_["sm"]
                            po = S_["po"]
                            r_ptr = S_["r_pst"]
                            tc0 = S_["tcol"]
                            qb = QT0 + qi
                            nj = qb + 1
                            for jp in range(0, nj, 2):
                                pS, r_pS = s_tiles[sctr[0] % len(s_tiles)]
                                sctr[0] += 1
                                js = [j for j in (jp, jp + 1) if j < nj]
                                for jj, j in enumerate(js):
                                    near = j >= qb - 1
                                    for m in range(2):
                                        col = jj * 256 + m * 128
                                        K.op(pe, lambda: PE.matmul(pS[:, col:col + 128], lhsT=kTd[m][0][:, j * 128:(j + 1) * 128],
                                                                   rhs=qTd[m][0][:, qi * 128:(qi + 1) * 128], start=True,
                                                                   stop=not near), reads=[kTd[m][1], qTd[m][1]], writes=[r_pS],
                                             inc=not near)
                                        if near:
                                            off = 0 if j == qb else 128
                                            K.op(pe, lambda: PE.matmul(pS[:, col:col + 128], lhsT=jmat[:], rhs=bd[:, off:off + 128],
                                                                       start=False, stop=True), reads=[r_jmat, r_bd],
                                                 writes=[r_pS])
                                ncol = len(js) * 256
                                bias_ = cm[:, 0:1] if jp < 8 else 0.0
                                rd = [r_pS, r_cm] if jp < 8 else [r_pS]
                                K.op(act, lambda: A.activation(out=ptd[:, jp * 256:jp * 256 + ncol], in_=pS[:, 0:ncol], func=AF.Exp,
                                                               bias=bias_), reads=rd, writes=[r_ptd])
                                yield "s"
                            for m in range(2):
                                for j in range(nj):
                                    K.op(pe, lambda: PE.matmul(po[m][0][:, 0:257], lhsT=ptd[:, j * 256 + m * 128:j * 256 + m * 128 + 128],
                                                               rhs=vaug[:, j, 0:257], start=(j == 0), stop=(j == nj - 1)),
                                         reads=[r_ptd, r_vaug], writes=[po[m][1]], inc=(j == nj - 1))
                                    if j % 4 == 3:
                                        yield "av"
                                yield "av"
                            K.op(dve, lambda: V.tensor_scalar(out=sm[:, 0:1], in0=po[0][0][:, 256:257], scalar1=1e-20,
                                                              scalar2=None, op0=ALU.max), reads=[po[0][1]], writes=[r_sm])
                            K.op(dve, lambda: V.tensor_scalar(out=sm[:, 1:2], in0=po[1][0][:, 256:257], scalar1=1e-20,
                                                              scalar2=None, op0=ALU.max), reads=[po[1][1]], writes=[r_sm])
                            yield "e"
                            K.op(dve, lambda: V.reciprocal(out=sm[:, 0:2], in_=sm[:, 0:2]), reads=[r_sm], writes=[r_sm])
                            yield "e"
                            K.op(dve, lambda: V.tensor_tensor(out=sm[:, 2:3], in0=sm[:, 1:2], in1=nlam[:], op=ALU.mult),
                                 reads=[r_sm, r_nlam], writes=[r_sm])
                            yield "e"
                            K.op(dve, lambda: V.tensor_scalar(out=od[:], in0=po[0][0][:, 0:256], scalar1=sm[:, 0:1], scalar2=None,
                                                              op0=ALU.mult), reads=[po[0][1], r_sm], writes=[r_od])
                            yield "e"
                            K.op(dve, lambda: V.scalar_tensor_tensor(out=od[:], in0=po[1][0][:, 0:256], scalar=sm[:, 2:3], in1=od[:],
                                                                     op0=ALU.mult, op1=ALU.add), reads=[po[1][1], r_sm, r_od],
                                 writes=[r_od])
                            yield "e"
                            K.op(act, lambda: A.activation(out=junkd[:], in_=od[:], func=AF.Square, accum_out=sm[:, 3:4]),
                                 reads=[r_od], writes=[r_junkd, r_sm])
                            yield "e"
                            K.op(act, lambda: A.activation(out=sm[:, 4:5], in_=sm[:, 3:4], func=AF.Ln, scale=1.0 / 256, bias=EPS),
                                 reads=[r_sm], writes=[r_sm])
                            yield "e"
                            K.op(act, lambda: A.activation(out=sm[:, 4:5], in_=sm[:, 4:5], func=AF.Exp, scale=-0.5),
                                 reads=[r_sm], writes=[r_sm])
                            yield "e"
                            K.op(dve, lambda: V.scalar_tensor_tensor(out=odb[:], in0=od[:], scalar=sm[:, 4:5], in1=sublbc[:],
                                                                     op0=ALU.mult, op1=ALU.mult), reads=[r_od, r_sm, r_sublbc],
                                 writes=[r_odb])
                            yield "e"
                            for hf in range(2):
                                K.op(pe, lambda: PE.transpose(out=ps_t[:, tc0 + hf * 128:tc0 + (hf + 1) * 128],
                                                              in_=odb[:, hf * 128:(hf + 1) * 128], identity=ident[:]),
                                     reads=[r_odb, r_ident], writes=[r_ptr], inc=(hf == 1))
                            yield "e"
                            K.op(act, lambda: A.copy(out=oT_diff[:, 2 * h:2 * h + 2, qi * 128:(qi + 1) * 128],
                                                     in_=ps_t[:, tc0:tc0 + 256].rearrange("p (a b) -> p a b", b=128)),
                                 reads=[r_ptr], writes=[r_oTd])

                        active = []
                        free_slots = [0, 1]
                        nxt = 0
                        primed = False
                        while active or nxt < NQT:
                            while len(active) < 2 and nxt < NQT:
                                sl = free_slots.pop(0)
                                gnr = diff_unit(nxt, sl)
                                nxt += 1
                                if not primed and active == []:
                                    primed = True
                                    live = True
                                    try:
                                        while next(gnr) == "s":
                                            pass
                                    except StopIteration:
                                        live = False
                                    if live:
                                        active.append((gnr, sl))
                                    else:
                                        free_slots.append(sl)
                                else:
                                    active.append((gnr, sl))
                            for ent in list(active):
                                try:
                                    next(ent[0])
                                except StopIteration:
                                    active.remove(ent)
                                    free_slots.append(ent[1])
                                    bg_step()
                    while bgst["done"] <= NBG:
                        bg_step()
                    K.barrier()

                if stage >= 3:
                  with ExitStack() as sn:
                    arena, _ = sbuf(sn, "arena", [128, 12288], BF16)
                    w1t = [(arena[:, 4096 + i * 4096:8192 + i * 4096].rearrange("p (l j) -> p l j", j=128), K.res("w1t%d" % i))
                           for i in range(2)]
                    w2t = [sbuf(sn, "w2t%d" % i, [128, 128], BF16) for i in range(2)]
                    pet = [sbuf(sn, "pet%d" % i, [128, 32], BF16) for i in range(2)]
                    cst, r_cst = sbuf(sn, "cst", [128, 2], F32)
                    ebf, r_ebf = sbuf(sn, "ebf", [32, T], BF16)
                    vcaug, r_vcaug = sbuf(sn, "vcaug", [128, 162], BF16)
                    gates, r_gates = sbuf(sn, "gates", [128, NQT, 24], F32)
                    smk, r_smk = sbuf(sn, "smk", [128, NQT, 64], F32)
                    w4t, r_w4t = sbuf(sn, "w4t", [128, 128], BF16)
                    K.dma(sp, smk[:], selmask, writes=[r_smk])
                    K.dma(pool, ebf[:], e_mat, writes=[r_ebf])
                    K.op(dve, lambda: V.memset(vcaug[:], 1.0), writes=[r_vcaug])
                    K.dma(pool, vcaug[0:127, 129:161], ovl, writes=[r_vcaug])
                    K.op(pool, lambda: G.memset(tmpf[:], 0.0), reads=[r_tmpf], writes=[r_tmpf])
                    K.op(pool, lambda: G.affine_select(out=tmpf[:], in_=tmpf[:], pattern=[[-1, 128]], compare_op=ALU.is_ge,
                                                       fill=NEGV, base=126, channel_multiplier=-1), reads=[r_tmpf], writes=[r_tmpf])
                    K.op(dve, lambda: V.tensor_copy(out=w4t[:], in_=tmpf[:]), reads=[r_tmpf], writes=[r_w4t])
                    for i in range(2):
                        K.dma(pool, w2t[i][0][:], cmp_w2[i], writes=[w2t[i][1]])
                        K.dma(pool, pet[i][0][:], cmp_pe[i], writes=[pet[i][1]])
                    wg3, r_wg = ws_next(OFF_NSA_G)
                    for qi in range(NQT):
                        pj, r_pj = ps_next()
                        for kc in range(16):
                            K.op(pe, lambda: PE.matmul(pj[:, 0:24], lhsT=hT[:, kc, Q0 + qi * 128:Q0 + (qi + 1) * 128], rhs=wg3[:, kc, :],
                                                       start=(kc == 0), stop=(kc == 15)), reads=[r_hT, r_wg], writes=[r_pj],
                                 inc=(kc == 15))
                        K.op(act, lambda: A.activation(out=gates[:, qi, :], in_=pj[:, 0:24], func=AF.Exp, scale=-1.0),
                             reads=[r_pj], writes=[r_gates])
                    ws_release()
                    K.op(dve, lambda: V.tensor_scalar(out=gates[:], in0=gates[:], scalar1=1.0, scalar2=None, op0=ALU.add),
                         reads=[r_gates], writes=[r_gates])
                    K.op(dve, lambda: V.reciprocal(out=gates[:], in_=gates[:]), reads=[r_gates], writes=[r_gates])

                    zT = [(arena[:, i * 2048:(i + 1) * 2048], K.res("zT%d" % i)) for i in range(2)]
                    kTn = [sbuf(sn, "kTn%d" % i, [128, T], BF16) for i in range(2)]
                    vaugn = [sbuf(sn, "vaugn%d" % i, [128, 16, 130], BF16) for i in range(2)]
                    qTn, r_qTn = sbuf(sn, "qTn", [128, NQT, 4, 128], BF16)
                    kcT, r_kcT = sbuf(sn, "kcT", [128, 128], BF16)
                    glu = [sbuf(sn, "glu%d" % i, [128, 128], BF16) for i in range(2)]
                    xg, r_xg = sbuf(sn, "xg", [128, 128], F32)
                    tg, r_tg = sbuf(sn, "tg", [128, 128], F32)
                    bn, r_bn = sbuf(sn, "bn", [128, 4, 256], BF16)
                    cbt = [sbuf(sn, "cbt%d" % i, [128, 4, 128], BF16) for i in range(2)]
                    pts, r_pts = arena[:, 0:8192].rearrange("p (a b) -> p a b", b=512), K.res("pts")
                    ptw, r_ptw = arena[:, 8192:8192 + 2560].rearrange("p (a b) -> p a b", b=512), K.res("ptw")
                    ptc, r_ptc = arena[:, 11264:11776], K.res("ptc")
                    oaccs = [sbuf(sn, "oacc%d" % i, [128, 4, 128], F32) for i in range(2)]
                    obf, r_obf = sbuf(sn, "obf", [128, 4, 128], BF16)
                    imp, r_imp = sbuf(sn, "imp", [128, 32], F32)
                    scr_, r_scr_ = sbuf(sn, "scr_", [128, 32], F32)
                    wk_, r_wk_ = sbuf(sn, "wk_", [128, 32], F32)
                    m8, r_m8 = sbuf(sn, "m8", [128, 16], F32)
                    negb, r_negb = sbuf(sn, "negb", [128, 32], BF16)
                    negT, r_negT = sbuf(sn, "negT", [32, 128], BF16)
                    smh, r_smh = sbuf(sn, "smh", [128, 16], F32)
                    smt, r_smt = sbuf(sn, "smt", [128, 16], F32)
                    r_pst_lo = K.res("pst_lo")
                    r_pst_hi = K.res("pst_hi")
                    impt, r_impt = sbuf(sn, "impt", [128, 4, 32], F32)
                    thr, r_thr = sbuf(sn, "thr", [128, 1], F32)
                    tmp2h = [sbuf(sn, "tmp2h%d" % i, [128, 2, 128], F32) for i in range(1)] * 2
                    tmp2t = [sbuf(sn, "tmp2t%d" % i, [128, 2, 128], F32) for i in range(1)] * 2
                    for i in range(2):
                        K.op(pool, lambda: G.memset(vaugn[i][0][:, :, 128:130], 1.0), writes=[vaugn[i][1]])

                    for g in range(2):
                        def kvcol(br, kv):
                            return OFF_NSA_KV + ((br * 2 + kv) * 2 + g) * 128
                        K.dma(sp, bn[:], _dram_ap(u_scr, (4 * g) * UL + 2048 - 127, [[1, 128], [UL, 4], [1, 256]]), reads=[r_uscr],
                              writes=[r_bn])
                        if g == 1:
                            K.barrier()
                            ps_rot[:] = list(ps_a) + [ps_b[2], ps_b[3]]
                        for i in range(2):
                            K.dma(pool, w1t[i][0], cmp_w1[i].rearrange("(l d) j -> d l j", d=128), writes=[w1t[i][1]])
                        if g == 0:
                            for i in range(2):
                                pt_, r_pt_ = ps_next()
                                for l in range(32):
                                    K.op(pe, lambda: PE.matmul(pt_[:, 0:1], lhsT=w1t[i][0][:, l, :], rhs=pet[i][0][:, l:l + 1],
                                                               start=(l == 0), stop=(l == 31)), reads=[w1t[i][1], pet[i][1]],
                                         writes=[r_pt_], inc=(l == 31))
                                K.op(dve, lambda: V.tensor_copy(out=cst[:, i:i + 1], in_=pt_[:, 0:1]), reads=[r_pt_], writes=[r_cst])
                        for i in range(2):
                            wz3, r_wz = ws_next(kvcol(0, i))
                            for tcn in range(4):
                                pj, r_pj = ps_next()
                                projT(pj[:], r_pj, wz3, r_wz, 0, tcn * 512, 512)
                                if tcn % 2 == 0:
                                    K.op(dve, lambda: V.tensor_copy(out=zT[i][0][:, tcn * 512:(tcn + 1) * 512], in_=pj[:]), reads=[r_pj],
                                         writes=[zT[i][1]])
                                else:
                                    K.op(act, lambda: A.copy(out=zT[i][0][:, tcn * 512:(tcn + 1) * 512], in_=pj[:]), reads=[r_pj],
                                         writes=[zT[i][1]])
                            if tcn == 3:
                                ws_release()
                        for i in range(2):
                            pj, r_pj = ps_next()
                            for l in range(32):
                                K.op(pe, lambda: PE.matmul(pj[:, 0:127], lhsT=w1t[i][0][:, l, :], rhs=zT[i][0][:, l:l + 16 * 126 + 1:16],
                                                           start=(l == 0), stop=(l == 31)), reads=[w1t[i][1], zT[i][1]], writes=[r_pj],
                                     inc=(l == 31))
                            K.op(dve, lambda: V.tensor_scalar(out=xg[:, 0:127], in0=pj[:, 0:127], scalar1=cst[:, i:i + 1], scalar2=None,
                                                              op0=ALU.add), reads=[r_pj, r_cst], writes=[r_xg])
                            K.op(dve, lambda: V.tensor_tensor(out=tg[:, 0:127], in0=xg[:, 0:127], in1=xg[:, 0:127], op=ALU.mult),
                                 reads=[r_xg], writes=[r_tg])
                            K.op(dve, lambda: V.tensor_scalar(out=tg[:, 0:127], in0=tg[:, 0:127], scalar1=0.044715, scalar2=1.0,
                                                              op0=ALU.mult, op1=ALU.add), reads=[r_tg], writes=[r_tg])
                            K.op(dve, lambda: V.tensor_tensor(out=tg[:, 0:127], in0=tg[:, 0:127], in1=xg[:, 0:127], op=ALU.mult),
                                 reads=[r_tg, r_xg], writes=[r_tg])
                            K.op(act, lambda: A.activation(out=tg[:, 0:127], in_=tg[:, 0:127], func=AF.Exp, scale=-1.5957691216),
                                 reads=[r_tg], writes=[r_tg])
                            K.op(dve, lambda: V.tensor_scalar(out=tg[:, 0:127], in0=tg[:, 0:127], scalar1=1.0, scalar2=None,
                                                              op0=ALU.add), reads=[r_tg], writes=[r_tg])
                            K.op(dve, lambda: V.reciprocal(out=tg[:, 0:127], in_=tg[:, 0:127]), reads=[r_tg], writes=[r_tg])
                            K.op(dve, lambda: V.tensor_tensor(out=glu[i][0][:, 0:127], in0=tg[:, 0:127], in1=xg[:, 0:127], op=ALU.mult),
                                 reads=[r_tg, r_xg], writes=[glu[i][1]])
                        pj, r_pj = ps_next()
                        K.op(pe, lambda: PE.matmul(pj[:, 0:127], lhsT=w2t[0][0][:], rhs=glu[0][0][:, 0:127], start=True, stop=True),
                             reads=[w2t[0][1], glu[0][1]], writes=[r_pj])
                        qknorm(pj[:, 0:127], r_pj, 127, qkg_t[:, 1:2], kcT[:, 0:127], r_kcT)
                        pj, r_pj = ps_next()
                        K.op(pe, lambda: PE.matmul(pj[0:127, 0:128], lhsT=glu[1][0][:, 0:127], rhs=w2t[1][0][:], start=True, stop=True),
                             reads=[w2t[1][1], glu[1][1]], writes=[r_pj])
                        K.op(dve, lambda: V.tensor_copy(out=vcaug[0:127, 0:128], in_=pj[0:127, 0:128]), reads=[r_pj], writes=[r_vcaug])
                        for bi, br in enumerate((1, 2)):
                            wz3, r_wz = ws_next(kvcol(br, 0))
                            for tcn in range(4):
                                pj, r_pj = ps_next()
                                projT(pj[:], r_pj, wz3, r_wz, 0, tcn * 512, 512)
                                qknorm(pj[:], r_pj, 512, qkg_t[:, 1 + br:2 + br], kTn[bi][0][:, tcn * 512:(tcn + 1) * 512], kTn[bi][1])
                            ws_release()
                            wz3, r_wz = ws_next(kvcol(br, 1))
                            for i in range(16):
                                pj, r_pj = ps_next()
                                for kc in range(16):
                                    K.op(pe, lambda: PE.matmul(pj[:, 0:128], lhsT=hT[:, kc, i * 128:(i + 1) * 128], rhs=wz3[:, kc, :],
                                                               start=(kc == 0), stop=(kc == 15)), reads=[r_hT, r_wz], writes=[r_pj],
                                         inc=(kc == 15))
                                if i % 2 == 0:
                                    K.op(dve, lambda: V.tensor_copy(out=vaugn[bi][0][:, i, 0:128], in_=pj[:, 0:128]), reads=[r_pj],
                                         writes=[vaugn[bi][1]])
                                else:
                                    K.op(act, lambda: A.copy(out=vaugn[bi][0][:, i, 0:128], in_=pj[:, 0:128]), reads=[r_pj],
                                         writes=[vaugn[bi][1]])
                            ws_release()
                        for hp in range(2):
                            wq3, r_wq = ws_next((4 * g + 2 * hp) * 128)
                            for hh in range(2):
                                hl = 2 * hp + hh
                                for (t0, n) in QCH:
                                    pj, r_pj = ps_next()
                                    projT(pj[:, 0:n], r_pj, wq3, r_wq, hh * 128, t0, n)
                                    q0 = (t0 - Q0) // 128
                                    qknorm(pj[:, 0:n], r_pj, n, qkg_t[:, 0:1], qTn[:, q0:q0 + n // 128, hl, :], r_qTn,
                                           view=lambda a: a.rearrange("p (a b) -> p a b", b=128))
                            ws_release()
                        K.barrier()
                        ps_rot[:] = list(ps_a) + [ps_b[4]]
                        def nsa_head(qi):
                            oacc_, r_oacc_ = oaccs[qi % 2]
                            qb = QT0 + qi
                            qrhs = qTn[:, qi].rearrange("p a b -> p (a b)")
                            cb, r_cb = cbt[qi % 2]
                            K.dma(sp, cb[0:127], _dram_ap(u_scr, (4 * g) * UL + 128 * qb + 1, [[16, 127], [UL, 4], [1, 128]]),
                                  reads=[r_uscr], writes=[r_cb])
                            pS, r_pS = ps_next()
                            K.op(pe, lambda: PE.matmul(pS[0:127, :], lhsT=kcT[:, 0:127], rhs=qrhs, start=True, stop=False),
                                 reads=[r_kcT, r_qTn], writes=[r_pS], inc=False)
                            K.op(pe, lambda: PE.matmul(pS[0:127, :], lhsT=j127[0:127, 0:127],
                                                       rhs=cb[0:127].rearrange("p a b -> p (a b)"), start=False, stop=True),
                                 reads=[r_j127, r_cb], writes=[r_pS])
                            K.op(act, lambda: A.activation(out=ptc[0:127, :], in_=pS[0:127, :], func=AF.Exp, bias=cm[0:127, 2:3]),
                                 reads=[r_pS, r_cm], writes=[r_ptc])
                            yield
                            pc_ = [ps_b[0], ps_b[1]]
                            for hl in range(4):
                                pcb, r_pcb = pc_[hl // 2]
                                c0 = (hl % 2) * 161
                                K.op(pe, lambda: PE.matmul(pcb[:, c0:c0 + 161], lhsT=ptc[0:127, hl * 128:(hl + 1) * 128],
                                                           rhs=vcaug[0:127, 0:161], start=True, stop=True), reads=[r_ptc, r_vcaug],
                                     writes=[r_pcb])
                            for b_ in range(2):
                                pcb, r_pcb = pc_[b_]
                                K.op(dve, lambda: V.tensor_scalar(out=smh[:, 2 * b_:2 * b_ + 2], in0=pcb[:, 128:290:161], scalar1=1e-20,
                                                                  scalar2=None, op0=ALU.max), reads=[r_pcb], writes=[r_smh])
                            yield
                            K.op(dve, lambda: V.reciprocal(out=smh[:, 0:4], in_=smh[:, 0:4]), reads=[r_smh], writes=[r_smh])
                            K.op(dve, lambda: V.tensor_tensor(out=smh[:, 4:8], in0=smh[:, 0:4], in1=gates[:, qi, 12 * g:12 * g + 10:3],
                                                              op=ALU.mult), reads=[r_smh, r_gates], writes=[r_smh])
                            for b_ in range(2):
                                pcb, r_pcb = pc_[b_]
                                pv = pcb[:, 0:322].rearrange("p (h c) -> p h c", c=161)
                                K.op(dve, lambda: V.tensor_tensor(out=oacc_[:, 2 * b_:2 * b_ + 2, :], in0=pv[:, :, 0:128],
                                                                  in1=smh[:, 4 + 2 * b_:6 + 2 * b_].unsqueeze(2).to_broadcast([128, 2, 128]),
                                                                  op=ALU.mult), reads=[r_pcb, r_smh], writes=[r_oacc_])
                                K.op(dve, lambda: V.tensor_tensor(out=impt[:, 2 * b_:2 * b_ + 2, :], in0=pv[:, :, 129:161],
                                                                  in1=smh[:, 2 * b_:2 * b_ + 2].unsqueeze(2).to_broadcast([128, 2, 32]),
                                                                  op=ALU.mult), reads=[r_pcb, r_smh], writes=[r_impt])
                            yield
                            K.op(dve, lambda: V.tensor_reduce(out=imp[:], in_=impt[:].rearrange("p h s -> p s h"), axis=AX.X, op=ALU.add),
                                 reads=[r_impt], writes=[r_imp])
                            for jj in range(5):
                                j = qb - 4 + jj
                                pS, r_pS = ps_next()
                                hasb = jj in (0, 3, 4)
                                K.op(pe, lambda: PE.matmul(pS[:], lhsT=kTn[1][0][:, j * 128:(j + 1) * 128], rhs=qrhs, start=True,
                                                           stop=not hasb), reads=[kTn[1][1], r_qTn], writes=[r_pS], inc=not hasb)
                                if jj == 0:
                                    K.op(pe, lambda: PE.matmul(pS[:], lhsT=jmat[:], rhs=w4t[:, 0:128].unsqueeze(1).to_broadcast([128, 4, 128]),
                                                               start=False, stop=True), reads=[r_jmat, r_w4t], writes=[r_pS])
                                elif jj >= 3:
                                    off = 0 if jj == 4 else 128
                                    K.op(pe, lambda: PE.matmul(pS[:], lhsT=jmat[:], rhs=bn[:, :, off:off + 128], start=False, stop=True),
                                         reads=[r_jmat, r_bn], writes=[r_pS])
                                bias_ = cm[:, 0:1] if j < 8 else 0.0
                                rd = [r_pS, r_cm] if j < 8 else [r_pS]
                                K.op(act, lambda: A.activation(out=ptw[:, jj, :], in_=pS[:], func=AF.Exp, bias=bias_), reads=rd,
                                     writes=[r_ptw])
                                yield
                            pw_ = [ps_b[0], ps_b[1]]
                            for hl in range(4):
                                pwb, r_pwb = pw_[hl // 2]
                                c0 = (hl % 2) * 129
                                for jj in range(5):
                                    j = qb - 4 + jj
                                    K.op(pe, lambda: PE.matmul(pwb[:, c0:c0 + 129], lhsT=ptw[:, jj, hl * 128:(hl + 1) * 128],
                                                               rhs=vaugn[1][0][:, j, 0:129], start=(jj == 0), stop=(jj == 4)),
                                         reads=[r_ptw, vaugn[1][1]], writes=[r_pwb], inc=(jj == 4))
                                yield
                            K.op(dve, lambda: V.tensor_tensor(out=scr_[:], in0=imp[:], in1=smk[:, qi, 0:32], op=ALU.mult),
                                 reads=[r_imp, r_smk], writes=[r_scr_])
                            K.op(dve, lambda: V.tensor_tensor(out=scr_[:], in0=scr_[:], in1=smk[:, qi, 32:64], op=ALU.add),
                                 reads=[r_scr_, r_smk], writes=[r_scr_])
                            yield
                            K.op(dve, lambda: V.max(out=m8[:, 0:8], in_=scr_[:]), reads=[r_scr_], writes=[r_m8])
                            K.op(dve, lambda: V.match_replace(out=wk_[:], in_to_replace=m8[:, 0:8], in_values=scr_[:], imm_value=-3e4),
                                 reads=[r_scr_, r_m8], writes=[r_wk_])
                            K.op(dve, lambda: V.max(out=m8[:, 8:16], in_=wk_[:]), reads=[r_wk_, r_m8], writes=[r_m8])
                            yield
                            K.op(dve, lambda: V.tensor_reduce(out=thr[:], in_=m8[:, 8:16], axis=AX.X, op=ALU.min), reads=[r_m8],
                                 writes=[r_thr])
                            K.op(dve, lambda: V.tensor_scalar(out=scr_[:], in0=scr_[:], scalar1=thr[:], scalar2=None, op0=ALU.is_ge),
                                 reads=[r_scr_, r_thr], writes=[r_scr_])
                            K.op(dve, lambda: V.tensor_scalar(out=negb[:], in0=scr_[:], scalar1=-NEGV, scalar2=NEGV, op0=ALU.mult,
                                                              op1=ALU.add), reads=[r_scr_], writes=[r_negb])
                            yield
                            ptnb, r_ptn = ps_t[:, 512:1024], r_pst_hi
                            K.op(pe, lambda: PE.transpose(out=ptnb[0:32, 0:128], in_=negb[:, 0:32], identity=ident[:]),
                                 reads=[r_negb, r_ident], writes=[r_ptn])
                            K.op(dve, lambda: V.tensor_copy(out=negT[:], in_=ptnb[0:32, 0:128]), reads=[r_ptn], writes=[r_negT])
                            yield
                            for b_ in range(2):
                                K.op(dve, lambda: V.tensor_scalar(out=smh[:, 8 + 2 * b_:10 + 2 * b_], in0=pw_[b_][0][:, 128:258:129],
                                                                  scalar1=1e-20, scalar2=None, op0=ALU.max), reads=[pw_[b_][1]],
                                     writes=[r_smh])
                            K.op(dve, lambda: V.reciprocal(out=smh[:, 8:12], in_=smh[:, 8:12]), reads=[r_smh], writes=[r_smh])
                            K.op(dve, lambda: V.tensor_tensor(out=smh[:, 8:12], in0=smh[:, 8:12], in1=gates[:, qi, 12 * g + 2:12 * g + 12:3],
                                                              op=ALU.mult), reads=[r_smh, r_gates], writes=[r_smh])
                            yield
                            for b_ in range(2):
                                pb_, r_pb_ = pw_[b_]
                                pv = pb_[:, 0:258].rearrange("p (h c) -> p h c", c=129)
                                t2, r_t2 = tmp2h[b_]
                                K.op(dve, lambda: V.tensor_tensor(out=t2[:], in0=pv[:, :, 0:128],
                                                                  in1=smh[:, 8 + 2 * b_:10 + 2 * b_].unsqueeze(2).to_broadcast([128, 2, 128]),
                                                                  op=ALU.mult), reads=[r_pb_, r_smh], writes=[r_t2])
                                K.op(dve, lambda: V.tensor_tensor(out=oacc_[:, 2 * b_:2 * b_ + 2, :], in0=oacc_[:, 2 * b_:2 * b_ + 2, :],
                                                                  in1=t2[:], op=ALU.add), reads=[r_oacc_, r_t2], writes=[r_oacc_])
                                yield
                            nj = qb + 1
                            for j in range(nj):
                                pS, r_pS = ps_next()
                                near = j >= qb - 1
                                K.op(pe, lambda: PE.matmul(pS[:], lhsT=kTn[0][0][:, j * 128:(j + 1) * 128], rhs=qrhs, start=True,
                                                           stop=False), reads=[kTn[0][1], r_qTn], writes=[r_pS], inc=False)
                                if j != qb:
                                    K.op(pe, lambda: PE.matmul(pS[:], lhsT=ebf[:, j * 128:(j + 1) * 128],
                                                               rhs=negT[:, 0:128].unsqueeze(1).to_broadcast([32, 4, 128]), start=False,
                                                               stop=not near), reads=[r_ebf, r_negT], writes=[r_pS], inc=not near)
                                if near:
                                    off = 0 if j == qb else 128
                                    K.op(pe, lambda: PE.matmul(pS[:], lhsT=jmat[:], rhs=bn[:, :, off:off + 128], start=False, stop=True),
                                         reads=[r_jmat, r_bn], writes=[r_pS])
                                bias_ = cm[:, 0:1] if j < 8 else 0.0
                                rd = [r_pS, r_cm] if j < 8 else [r_pS]
                                K.op(act, lambda: A.activation(out=pts[:, j, :], in_=pS[:], func=AF.Exp, bias=bias_), reads=rd,
                                     writes=[r_pts])
                                yield
                            px_ = [ps_b[2], ps_b[3]]
                            for hl in range(4):
                                pxb, r_pxb = px_[hl // 2]
                                c0 = (hl % 2) * 129
                                for j in range(nj):
                                    K.op(pe, lambda: PE.matmul(pxb[:, c0:c0 + 129], lhsT=pts[:, j, hl * 128:(hl + 1) * 128],
                                                               rhs=vaugn[0][0][:, j, 0:129], start=(j == 0), stop=(j == nj - 1)),
                                         reads=[r_pts, vaugn[0][1]], writes=[r_pxb], inc=(j == nj - 1))
                        def nsa_tail(qi):
                            oacc_, r_oacc_ = oaccs[qi % 2]
                            qb = QT0 + qi
                            px_ = [ps_b[2], ps_b[3]]
                            for b_ in range(2):
                                K.op(dve, lambda: V.tensor_scalar(out=smt[:, 12 + 2 * b_:14 + 2 * b_], in0=px_[b_][0][:, 128:258:129],
                                                                  scalar1=1e-20, scalar2=None, op0=ALU.max), reads=[px_[b_][1]],
                                     writes=[r_smt])
                            K.op(dve, lambda: V.reciprocal(out=smt[:, 12:16], in_=smt[:, 12:16]), reads=[r_smt], writes=[r_smt])
                            yield
                            K.op(dve, lambda: V.tensor_tensor(out=smt[:, 12:16], in0=smt[:, 12:16], in1=gates[:, qi, 12 * g + 1:12 * g + 11:3],
                                                              op=ALU.mult), reads=[r_smt, r_gates], writes=[r_smt])
                            yield
                            for bi, pp_ in ((1, px_),):
                                for b_ in range(2):
                                    pb_, r_pb_ = pp_[b_]
                                    pv = pb_[:, 0:258].rearrange("p (h c) -> p h c", c=129)
                                    t2, r_t2 = tmp2t[b_]
                                    cc0 = 8 + 4 * bi + 2 * b_
                                    K.op(dve, lambda: V.tensor_tensor(out=t2[:], in0=pv[:, :, 0:128],
                                                                      in1=smt[:, cc0:cc0 + 2].unsqueeze(2).to_broadcast([128, 2, 128]),
                                                                      op=ALU.mult), reads=[r_pb_, r_smt], writes=[r_t2])
                                    if bi == 0:
                                        K.op(dve, lambda: V.tensor_tensor(out=oacc_[:, 2 * b_:2 * b_ + 2, :], in0=oacc_[:, 2 * b_:2 * b_ + 2, :],
                                                                          in1=t2[:], op=ALU.add), reads=[r_oacc_, r_t2], writes=[r_oacc_])
                                    else:
                                        K.op(dve, lambda: V.tensor_tensor(out=obf[:, 2 * b_:2 * b_ + 2, :], in0=oacc_[:, 2 * b_:2 * b_ + 2, :],
                                                                          in1=t2[:], op=ALU.add), reads=[r_oacc_, r_t2], writes=[r_obf])
                                    yield
                            yield
                            ptrb, r_ptr = ps_t[:, 0:512], r_pst_lo
                            for hl in range(4):
                                K.op(pe, lambda: PE.transpose(out=ptrb[:, hl * 128:(hl + 1) * 128], in_=obf[:, hl, :], identity=ident[:]),
                                     reads=[r_obf, r_ident], writes=[r_ptr], inc=(hl == 3))
                            yield
                            K.op(act, lambda: A.copy(out=oT_nsa[:, 4 * g:4 * g + 4, qi * 128:(qi + 1) * 128],
                                                     in_=ptrb[:, 0:512].rearrange("p (a b) -> p a b", b=128)),
                                 reads=[r_ptr], writes=[r_oTn])
                        prev_tail = None
                        for qi in range(NQT):
                            hg = nsa_head(qi)
                            if prev_tail is not None:
                                t_alive = True
                                h_alive = True
                                while t_alive:
                                    if h_alive:
                                        try:
                                            next(hg)
                                        except StopIteration:
                                            h_alive = False
                                    try:
                                        next(prev_tail)
                                    except StopIteration:
                                        t_alive = False
                                if h_alive:
                                    for _ in hg:
                                        pass
                            else:
                                for _ in hg:
                                    pass
                            prev_tail = nsa_tail(qi)
                        for _ in prev_tail:
                            pass

                    K.barrier()
                K.barrier()

            if debug:
                of, r_of = sbuf(stB, "of", [128, 16 * NQ], F32)
                d = dbg_out("oT", [128, 16 * NQ])
                K.op(dve, lambda: V.tensor_copy(out=of[:, 0:8 * NQ], in_=oT_nsa[:].rearrange("p a b -> p (a b)")), reads=[r_oTn],
                     writes=[r_of])
                K.op(dve, lambda: V.tensor_copy(out=of[:, 8 * NQ:16 * NQ], in_=oT_diff[:].rearrange("p a b -> p (a b)")),
                     reads=[r_oTd], writes=[r_of])
                K.dma(sp, d, of[:], reads=[r_of], writes=[r_out], semres=r_of)
            if stage <= 3:
                K.barrier()
                return nc, dbg

            with ExitStack() as st:
                mergedT, r_mT = sbuf(st, "mergedT", [128, 16, NQ], BF16)
                wbufs = [sbuf(st, "wc%d" % i, [128, 4096], BF16) for i in range(5)]
                g1bc, r_g1bc = sbuf(st, "g1bc", [128, D], F32)
                sg = [sbuf(st, "sg%d" % i, [128, 512], F32) for i in range(4)]
                xts = [sbuf(st, "xts%d" % i, [128, 256], F32) for i in range(3)]
                x1s = [sbuf(st, "x1s%d" % i, [128, 256], F32) for i in range(3)]
                pss_ = [psum(st, "pcc%d" % i, [128, 512], F32) for i in range(8)]
                K.dma(sp, g1bc[:], _dram_ap(g_scr, 0, [[0, 128], [1, D]]), reads=[r_gscr], writes=[r_g1bc])
                wno_v = w_nsa_out.rearrange("(c p) n -> p c n", p=128)
                wdo_v = w_diff_out.rearrange("(c p) n -> p c n", p=128)
                TCH = [(0, 512), (512, 512), (1024, 128)]
                unit = 0
                wi = 0
                cload = {}

                def c_load(nch_):
                    wA_, r_wA_ = wbufs[(2 * nch_) % 5]
                    wa3_ = wA_[:, 0:1024].rearrange("p (c n) -> p c n", n=128)
                    wd3_ = wA_[:, 1024:2048].rearrange("p (c n) -> p c n", n=128)
                    K.dma(pool, wa3_, wno_v[:, :, nch_ * 128:(nch_ + 1) * 128], writes=[r_wA_])
                    K.dma(pool, wd3_, wdo_v[:, :, nch_ * 128:(nch_ + 1) * 128], writes=[r_wA_])
                    wB_, r_wB_ = wbufs[(2 * nch_ + 1) % 5]
                    wm0_ = wB_[:, 0:2048].rearrange("p (c n) -> p c n", n=128)
                    wm1_ = wB_[:, 2048:4096].rearrange("p (c n) -> p c n", n=128)
                    K.dma(pool, wm0_, w_in_v[:, :, OFF_MG + nch_ * 128:OFF_MG + (nch_ + 1) * 128], writes=[r_wB_])
                    K.dma(pool, wm1_, w_in_v[:, :, OFF_MG + D + nch_ * 128:OFF_MG + D + (nch_ + 1) * 128], writes=[r_wB_])
                    cload[nch_] = (wa3_, wd3_, r_wA_, wm0_, wm1_, r_wB_)
                c_load(0)
                for nch in range(16):
                    if nch + 1 < 16:
                        c_load(nch + 1)
                    wa3, wd3, r_wA, wm0, wm1, r_wB = cload.pop(nch)
                    for (t0, n) in TCH:
                        pset = pss_[(unit % 2) * 4:(unit % 2) * 4 + 4]
                        (pn, r_pn), (pd, r_pd), (p0, r_p0), (p1, r_p1) = pset
                        s0, r_s0 = sg[(unit % 2) * 2]
                        s1, r_s1 = sg[(unit % 2) * 2 + 1]
                        unit += 1
                        for c in range(8):
                            K.op(pe, lambda: PE.matmul(pn[:, 0:n], lhsT=wa3[:, c, :], rhs=oT_nsa[:, c, t0:t0 + n], start=(c == 0),
                                                       stop=(c == 7)), reads=[r_wA, r_oTn], writes=[r_pn], inc=(c == 7))
                        for c in range(8):
                            K.op(pe, lambda: PE.matmul(pd[:, 0:n], lhsT=wd3[:, c, :], rhs=oT_diff[:, c, t0:t0 + n], start=(c == 0),
                                                       stop=(c == 7)), reads=[r_wA, r_oTd], writes=[r_pd], inc=(c == 7))
                        for kc in range(16):
                            K.op(pe, lambda: PE.matmul(p0[:, 0:n], lhsT=wm0[:, kc, :], rhs=hT[:, kc, Q0 + t0:Q0 + t0 + n],
                                                       start=(kc == 0), stop=(kc == 15)), reads=[r_wB, r_hT], writes=[r_p0],
                                 inc=(kc == 15))
                        for kc in range(16):
                            K.op(pe, lambda: PE.matmul(p1[:, 0:n], lhsT=wm1[:, kc, :], rhs=hT[:, kc, Q0 + t0:Q0 + t0 + n],
                                                       start=(kc == 0), stop=(kc == 15)), reads=[r_wB, r_hT], writes=[r_p1],
                                 inc=(kc == 15))
                        K.op(act, lambda: A.activation(out=s0[:, 0:n], in_=p0[:, 0:n], func=AF.Sigmoid), reads=[r_p0], writes=[r_s0])
                        K.op(act, lambda: A.activation(out=s1[:, 0:n], in_=p1[:, 0:n], func=AF.Sigmoid), reads=[r_p1], writes=[r_s1])
                        K.op(dve, lambda: V.tensor_tensor(out=s0[:, 0:n], in0=pn[:, 0:n], in1=s0[:, 0:n], op=ALU.mult),
                             reads=[r_pn, r_s0], writes=[r_s0])
                        K.op(dve, lambda: V.tensor_tensor(out=s1[:, 0:n], in0=pd[:, 0:n], in1=s1[:, 0:n], op=ALU.mult),
                             reads=[r_pd, r_s1], writes=[r_s1])
                        K.op(dve, lambda: V.tensor_tensor(out=mergedT[:, nch, t0:t0 + n], in0=s0[:, 0:n], in1=s1[:, 0:n], op=ALU.add),
                             reads=[r_s0, r_s1], writes=[r_mT])
                w_o_v = w_o.rearrange("(kc p) n -> p kc n", p=128)
                u2 = 0
                oload = {}

                def o_load(cc_):
                    wO_, r_wO_ = wbufs[(32 + cc_) % 5]
                    wo3_ = wO_[:].rearrange("p (c n) -> p c n", n=256)
                    K.dma(pool, wo3_, w_o_v[:, :, cc_ * 256:(cc_ + 1) * 256], writes=[r_wO_])
                    oload[cc_] = (wo3_, r_wO_)
                o_load(0)
                o_load(1)
                for cc in range(8):
                    if cc + 2 < 8:
                        o_load(cc + 2)
                    wo3, r_wO = oload.pop(cc)
                    for qi in range(NQT):
                        xt, r_xt = xts[u2 % 3]
                        x1t, r_x1t = x1s[u2 % 3]
                        pq, r_pq = pss_[u2 % 8]
                        u2 += 1
                        K.dma(act, xt[:], xctx[Q0 + qi * 128:Q0 + (qi + 1) * 128, cc * 256:(cc + 1) * 256], writes=[r_xt])
                        for nch in range(16):
                            K.op(pe, lambda: PE.matmul(pq[:, 0:256], lhsT=mergedT[:, nch, qi * 128:(qi + 1) * 128], rhs=wo3[:, nch, :],
                                                       start=(nch == 0), stop=(nch == 15)), reads=[r_mT, r_wO], writes=[r_pq],
                                 inc=(nch == 15))
                        K.op(dve, lambda: V.tensor_tensor(out=x1t[:], in0=pq[:, 0:256], in1=g1bc[:, cc * 256:(cc + 1) * 256], op=ALU.mult),
                             reads=[r_pq, r_g1bc], writes=[r_x1t])
                        K.op(dve, lambda: V.tensor_tensor(out=x1t[:], in0=x1t[:], in1=xt[:], op=ALU.add), reads=[r_x1t, r_xt],
                             writes=[r_x1t])
                        K.dma(sp, x1_scr[qi * 128:(qi + 1) * 128, cc * 256:(cc + 1) * 256], x1t[:], reads=[r_x1t], writes=[r_x1scr],
                              semres=r_x1t)
                K.barrier()
        if stage <= 4:
            K.barrier()
            return nc, dbg

        with ExitStack() as stE:
            h2T, r_h2T = sbuf(stE, "h2T", [128, 16, NQ], BF16)
            K.op(dve, lambda: V.scalar_tensor_tensor(out=gcol[:, 16:32], in0=modcol[:, 64:80], scalar=1.0, in1=gl[:, 16:32],
                                                     op0=ALU.add, op1=ALU.mult), reads=[r_modcol, r_gl], writes=[r_gcol])
            with ExitStack() as st:
                norm_to_featmajor(st, lambda i: x1_scr[i * 128:(i + 1) * 128, :], NQT, h2T, r_h2T, 16, 48, 0, "n2",
                                  src_reads=[r_x1scr])
                K.barrier()
            with ExitStack() as st:
                acc, r_acc = sbuf(st, "acc", [128, 8, D], F32)
                g2bc, r_g2bc = sbuf(st, "g2bc", [128, D], F32)
                cvt, r_cvt = sbuf(st, "cvt", [128, 88, 4], F32)
                gT, r_gT = sbuf(st, "gT", [128, 11, 1024], BF16)
                wbufs = [sbuf(st, "we%d" % i, [128, 4096], BF16) for i in range(3)]
                uu = [sbuf(st, "uu%d" % i, [128, 1026], F32) for i in range(2)]
                cc_ = [sbuf(st, "cc%d" % i, [128, 1024], F32) for i in range(2)]
                sa, r_sa = sbuf(st, "sa", [128, 1024], F32)
                tmpd = [sbuf(st, "tmpd%d" % i, [128, 256], F32) for i in range(2)]
                pss_ = [psum(st, "pee%d" % i, [128, 512], F32) for i in range(8)]
                racc = [K.res("acc%d" % i) for i in range(8)]
                for i in range(8):
                    K.dma(sp, acc[:, i, :], x1_scr[128 + i * 128:128 + (i + 1) * 128, :], reads=[r_x1scr], writes=[racc[i]])
                K.dma(sp, g2bc[:], _dram_ap(g_scr, D, [[0, 128], [1, D]]), reads=[r_gscr], writes=[r_g2bc])
                K.dma(sp, cvt[:], conv_l, writes=[r_cvt])
                w_up_v = w_up.rearrange("(kc p) n -> p kc n", p=128)
                w_down_v = w_down.rearrange("(fc p) n -> p fc n", p=128)
                UCH = [(126, 512, 0), (638, 512, 512), (1150, 2, 1024)]
                pi = [0]
                estages = []
                for fp_ in range(4):
                    for fl_ in range(11):
                        estages.append(("up", fp_, fl_))
                    for cx_ in range(8):
                        estages.append(("down", fp_, cx_))
                eload = {}

                def e_load(si):
                    kind, fp_, x_ = estages[si]
                    wt_, r_wt_ = wbufs[si % 3]
                    if kind == "up":
                        fc_ = fp_ * 11 + x_
                        v0 = wt_[:, 0:2048].rearrange("p (c n) -> p c n", n=128)
                        v1 = wt_[:, 2048:4096].rearrange("p (c n) -> p c n", n=128)
                        K.dma(pool, v0, w_up_v[:, :, fc_ * 128:(fc_ + 1) * 128], writes=[r_wt_])
                        K.dma(pool, v1, w_up_v[:, :, DFF + fc_ * 128:DFF + (fc_ + 1) * 128], writes=[r_wt_])
                        eload[si] = ([v0, v1], r_wt_)
                    else:
                        v0 = wt_[:, 0:11 * 256].rearrange("p (c n) -> p c n", n=256)
                        K.dma(pool, v0, w_down_v[:, fp_ * 11:(fp_ + 1) * 11, x_ * 256:(x_ + 1) * 256], writes=[r_wt_])
                        eload[si] = (v0, r_wt_)

                def e_up(fp, fl, wu3, r_wU):
                    fc = fp * 11 + fl
                    for part in range(2):
                        u, r_u = uu[part]
                        cv_, r_cv = cc_[part]
                        ch = fc + 44 * part
                        for (ti, n, dc) in UCH:
                            pu, r_pu = pss_[pi[0] % 8]; pi[0] += 1
                            for kc in range(16):
                                K.op(pe, lambda: PE.matmul(pu[:, 0:n], lhsT=wu3[part][:, kc, :], rhs=h2T[:, kc, ti:ti + n],
                                                           start=(kc == 0), stop=(kc == 15)), reads=[r_wU, r_h2T], writes=[r_pu],
                                     inc=(kc == 15))
                            K.op(act, lambda: A.copy(out=u[:, dc:dc + n], in_=pu[:, 0:n]), reads=[r_pu], writes=[r_u])
                        K.op(dve, lambda: V.tensor_scalar(out=u[:, 0:2], in0=u[:, 0:2], scalar1=cm[:, 1:2], scalar2=None,
                                                          op0=ALU.mult), reads=[r_u, r_cm], writes=[r_u])
                        K.op(dve, lambda: V.tensor_scalar(out=cv_[:], in0=u[:, 2:1026], scalar1=cvt[:, ch, 2:3],
                                                          scalar2=cvt[:, ch, 3:4], op0=ALU.mult, op1=ALU.add),
                             reads=[r_u, r_cvt], writes=[r_cv])
                        K.op(dve, lambda: V.scalar_tensor_tensor(out=cv_[:], in0=u[:, 1:1025], scalar=cvt[:, ch, 1:2], in1=cv_[:],
                                                                 op0=ALU.mult, op1=ALU.add), reads=[r_u, r_cvt, r_cv],
                             writes=[r_cv])
                        K.op(dve, lambda: V.scalar_tensor_tensor(out=cv_[:], in0=u[:, 0:1024], scalar=cvt[:, ch, 0:1], in1=cv_[:],
                                                                 op0=ALU.mult, op1=ALU.add), reads=[r_u, r_cvt, r_cv],
                             writes=[r_cv])
                    K.op(act, lambda: A.activation(out=sa[:], in_=cc_[0][0][:], func=AF.Silu), reads=[cc_[0][1]], writes=[r_sa])
                    K.op(pool, lambda: G.tensor_tensor(out=gT[:, fl, :], in0=sa[:], in1=cc_[1][0][:], op=ALU.mult),
                         reads=[r_sa, cc_[1][1]], writes=[r_gT])

                def e_down(fp, cc, wd3, r_wD):
                    for i in range(8):
                        pq, r_pq = pss_[pi[0] % 8]; pi[0] += 1
                        td, r_td = tmpd[pi[0] % 2]
                        for fl in range(11):
                            K.op(pe, lambda: PE.matmul(pq[:, 0:256], lhsT=gT[:, fl, i * 128:(i + 1) * 128], rhs=wd3[:, fl, :],
                                                       start=(fl == 0), stop=(fl == 10)), reads=[r_gT, r_wD], writes=[r_pq],
                                 inc=(fl == 10))
                        K.op(dve, lambda: V.tensor_tensor(out=td[:], in0=pq[:, 0:256], in1=g2bc[:, cc * 256:(cc + 1) * 256],
                                                          op=ALU.mult), reads=[r_pq, r_g2bc], writes=[r_td])
                        K.op(dve, lambda: V.tensor_tensor(out=acc[:, i, cc * 256:(cc + 1) * 256],
                                                          in0=acc[:, i, cc * 256:(cc + 1) * 256], in1=td[:], op=ALU.add),
                             reads=[r_td, racc[i]], writes=[racc[i]])

                e_load(0)
                e_load(1)
                for si in range(len(estages)):
                    if si + 2 < len(estages):
                        e_load(si + 2)
                    kind, fp, x_ = estages[si]
                    wv_, r_wv_ = eload.pop(si)
                    if kind == "up":
                        e_up(fp, x_, wv_, r_wv_)
                    else:
                        e_down(fp, x_, wv_, r_wv_)
                for i in range(8):
                    K.dma(sp, out[i * 128:(i + 1) * 128, :], acc[:, i, :], reads=[racc[i]], writes=[r_out], semres=racc[i])
                K.barrier()
        print('ninst', K.ninst, 'nwaits', K.nwaits, 'nsem', K.nsem)
    return nc, dbg


def _t5_bucket_np(n):
    n = np.maximum(np.asarray(n, np.int64), 0)
    nf = np.maximum(n, 16).astype(np.float32)
    large = 16 + (np.log(nf / np.float32(16)) / np.float32(math.log(128 / 16)) * np.float32(16)).astype(np.int32)
    large = np.minimum(large, 31)
    return np.where(n < 16, n, large)


def _consts():
    d = np.arange(UL) - 2048
    oh = np.zeros((33, UL), np.float32)
    b = _t5_bucket_np(d)
    for i in range(UL):
        if d[i] < 0:
            oh[32, i] = 1.0
        else:
            oh[b[i], i] = 1.0
    e_mat = np.zeros((32, T), np.float32)
    for s in range(32):
        e_mat[s, s * 64:(s + 1) * 64] = 1.0
    starts = np.arange(127) * 16
    sel_start = np.arange(32) * 64
    ovl = np.clip(np.minimum(starts[:, None] + 32, sel_start[None, :] + 64) - np.maximum(starts[:, None], sel_start[None, :]),
                  0, None).astype(np.float32) / 16.0
    return oh, e_mat, ovl


def _core_masks(half):
    cm = np.zeros((128, 8), np.float32)
    cm[:, 0] = 0.0 if half == 1 else NEGV
    cm[:, 1] = 1.0 if half == 1 else 0.0
    if half == 0:
        cm[:64, 2] = NEGV
    shift = 0 if half == 1 else 1024
    t_loc = Q0 + np.arange(NQ)
    t_real = t_loc - shift
    cur = np.floor_divide(t_real, 64)
    j_real = np.arange(32)[None, :] - shift // 64
    valid = (j_real >= 0) & (j_real <= cur[:, None])
    forced = valid & ((j_real == 0) | (j_real == cur[:, None]) | (j_real == cur[:, None] - 1))
    mul = (valid & ~forced).astype(np.float32)
    add = np.where(forced, 1e4, np.where(valid, 0.0, -1e4)).astype(np.float32)
    sm = np.concatenate([mul, add], axis=1).reshape(NQT, 128, 64).transpose(1, 0, 2)
    return cm, np.ascontiguousarray(sm)


def _col_layout(v):
    return np.ascontiguousarray(np.asarray(v, np.float32).reshape(16, 128).T)


def make_in_maps(inp):
    x = np.asarray(inp["x"], np.float32)
    oh, e_mat, ovl = _consts()
    f = lambda k: np.ascontiguousarray(np.asarray(inp[k], np.float32)[0])
    qkg = np.zeros((128, 8), np.float32)
    qkg[:, 0] = f("nsa_q_gain")
    qkg[:, 1:4] = f("nsa_k_gain").T
    qkg[:, 4] = f("diff_q_gain")
    qkg[:, 5] = f("diff_k_gain")
    lam_qk = np.concatenate([f("diff_lambda_q").reshape(-1), f("diff_lambda_k").reshape(-1)])[None, :]
    cw = f("ffn_conv_w")
    cb = f("ffn_conv_b")
    conv = np.concatenate([cw, cb[None, :]], axis=0)
    conv_l = np.ascontiguousarray(conv.reshape(4, 88, 128).transpose(2, 1, 0))
    shared = {
        "w_ada": f("w_ada"), "b_ada": np.asarray(inp["b_ada"], np.float32).reshape(1, -1),
        "g1_l": _col_layout(f("norm1_gain")), "g2_l": _col_layout(f("norm2_gain")),
        "w_in": f("w_in"), "qkg": qkg, "cmp_pe": np.ascontiguousarray(f("cmp_pe").transpose(0, 2, 1)), "cmp_w1": f("cmp_w1"), "cmp_w2": f("cmp_w2"),
        "lam_qk": np.ascontiguousarray(lam_qk), "subln": f("diff_subln_gain")[None, :],
        "w_nsa_out": f("w_nsa_out"), "w_diff_out": f("w_diff_out"), "w_o": f("w_o"), "w_up": f("w_ffn_up"),
        "conv_l": conv_l, "w_down": f("w_ffn_down"), "rel_bias": np.asarray(inp["rel_bias"], np.float32),
        "oh_tab": oh, "e_mat": e_mat, "ovl": ovl,
    }
    maps = []
    for core in range(8):
        b, half = core // 2, core % 2
        if half == 1:
            xc = np.ascontiguousarray(x[b])
        else:
            xc = np.concatenate([np.zeros((1024, D), np.float32), x[b, :1024]], axis=0)
        cm, sm = _core_masks(half)
        m = dict(shared)
        m.update({"xctx": xc, "c_l": _col_layout(np.asarray(inp["c"], np.float32)[b]), "cmasks": cm, "selmask": sm})
        maps.append(m)
    return maps


_PROG = {}


def kernel(**inputs):
    if "p" not in _PROG:
        _PROG["p"] = build_program()[0]
    nc = _PROG["p"]
    maps = make_in_maps(inputs)
    res = run_bass_kernel_spmd(nc, maps, core_ids=list(range(8)))
    outp = np.zeros((4, T, D), np.float32)
    for core in range(8):
        b, half = core // 2, core % 2
        outp[b, half * 1024:(half + 1) * 1024] = res.results[core]["out"]
    return outp
```

```python
import os
import math
import numpy as np
from contextlib import ExitStack
import concourse.bass as bass
import concourse.mybir as mybir
from concourse.bass_utils import run_bass_kernel_spmd

F32 = mybir.dt.float32
BF16 = mybir.dt.bfloat16
ALU = mybir.AluOpType
AF = mybir.ActivationFunctionType
AX = mybir.AxisListType

D = 2048
T = 2048
DFF = 5632
NQT = 9
QT0 = 7
NQ = NQT * 128
Q0 = QT0 * 128
OFF_NSA_KV = 1024
OFF_NSA_G = 1024 + 1536
OFF_DQ = OFF_NSA_G + 24
OFF_DK = OFF_DQ + 1024
OFF_DV = OFF_DK + 1024
OFF_MG = OFF_DV + 1024
IN_COLS = OFF_MG + 4096
NEGV = -30000.0
EPS = 1e-6
UL = 4096


class Res:
    __slots__ = ("name", "w", "r", "dsem", "dcount")

    def __init__(self, name):
        self.name = name
        self.w = None
        self.r = {}
        self.dsem = None
        self.dcount = 0


class Eng:
    def __init__(self, name, e, sem):
        self.name = name
        self.e = e
        self.sem = sem
        self.count = 0
        self.seen = {}


class Kern:
    def __init__(self, nc, stack):
        self.nc = nc
        self.stack = stack
        self.nsem = 0
        self.pe = Eng("pe", nc.tensor, self.newsem("s_pe"))
        self.act = Eng("act", nc.scalar, self.newsem("s_act"))
        self.dve = Eng("dve", nc.vector, self.newsem("s_dve"))
        self.pool = Eng("pool", nc.gpsimd, self.newsem("s_pool"))
        self.sp = Eng("sp", nc.sync, self.newsem("s_sp"))
        self.engines = [self.pe, self.act, self.dve, self.pool, self.sp]
        self.dres = []
        self.nwaits = 0
        self.ninst = 0
        self.freesems = []

    def newsem(self, name):
        self.nsem += 1
        return self.stack.enter_context(self.nc.semaphore(name))

    def res(self, name):
        return Res(name)

    def _wait(self, E, tok):
        sem, val = tok
        key = id(sem)
        if E.seen.get(key, 0) < val:
            E.e.wait_ge(sem, val)
            E.seen[key] = val
            self.nwaits += 1

    def _needs(self, E, reads, writes, skip_sem=None):
        for r in reads:
            if r.w is not None and r.w[0] is not skip_sem:
                self._wait(E, r.w)
        for w in writes:
            if w.w is not None and w.w[0] is not skip_sem:
                self._wait(E, w.w)
            for sem, val in list(w.r.values()):
                if sem is not skip_sem:
                    self._wait(E, (sem, val))

    def _mark(self, tok, reads, writes):
        sem, val = tok
        for r in reads:
            old = r.r.get(id(sem))
            if old is None or old[1] < val:
                r.r[id(sem)] = (sem, val)
        for w in writes:
            w.w = tok
            w.r = {}

    def op(self, E, fn, reads=(), writes=(), inc=True):
        skip = E.sem if E is self.pe else None
        self._needs(E, reads, writes, skip_sem=skip)
        inst = fn()
        self.ninst += 1
        if inc:
            E.count += 1
            inst.then_inc(E.sem, 1)
            tok = (E.sem, E.count)
        else:
            tok = (E.sem, E.count + 1)
        self._mark(tok, reads, writes)
        return inst

    def dma(self, E, out, in_, reads=(), writes=(), semres=None, **kw):
        if semres is None:
            semres = writes[0] if writes else reads[0]
        if semres.dsem is None:
            semres.dsem = self.newsem("d_" + semres.name)
            self.dres.append(semres)
        self._needs(E, reads, writes, skip_sem=semres.dsem)
        inst = E.e.dma_start(out=out, in_=in_, **kw)
        self.ninst += 1
        semres.dcount += 16
        inst.then_inc(semres.dsem, 16)
        tok = (semres.dsem, semres.dcount)
        self._mark(tok, reads, writes)
        return inst

    def barrier(self):
        toks = []
        for E in self.engines:
            if E.count > 0:
                toks.append((E.sem, E.count))
        for r in self.dres:
            if r.dcount > 0:
                toks.append((r.dsem, r.dcount))
        for E in self.engines:
            for t in toks:
                if t[0] is E.sem and E is self.pe:
                    continue
                self._wait(E, t)


def _dram_ap(t, offset, ap):
    return bass.AP(tensor=t.tensor, offset=offset, ap=ap)


def build_program(stage=99, debug=False):
    nc = bass.Bass("TRN2", target_bir_lowering=False)
    dt_in = lambda name, shape: nc.dram_tensor(name, list(shape), F32, kind="ExternalInput").ap()
    xctx = dt_in("xctx", [T, D])
    c_l = dt_in("c_l", [128, 16])
    w_ada = dt_in("w_ada", [D, 6 * D])
    b_ada = dt_in("b_ada", [1, 6 * D])
    g1_l = dt_in("g1_l", [128, 16])
    g2_l = dt_in("g2_l", [128, 16])
    w_in = dt_in("w_in", [D, IN_COLS])
    qkg = dt_in("qkg", [128, 8])
    cmp_pe = dt_in("cmp_pe", [2, 128, 32])
    cmp_w1 = dt_in("cmp_w1", [2, 4096, 128])
    cmp_w2 = dt_in("cmp_w2", [2, 128, 128])
    lam_qk = dt_in("lam_qk", [1, 512])
    subln = dt_in("subln", [1, 256])
    w_nsa_out = dt_in("w_nsa_out", [1024, D])
    w_diff_out = dt_in("w_diff_out", [1024, D])
    w_o = dt_in("w_o", [D, D])
    w_up = dt_in("w_up", [D, 2 * DFF])
    conv_l = dt_in("conv_l", [128, 88, 4])
    w_down = dt_in("w_down", [DFF, D])
    rel_bias = dt_in("rel_bias", [32, 12])
    oh_tab = dt_in("oh_tab", [33, UL])
    cmasks = dt_in("cmasks", [128, 8])
    selmask = dt_in("selmask", [128, NQT, 64])
    e_mat = dt_in("e_mat", [32, T])
    ovl = dt_in("ovl", [127, 32])
    out = nc.dram_tensor("out", [1024, D], F32, kind="ExternalOutput").ap()
    dbg = {}

    def dbg_out(name, shape):
        dbg[name] = nc.dram_tensor("dbg_" + name, list(shape), F32, kind="ExternalOutput").ap()
        return dbg[name]

    x1_scr = nc.dram_tensor("x1_scr", [NQ, D], F32, kind="Internal").ap()
    u_scr = nc.dram_tensor("u_scr", [12, UL], BF16, kind="Internal").ap()
    g_scr = nc.dram_tensor("g_scr", [2, D], F32, kind="Internal").ap()

    with ExitStack() as top:
        K = Kern(nc, top)
        pe, act, dve, pool, sp = K.pe, K.act, K.dve, K.pool, K.sp
        V = nc.vector
        A = nc.scalar
        G = nc.gpsimd
        PE = nc.tensor

        def sbuf(st, name, shape, dt):
            t = st.enter_context(nc.sbuf_tensor(name, list(shape), dt))
            return t, K.res(name)

        def psum(st, name, shape, dt):
            t = st.enter_context(nc.psum_tensor(name, list(shape), dt))
            return t, K.res(name)

        r_out = K.res("out")
        r_x1scr = K.res("x1scr")
        r_uscr = K.res("uscr")

        ident_f, r_identf = sbuf(top, "ident_f", [128, 128], F32)
        ident, r_ident = sbuf(top, "ident", [128, 128], BF16)
        jmat, r_jmat = sbuf(top, "jmat", [128, 128], BF16)
        j127, r_j127 = sbuf(top, "j127", [128, 128], BF16)
        ones_bf, r_onesbf = sbuf(top, "ones_bf", [128, 128], BF16)
        ones_f, r_onesf = sbuf(top, "ones_f", [128, 128], F32)
        modcol, r_modcol = sbuf(top, "modcol", [128, 96], F32)
        gcol, r_gcol = sbuf(top, "gcol", [128, 32], F32)
        r_gscr = K.res("gscr")
        cm, r_cm = sbuf(top, "cm", [128, 8], F32)
        qkg_t, r_qkg = sbuf(top, "qkg_t", [128, 8], F32)
        tmpf, r_tmpf = sbuf(top, "tmpf", [128, 128], F32)
        scb, r_scb = sbuf(top, "scb", [128, 16], BF16)
        nlam, r_nlam = sbuf(top, "nlam", [128, 1], F32)
        sublbc, r_sublbc = sbuf(top, "sublbc", [128, 256], F32)
        gl, r_gl = sbuf(top, "gl", [128, 32], F32)

        K.dma(sp, cm[:], cmasks, writes=[r_cm])
        K.dma(sp, qkg_t[:], qkg, writes=[r_qkg])
        K.op(pool, lambda: G.memset(ident_f[:], 0.0), writes=[r_identf])
        K.op(pool, lambda: G.affine_select(out=ident_f[:], in_=ident_f[:], pattern=[[-1, 128]], compare_op=ALU.not_equal,
                                           fill=1.0, base=0, channel_multiplier=1), reads=[r_identf], writes=[r_identf])
        K.op(dve, lambda: V.tensor_copy(out=ident[:], in_=ident_f[:]), reads=[r_identf], writes=[r_ident])
        K.op(pool, lambda: G.memset(tmpf[:], 0.0), writes=[r_tmpf])
        K.op(pool, lambda: G.affine_select(out=tmpf[:], in_=tmpf[:], pattern=[[1, 128]], compare_op=ALU.not_equal,
                                           fill=1.0, base=-127, channel_multiplier=1), reads=[r_tmpf], writes=[r_tmpf])
        K.op(dve, lambda: V.tensor_copy(out=jmat[:], in_=tmpf[:]), reads=[r_tmpf], writes=[r_jmat])
        K.op(pool, lambda: G.memset(tmpf[:], 0.0), reads=[r_tmpf], writes=[r_tmpf])
        K.op(pool, lambda: G.affine_select(out=tmpf[:], in_=tmpf[:], pattern=[[1, 128]], compare_op=ALU.not_equal,
                                           fill=1.0, base=-126, channel_multiplier=1), reads=[r_tmpf], writes=[r_tmpf])
        K.op(dve, lambda: V.tensor_copy(out=j127[:], in_=tmpf[:]), reads=[r_tmpf], writes=[r_j127])
        K.op(dve, lambda: V.memset(ones_bf[:], 1.0), writes=[r_onesbf])
        K.op(dve, lambda: V.memset(ones_f[:], 1.0), writes=[r_onesf])
        sc_ = 128.0 ** -0.5
        K.op(dve, lambda: V.tensor_scalar(out=qkg_t[:, 0:1], in0=qkg_t[:, 0:1], scalar1=sc_, scalar2=None, op0=ALU.mult),
             reads=[r_qkg], writes=[r_qkg])
        K.op(dve, lambda: V.tensor_scalar(out=qkg_t[:, 4:5], in0=qkg_t[:, 4:5], scalar1=sc_, scalar2=None, op0=ALU.mult),
             reads=[r_qkg], writes=[r_qkg])

        with ExitStack() as st:
            cl, r_cl = sbuf(st, "cl", [128, 16], F32)
            wb = [sbuf(st, "wada%d" % i, [128, 16, 512], BF16) for i in range(2)]
            brow, r_brow = sbuf(st, "brow", [1, 512], F32)
            mrow = [sbuf(st, "mrow%d" % i, [1, 512], F32) for i in range(2)]
            pm = [psum(st, "pm%d" % i, [1, 512], F32) for i in range(2)]
            pc = [psum(st, "pc%d" % i, [128, 4], F32) for i in range(2)]
            K.dma(sp, cl[:], c_l, writes=[r_cl])
            K.dma(sp, gl[:, 0:16], g1_l, writes=[r_gl])
            K.dma(sp, gl[:, 16:32], g2_l, writes=[r_gl])
            K.op(act, lambda: A.activation(out=scb[:], in_=cl[:], func=AF.Silu), reads=[r_cl], writes=[r_scb])
            pu0 = [psum(st, "pu0_%d" % i, [128, 512], F32) for i in range(2)]
            pu0c = [0]

            def pu0_next():
                t_ = pu0[pu0c[0] % 2]
                pu0c[0] += 1
                return t_
            su = st
            rba, r_rba = sbuf(su, "rba", [33, 12], F32)
            b31, r_b31 = sbuf(su, "b31", [33, 12], F32)
            oht, r_oht = sbuf(su, "oht", [33, UL], F32)
            ub, r_ub = sbuf(su, "ub", [12, UL], BF16)
            K.op(dve, lambda: V.memset(rba[:], NEGV), writes=[r_rba])
            K.dma(sp, rba[0:32, :], rel_bias, writes=[r_rba])
            K.dma(sp, b31[0:32, :], _dram_ap(rel_bias, 31 * 12, [[0, 32], [1, 12]]), writes=[r_b31])
            K.dma(sp, oht[:], oh_tab, writes=[r_oht])
            K.op(dve, lambda: V.tensor_tensor(out=rba[0:32, :], in0=rba[0:32, :], in1=b31[0:32, :], op=ALU.subtract),
                 reads=[r_rba, r_b31], writes=[r_rba])
            for ch in range(UL // 512):
                pt_, r_pt_ = pu0_next()
                K.op(pe, lambda: PE.matmul(pt_[0:12, :], lhsT=rba[:, :], rhs=oht[:, ch * 512:(ch + 1) * 512], start=True,
                                           stop=True), reads=[r_rba, r_oht], writes=[r_pt_])
                K.op(dve, lambda: V.tensor_copy(out=ub[:, ch * 512:(ch + 1) * 512], in_=pt_[0:12, :]), reads=[r_pt_],
                     writes=[r_ub])
            K.dma(sp, u_scr, ub[:], reads=[r_ub], writes=[r_uscr], semres=r_ub)

            lamt, r_lamt = sbuf(st, "lamt", [1, 512], F32)
            lamw, r_lamw = sbuf(st, "lamw", [1, 8], F32)
            subl, r_subl = sbuf(st, "subl", [1, 256], F32)
            K.dma(sp, lamt[:], lam_qk, writes=[r_lamt])
            K.dma(sp, subl[:], subln, writes=[r_subl])
            K.op(dve, lambda: V.tensor_tensor(out=lamt[:, 0:256], in0=lamt[:, 0:256], in1=lamt[:, 256:512], op=ALU.mult),
                 reads=[r_lamt], writes=[r_lamt])
            K.op(dve, lambda: V.tensor_reduce(out=lamw[:, 0:2], in_=lamt[:, 0:256].rearrange("p (a b) -> p a b", b=128),
                                              axis=AX.X, op=ALU.add), reads=[r_lamt], writes=[r_lamw])
            K.op(act, lambda: A.activation(out=lamw[:, 2:4], in_=lamw[:, 0:2], func=AF.Exp), reads=[r_lamw], writes=[r_lamw])
            K.op(dve, lambda: V.tensor_tensor(out=lamw[:, 4:5], in0=lamw[:, 3:4], in1=lamw[:, 2:3], op=ALU.subtract),
                 reads=[r_lamw], writes=[r_lamw])
            K.op(dve, lambda: V.tensor_scalar(out=lamw[:, 4:5], in0=lamw[:, 4:5], scalar1=-0.2, scalar2=None, op0=ALU.add),
                 reads=[r_lamw], writes=[r_lamw])
            pt_, r_pt_ = pu0_next()
            K.op(pe, lambda: PE.matmul(pt_[:, 0:1], lhsT=ones_f[0:1, :], rhs=lamw[0:1, 4:5], start=True, stop=True),
                 reads=[r_onesf, r_lamw], writes=[r_pt_])
            K.op(dve, lambda: V.tensor_copy(out=nlam[:], in_=pt_[:, 0:1]), reads=[r_pt_], writes=[r_nlam])
            pt_, r_pt_ = pu0_next()
            K.op(pe, lambda: PE.matmul(pt_[:, 0:256], lhsT=ones_f[0:1, :], rhs=subl[0:1, :], start=True, stop=True),
                 reads=[r_onesf, r_subl], writes=[r_pt_])
            K.op(dve, lambda: V.tensor_scalar(out=sublbc[:], in0=pt_[:, 0:256], scalar1=0.8, scalar2=None, op0=ALU.mult),
                 reads=[r_pt_], writes=[r_sublbc])

            w_ada_v = w_ada.rearrange("(kc p) n -> p kc n", p=128)
            for ch in range(8):
                wt, r_wt = wb[ch % 2]
                K.dma(pool, wt[:], w_ada_v[:, :, ch * 512:(ch + 1) * 512], writes=[r_wt])
                K.dma(sp, brow[:], b_ada[:, ch * 512:(ch + 1) * 512], writes=[r_brow])
                pmt, r_pm = pm[ch % 2]
                for kc in range(16):
                    K.op(pe, lambda kc=kc: PE.matmul(pmt[:], lhsT=scb[:, kc:kc + 1], rhs=wt[:, kc, :], start=(kc == 0),
                                                     stop=(kc == 15)), reads=[r_scb, r_wt], writes=[r_pm], inc=(kc == 15))
                mr, r_mr = mrow[ch % 2]
                K.op(dve, lambda: V.tensor_tensor(out=mr[:], in0=pmt[:], in1=brow[:], op=ALU.add),
                     reads=[r_pm, r_brow], writes=[r_mr])
                pct, r_pc = pc[ch % 2]
                for j in range(4):
                    K.op(pe, lambda j=j: PE.matmul(pct[:, j:j + 1], lhsT=mr[0:1, j * 128:(j + 1) * 128], rhs=ones_f[0:1, 0:1],
                                                   start=True, stop=True), reads=[r_mr, r_onesf], writes=[r_pc], inc=(j == 3))
                K.op(dve, lambda: V.tensor_copy(out=modcol[:, ch * 4:(ch + 1) * 4], in_=pct[:]), reads=[r_pc], writes=[r_modcol])
            K.op(dve, lambda: V.scalar_tensor_tensor(out=gcol[:, 0:16], in0=modcol[:, 16:32], scalar=1.0, in1=gl[:, 0:16],
                                                     op0=ALU.add, op1=ALU.mult), reads=[r_modcol, r_gl], writes=[r_gcol])
            K.barrier()
        if debug:
            d = dbg_out("modcol", [128, 96])
            K.dma(sp, d, modcol[:], reads=[r_modcol], writes=[r_out], semres=r_modcol)

        def norm_to_featmajor(st, src_ap_fn, ntiles, dstT, r_dstT, gc0, shc0, tok0, tag, src_reads=()):
            xs = [sbuf(st, "%s_x%d" % (tag, i), [128, D], F32) for i in range(8)]
            xn = [sbuf(st, "%s_xn%d" % (tag, i), [128, D], BF16) for i in range(8)]
            junk, r_junk = sbuf(st, tag + "_junk", [128, D], BF16)
            sss = [sbuf(st, "%s_ss%d" % (tag, i), [128, 8], F32) for i in range(2)]
            pt = [psum(st, "%s_pt%d" % (tag, i), [128, 512], BF16) for i in range(4)]
            ngroups = (ntiles + 3) // 4
            ev = [0]

            def load_a(g):
                nt = min(4, ntiles - g * 4)
                for i in range(nt):
                    xt, r_xt = xs[(g % 2) * 4 + i]
                    K.dma(sp, xt[:], src_ap_fn(g * 4 + i), reads=list(src_reads), writes=[r_xt])

            def stage_a(g):
                nt = min(4, ntiles - g * 4)
                ss, r_ss = sss[g % 2]
                for i in range(nt):
                    xt, r_xt = xs[(g % 2) * 4 + i]
                    K.op(act, lambda: A.activation(out=junk[:], in_=xt[:], func=AF.Square, accum_out=ss[:, i:i + 1]),
                         reads=[r_xt], writes=[r_junk, r_ss])
                K.op(act, lambda: A.activation(out=ss[:, 4:4 + nt], in_=ss[:, 0:nt], func=AF.Sqrt, scale=1.0 / D, bias=EPS),
                     reads=[r_ss], writes=[r_ss])
                K.op(dve, lambda: V.reciprocal(out=ss[:, 4:4 + nt], in_=ss[:, 4:4 + nt]), reads=[r_ss], writes=[r_ss])
                for i in range(nt):
                    xt, r_xt = xs[(g % 2) * 4 + i]
                    xnt, r_xnt = xn[(g % 2) * 4 + i]
                    K.op(dve, lambda: V.tensor_scalar(out=xnt[:], in0=xt[:], scalar1=ss[:, 4 + i:5 + i], scalar2=None,
                                                      op0=ALU.mult), reads=[r_xt, r_ss], writes=[r_xnt])

            def stage_b(g):
                nt = min(4, ntiles - g * 4)
                for kc in range(16):
                    ptt, r_pt = pt[kc % 4]
                    for i in range(nt):
                        xnt, r_xnt = xn[(g % 2) * 4 + i]
                        K.op(pe, lambda: PE.transpose(out=ptt[:, i * 128:(i + 1) * 128], in_=xnt[:, kc * 128:(kc + 1) * 128],
                                                      identity=ident[:]), reads=[r_xnt, r_ident], writes=[r_pt],
                             inc=(i == nt - 1))
                    t0 = tok0 + g * 512
                    K.op(dve, lambda: V.tensor_scalar(out=dstT[:, kc, t0:t0 + nt * 128], in0=ptt[:, 0:nt * 128],
                                                      scalar1=gcol[:, gc0 + kc:gc0 + kc + 1],
                                                      scalar2=modcol[:, shc0 + kc:shc0 + kc + 1], op0=ALU.mult, op1=ALU.add),
                         reads=[r_pt, r_gcol, r_modcol], writes=[r_dstT])
                    ev[0] += 1

            load_a(0)
            if ngroups > 1:
                load_a(1)
            stage_a(0)
            for g in range(ngroups):
                stage_b(g)
                if g + 1 < ngroups:
                    stage_a(g + 1)
                if g + 2 < ngroups:
                    load_a(g + 2)

        with ExitStack() as stB:
            hT, r_hT = sbuf(stB, "hT", [128, 16, T], BF16)
            with ExitStack() as st:
                norm_to_featmajor(st, lambda i: xctx[i * 128:(i + 1) * 128, :], 16, hT, r_hT, 0, 0, 0, "n1")
                K.barrier()
            if debug and stage <= 1:
                d = dbg_out("hT", [128, 16 * T])
                hf, r_hf = sbuf(stB, "hf", [128, T], F32)
                for kc in range(16):
                    K.op(dve, lambda: V.tensor_copy(out=hf[:], in_=hT[:, kc, :]), reads=[r_hT], writes=[r_hf])
                    K.dma(sp, d[:, kc * T:(kc + 1) * T], hf[:], reads=[r_hf], writes=[r_out], semres=r_hf)
            if stage <= 1:
                K.barrier()
                return nc, dbg

            oT_nsa, r_oTn = sbuf(stB, "oT_nsa", [128, 8, NQ], BF16)
            oT_diff, r_oTd = sbuf(stB, "oT_diff", [128, 8, NQ], BF16)
            w_in_v = w_in.rearrange("(kc p) n -> p kc n", p=128)

            with ExitStack() as st:
                wbufs = [sbuf(st, "wb%d" % i, [128, 4096], BF16) for i in range(3)]
                wctr = [0]

                def wnext():
                    t = wbufs[wctr[0] % 3]
                    wctr[0] += 1
                    return t

                def wload_in(c0, ncols, dst3, r_w):
                    K.dma(pool, dst3, w_in_v[:, :, c0:c0 + ncols], writes=[r_w])

                def _kvcol(g_, br, kv):
                    return OFF_NSA_KV + ((br * 2 + kv) * 2 + g_) * 128
                wlist = []
                if stage >= 2:
                    for h_ in range(4):
                        wlist += [(OFF_DQ + h_ * 256, 256), (OFF_DK + h_ * 256, 256), (OFF_DV + h_ * 256, 256)]
                if stage >= 3:
                    wlist.append((OFF_NSA_G, 24))
                    for g_ in range(2):
                        wlist += [(_kvcol(g_, 0, 0), 128), (_kvcol(g_, 0, 1), 128)]
                        for br_ in (1, 2):
                            wlist += [(_kvcol(g_, br_, 0), 128), (_kvcol(g_, br_, 1), 128)]
                        wlist += [((4 * g_) * 128, 256), ((4 * g_ + 2) * 128, 256)]
                wstate = {"emitted": 0, "cur": 0, "released": -1}
                wloaded = {}

                def _ws_emit():
                    while wstate["emitted"] < len(wlist) and wstate["emitted"] - 3 <= wstate["released"]:
                        j = wstate["emitted"]
                        c0_, nc_ = wlist[j]
                        t_, r_ = wbufs[j % 3]
                        v3 = t_[:, 0:16 * nc_].rearrange("p (k c) -> p k c", c=nc_)
                        wload_in(c0_, nc_, v3, r_)
                        wloaded[j] = (v3, r_)
                        wstate["emitted"] += 1

                def ws_next(c0_expect):
                    i = wstate["cur"]
                    _ws_emit()
                    assert i in wloaded, (i, wstate)
                    assert wlist[i][0] == c0_expect, (i, wlist[i], c0_expect)
                    wstate["cur"] += 1
                    return wloaded.pop(i)

                def ws_release():
                    wstate["released"] = wstate["cur"] - 1
                    _ws_emit()

                sqb = [sbuf(st, "sqb%d" % i, [128, 512], BF16) for i in range(2)]
                rsb = [sbuf(st, "rsb%d" % i, [128, 512], F32) for i in range(2)]
                nctr = [0]
                ps_a = [psum(st, "ps_a%d" % i, [128, 512], F32) for i in range(2)]
                ps_b = [psum(st, "ps_b%d" % i, [128, 512], F32) for i in range(5)]
                ps_t, r_ps_t = psum(st, "ps_t", [128, 1024], BF16)
                pctr = [0]

                ps_rot = list(ps_a) + [ps_b[2], ps_b[3]]

                def ps_next():
                    t = ps_rot[pctr[0] % len(ps_rot)]
                    pctr[0] += 1
                    return t

                def qknorm(ps_ap, r_ps, n, gcol, dst_ap, r_dst, view=None):
                    i = nctr[0] % 2
                    nctr[0] += 1
                    sq, r_sq = sqb[i]
                    rs, r_rs = rsb[i]
                    pss, r_pss = ps_b[4]
                    K.op(act, lambda: A.activation(out=sq[:, 0:n], in_=ps_ap, func=AF.Square), reads=[r_ps], writes=[r_sq])
                    K.op(pe, lambda: PE.matmul(pss[:, 0:n], lhsT=ones_bf[:], rhs=sq[:, 0:n], start=True, stop=True),
                         reads=[r_sq, r_onesbf], writes=[r_pss])
                    K.op(act, lambda: A.activation(out=rs[:, 0:n], in_=pss[:, 0:n], func=AF.Ln, scale=1.0 / 128, bias=EPS),
                         reads=[r_pss], writes=[r_rs])
                    K.op(act, lambda: A.activation(out=rs[:, 0:n], in_=rs[:, 0:n], func=AF.Exp, scale=-0.5),
                         reads=[r_rs], writes=[r_rs])
                    vw = view if view is not None else (lambda a: a)
                    K.op(dve, lambda: V.scalar_tensor_tensor(out=dst_ap, in0=vw(ps_ap), scalar=gcol, in1=vw(rs[:, 0:n]),
                                                             op0=ALU.mult, op1=ALU.mult),
                         reads=[r_ps, r_rs, r_qkg], writes=[r_dst])

                def projT(ps_ap, r_ps, w3, r_w, c0, t0, n):
                    for kc in range(16):
                        K.op(pe, lambda: PE.matmul(ps_ap, lhsT=w3[:, kc, c0:c0 + 128], rhs=hT[:, kc, t0:t0 + n],
                                                   start=(kc == 0), stop=(kc == 15)), reads=[r_w, r_hT], writes=[r_ps],
                             inc=(kc == 15))

                QCH = [(Q0, 512), (Q0 + 512, 512), (Q0 + 1024, 128)]

                if stage >= 2:
                  with ExitStack() as sd:
                    kTd = [sbuf(sd, "kTd%d" % m, [128, T], BF16) for m in range(2)]
                    qTd = [sbuf(sd, "qTd%d" % m, [128, NQ], BF16) for m in range(2)]
                    vaug, r_vaug = sbuf(sd, "vaugd", [128, 16, 258], BF16)
                    bd, r_bd = sbuf(sd, "bd", [128, 256], BF16)
                    dslots = []
                    for s_ in range(2):
                        dslots.append({
                            "ptd": sbuf(sd, "ptd%d" % s_, [128, 16 * 256], BF16),
                            "od": sbuf(sd, "od%d" % s_, [128, 256], F32),
                            "odb": sbuf(sd, "odb%d" % s_, [128, 256], BF16),
                            "junk": sbuf(sd, "junkd%d" % s_, [128, 256], F32),
                            "sm": sbuf(sd, "smd%d" % s_, [128, 8], F32),
                            "po": [ps_b[2 * s_], ps_b[2 * s_ + 1]],
                            "r_pst": K.res("pst%d" % s_),
                            "tcol": s_ * 256,
                        })
                    s_tiles = [ps_a[0], ps_a[1], ps_b[4]]
                    sctr = [0]
                    bgw = [sbuf(sd, "bgw%d" % i, [128, 16, 256], BF16) for i in range(2)]
                    bgb = [sbuf(sd, "bgb%d" % i, [1, 256], F32) for i in range(2)]
                    bgm = [sbuf(sd, "bgm%d" % i, [1, 256], F32) for i in range(2)]
                    w_ada_v2 = w_ada.rearrange("(kc p) n -> p kc n", p=128)
                    bgst = {"loaded": 0, "done": 0}
                    bgps = [ps_a[0], ps_a[1]]
                    NBG = 32

                    def bg_load():
                        c = bgst["loaded"]
                        if c >= NBG:
                            return
                        col0 = 4096 + c * 256
                        K.dma(pool, bgw[c % 2][0][:], w_ada_v2[:, :, col0:col0 + 256], writes=[bgw[c % 2][1]])
                        K.dma(sp, bgb[c % 2][0][:], b_ada[:, col0:col0 + 256], writes=[bgb[c % 2][1]])
                        bgst["loaded"] += 1

                    def bg_finish(c):
                        mr_, r_mr_ = bgm[c % 2]
                        pS, r_pS = bgps[c % 2]
                        for j in range(2):
                            K.op(pe, lambda: PE.matmul(pS[:, 256 + j:257 + j], lhsT=mr_[0:1, j * 128:(j + 1) * 128], rhs=ones_f[0:1, 0:1],
                                                       start=True, stop=True), reads=[r_mr_, r_onesf], writes=[r_pS], inc=(j == 1))
                        K.op(dve, lambda: V.tensor_copy(out=modcol[:, 32 + 2 * c:34 + 2 * c], in_=pS[:, 256:258]), reads=[r_pS],
                             writes=[r_modcol])

                    def bg_step():
                        c = bgst["done"]
                        if c > NBG:
                            return
                        bgst["done"] += 1
                        if c >= 1:
                            bg_finish(c - 1)
                        if c == NBG:
                            return
                        wt_, r_wt_ = bgw[c % 2]
                        br_, r_br_ = bgb[c % 2]
                        mr_, r_mr_ = bgm[c % 2]
                        pS, r_pS = bgps[c % 2]
                        for kc in range(16):
                            K.op(pe, lambda: PE.matmul(pS[0:1, 0:256], lhsT=scb[:, kc:kc + 1], rhs=wt_[:, kc, :], start=(kc == 0),
                                                       stop=(kc == 15)), reads=[r_scb, r_wt_], writes=[r_pS], inc=(kc == 15))
                        K.op(dve, lambda: V.tensor_tensor(out=mr_[:], in0=pS[0:1, 0:256], in1=br_[:], op=ALU.add),
                             reads=[r_pS, r_br_], writes=[r_mr_])
                        if c < 8:
                            K.dma(sp, g_scr[0:1, c * 256:(c + 1) * 256], mr_[:], reads=[r_mr_], writes=[r_gscr], semres=r_mr_)
                        elif c >= 24:
                            K.dma(sp, g_scr[1:2, (c - 24) * 256:(c - 23) * 256], mr_[:], reads=[r_mr_], writes=[r_gscr], semres=r_mr_)
                        bg_load()

                    bg_load()
                    bg_load()
                    K.op(pool, lambda: G.memset(vaug[:, :, 256:258], 1.0), writes=[r_vaug])
                    for h in range(4):
                        wq3, r_wq = ws_next(OFF_DQ + h * 256)
                        wk3, r_wk = ws_next(OFF_DK + h * 256)
                        wv3, r_wv = ws_next(OFF_DV + h * 256)
                        K.dma(sp, bd[:], _dram_ap(u_scr, (8 + h) * UL + 2048 - 127, [[1, 128], [1, 256]]), reads=[r_uscr],
                              writes=[r_bd])
                        for m in range(2):
                            for tcn in range(4):
                                pj, r_pj = ps_next()
                                projT(pj[:], r_pj, wk3, r_wk, m * 128, tcn * 512, 512)
                                qknorm(pj[:], r_pj, 512, qkg_t[:, 5:6], kTd[m][0][:, tcn * 512:(tcn + 1) * 512], kTd[m][1])
                            for (t0, n) in QCH:
                                pj, r_pj = ps_next()
                                projT(pj[:, 0:n], r_pj, wq3, r_wq, m * 128, t0, n)
                                qknorm(pj[:, 0:n], r_pj, n, qkg_t[:, 4:5], qTd[m][0][:, t0 - Q0:t0 - Q0 + n], qTd[m][1])
                        for i in range(16):
                            pj, r_pj = ps_next()
                            for kc in range(16):
                                K.op(pe, lambda: PE.matmul(pj[:, 0:256], lhsT=hT[:, kc, i * 128:(i + 1) * 128], rhs=wv3[:, kc, :],
                                                           start=(kc == 0), stop=(kc == 15)), reads=[r_hT, r_wv], writes=[r_pj],
                                     inc=(kc == 15))
                            if i % 2 == 0:
                                K.op(dve, lambda: V.tensor_copy(out=vaug[:, i, 0:256], in_=pj[:, 0:256]), reads=[r_pj],
                                     writes=[r_vaug])
                            else:
                                K.op(act, lambda: A.copy(out=vaug[:, i, 0:256], in_=pj[:, 0:256]), reads=[r_pj], writes=[r_vaug])
                        ws_release()
                        def diff_unit(qi, sl):
                            S_ = dslots[sl]
                            ptd, r_ptd = S_["ptd"]
                            od, r_od = S_["od"]
                            odb, r_odb = S_["odb"]
                            junkd, r_junkd = S_["junk"]
                            sm, r_sm = S_["sm"]
                            po = S_["po"]
                            r_ptr = S_["r_pst"]
                            tc0 = S_["tcol"]
                            qb = QT0 + qi
                            nj = qb + 1
                            for jp in range(0, nj, 2):
                                pS, r_pS = s_tiles[sctr[0] % len(s_tiles)]
                                sctr[0] += 1
                                js = [j for j in (jp, jp + 1) if j < nj]
                                for jj, j in enumerate(js):
                                    near = j >= qb - 1
                                    for m in range(2):
                                        col = jj * 256 + m * 128
                                        K.op(pe, lambda: PE.matmul(pS[:, col:col + 128], lhsT=kTd[m][0][:, j * 128:(j + 1) * 128],
                                                                   rhs=qTd[m][0][:, qi * 128:(qi + 1) * 128], start=True,
                                                                   stop=not near), reads=[kTd[m][1], qTd[m][1]], writes=[r_pS],
                                             inc=not near)
                                        if near:
                                            off = 0 if j == qb else 128
                                            K.op(pe, lambda: PE.matmul(pS[:, col:col + 128], lhsT=jmat[:], rhs=bd[:, off:off + 128],
                                                                       start=False, stop=True), reads=[r_jmat, r_bd],
                                                 writes=[r_pS])
                                ncol = len(js) * 256
                                bias_ = cm[:, 0:1] if jp < 8 else 0.0
                                rd = [r_pS, r_cm] if jp < 8 else [r_pS]
                                K.op(act, lambda: A.activation(out=ptd[:, jp * 256:jp * 256 + ncol], in_=pS[:, 0:ncol], func=AF.Exp,
                                                               bias=bias_), reads=rd, writes=[r_ptd])
                                yield "s"
                            for m in range(2):
                                for j in range(nj):
                                    K.op(pe, lambda: PE.matmul(po[m][0][:, 0:257], lhsT=ptd[:, j * 256 + m * 128:j * 256 + m * 128 + 128],
                                                               rhs=vaug[:, j, 0:257], start=(j == 0), stop=(j == nj - 1)),
                                         reads=[r_ptd, r_vaug], writes=[po[m][1]], inc=(j == nj - 1))
                                    if j % 4 == 3:
                                        yield "av"
                                yield "av"
                            K.op(dve, lambda: V.tensor_scalar(out=sm[:, 0:1], in0=po[0][0][:, 256:257], scalar1=1e-20,
                                                              scalar2=None, op0=ALU.max), reads=[po[0][1]], writes=[r_sm])
                            K.op(dve, lambda: V.tensor_scalar(out=sm[:, 1:2], in0=po[1][0][:, 256:257], scalar1=1e-20,
                                                              scalar2=None, op0=ALU.max), reads=[po[1][1]], writes=[r_sm])
                            yield "e"
                            K.op(dve, lambda: V.reciprocal(out=sm[:, 0:2], in_=sm[:, 0:2]), reads=[r_sm], writes=[r_sm])
                            yield "e"
                            K.op(dve, lambda: V.tensor_tensor(out=sm[:, 2:3], in0=sm[:, 1:2], in1=nlam[:], op=ALU.mult),
                                 reads=[r_sm, r_nlam], writes=[r_sm])
                            yield "e"
                            K.op(dve, lambda: V.tensor_scalar(out=od[:], in0=po[0][0][:, 0:256], scalar1=sm[:, 0:1], scalar2=None,
                                                              op0=ALU.mult), reads=[po[0][1], r_sm], writes=[r_od])
                            yield "e"
                            K.op(dve, lambda: V.scalar_tensor_tensor(out=od[:], in0=po[1][0][:, 0:256], scalar=sm[:, 2:3], in1=od[:],
                                                                     op0=ALU.mult, op1=ALU.add), reads=[po[1][1], r_sm, r_od],
                                 writes=[r_od])
                            yield "e"
                            K.op(act, lambda: A.activation(out=junkd[:], in_=od[:], func=AF.Square, accum_out=sm[:, 3:4]),
                                 reads=[r_od], writes=[r_junkd, r_sm])
                            yield "e"
                            K.op(act, lambda: A.activation(out=sm[:, 4:5], in_=sm[:, 3:4], func=AF.Ln, scale=1.0 / 256, bias=EPS),
                                 reads=[r_sm], writes=[r_sm])
                            yield "e"
                            K.op(act, lambda: A.activation(out=sm[:, 4:5], in_=sm[:, 4:5], func=AF.Exp, scale=-0.5),
                                 reads=[r_sm], writes=[r_sm])
                            yield "e"
                            K.op(dve, lambda: V.scalar_tensor_tensor(out=odb[:], in0=od[:], scalar=sm[:, 4:5], in1=sublbc[:],
                                                                     op0=ALU.mult, op1=ALU.mult), reads=[r_od, r_sm, r_sublbc],
                                 writes=[r_odb])
                            yield "e"
                            for hf in range(2):
                                K.op(pe, lambda: PE.transpose(out=ps_t[:, tc0 + hf * 128:tc0 + (hf + 1) * 128],
                                                              in_=odb[:, hf * 128:(hf + 1) * 128], identity=ident[:]),
                                     reads=[r_odb, r_ident], writes=[r_ptr], inc=(hf == 1))
                            yield "e"
                            K.op(act, lambda: A.copy(out=oT_diff[:, 2 * h:2 * h + 2, qi * 128:(qi + 1) * 128],
                                                     in_=ps_t[:, tc0:tc0 + 256].rearrange("p (a b) -> p a b", b=128)),
                                 reads=[r_ptr], writes=[r_oTd])

                        active = []
                        free_slots = [0, 1]
                        nxt = 0
                        primed = False
                        while active or nxt < NQT:
                            while len(active) < 2 and nxt < NQT:
                                sl = free_slots.pop(0)
                                gnr = diff_unit(nxt, sl)
                                nxt += 1
                                if not primed and active == []:
                                    primed = True
                                    live = True
                                    try:
                                        while next(gnr) == "s":
                                            pass
                                    except StopIteration:
                                        live = False
                                    if live:
                                        active.append((gnr, sl))
                                    else:
                                        free_slots.append(sl)
                                else:
                                    active.append((gnr, sl))
                            for ent in list(active):
                                try:
                                    next(ent[0])
                                except StopIteration:
                                    active.remove(ent)
                                    free_slots.append(ent[1])
                                    bg_step()
                    while bgst["done"] <= NBG:
                        bg_step()
                    K.barrier()

                if stage >= 3:
                  with ExitStack() as sn:
                    arena, _ = sbuf(sn, "arena", [128, 12288], BF16)
                    w1t = [(arena[:, 4096 + i * 4096:8192 + i * 4096].rearrange("p (l j) -> p l j", j=128), K.res("w1t%d" % i))
                           for i in range(2)]
                    w2t = [sbuf(sn, "w2t%d" % i, [128, 128], BF16) for i in range(2)]
                    pet = [sbuf(sn, "pet%d" % i, [128, 32], BF16) for i in range(2)]
                    cst, r_cst = sbuf(sn, "cst", [128, 2], F32)
                    ebf, r_ebf = sbuf(sn, "ebf", [32, T], BF16)
                    vcaug, r_vcaug = sbuf(sn, "vcaug", [128, 162], BF16)
                    gates, r_gates = sbuf(sn, "gates", [128, NQT, 24], F32)
                    smk, r_smk = sbuf(sn, "smk", [128, NQT, 64], F32)
                    w4t, r_w4t = sbuf(sn, "w4t", [128, 128], BF16)
                    K.dma(sp, smk[:], selmask, writes=[r_smk])
                    K.dma(pool, ebf[:], e_mat, writes=[r_ebf])
                    K.op(dve, lambda: V.memset(vcaug[:], 1.0), writes=[r_vcaug])
                    K.dma(pool, vcaug[0:127, 129:161], ovl, writes=[r_vcaug])
                    K.op(pool, lambda: G.memset(tmpf[:], 0.0), reads=[r_tmpf], writes=[r_tmpf])
                    K.op(pool, lambda: G.affine_select(out=tmpf[:], in_=tmpf[:], pattern=[[-1, 128]], compare_op=ALU.is_ge,
                                                       fill=NEGV, base=126, channel_multiplier=-1), reads=[r_tmpf], writes=[r_tmpf])
                    K.op(dve, lambda: V.tensor_copy(out=w4t[:], in_=tmpf[:]), reads=[r_tmpf], writes=[r_w4t])
                    for i in range(2):
                        K.dma(pool, w2t[i][0][:], cmp_w2[i], writes=[w2t[i][1]])
                        K.dma(pool, pet[i][0][:], cmp_pe[i], writes=[pet[i][1]])
                    wg3, r_wg = ws_next(OFF_NSA_G)
                    for qi in range(NQT):
                        pj, r_pj = ps_next()
                        for kc in range(16):
                            K.op(pe, lambda: PE.matmul(pj[:, 0:24], lhsT=hT[:, kc, Q0 + qi * 128:Q0 + (qi + 1) * 128], rhs=wg3[:, kc, :],
                                                       start=(kc == 0), stop=(kc == 15)), reads=[r_hT, r_wg], writes=[r_pj],
                                 inc=(kc == 15))
                        K.op(act, lambda: A.activation(out=gates[:, qi, :], in_=pj[:, 0:24], func=AF.Exp, scale=-1.0),
                             reads=[r_pj], writes=[r_gates])
                    ws_release()
                    K.op(dve, lambda: V.tensor_scalar(out=gates[:], in0=gates[:], scalar1=1.0, scalar2=None, op0=ALU.add),
                         reads=[r_gates], writes=[r_gates])
                    K.op(dve, lambda: V.reciprocal(out=gates[:], in_=gates[:]), reads=[r_gates], writes=[r_gates])

                    zT = [(arena[:, i * 2048:(i + 1) * 2048], K.res("zT%d" % i)) for i in range(2)]
                    kTn = [sbuf(sn, "kTn%d" % i, [128, T], BF16) for i in range(2)]
                    vaugn = [sbuf(sn, "vaugn%d" % i, [128, 16, 130], BF16) for i in range(2)]
                    qTn, r_qTn = sbuf(sn, "qTn", [128, NQT, 4, 128], BF16)
                    kcT, r_kcT = sbuf(sn, "kcT", [128, 128], BF16)
                    glu = [sbuf(sn, "glu%d" % i, [128, 128], BF16) for i in range(2)]
                    xg, r_xg = sbuf(sn, "xg", [128, 128], F32)
                    tg, r_tg = sbuf(sn, "tg", [128, 128], F32)
                    bn, r_bn = sbuf(sn, "bn", [128, 4, 256], BF16)
                    cbt = [sbuf(sn, "cbt%d" % i, [128, 4, 128], BF16) for i in range(2)]
                    pts, r_pts = arena[:, 0:8192].rearrange("p (a b) -> p a b", b=512), K.res("pts")
                    ptw, r_ptw = arena[:, 8192:8192 + 2560].rearrange("p (a b) -> p a b", b=512), K.res("ptw")
                    ptc, r_ptc = arena[:, 11264:11776], K.res("ptc")
                    oaccs = [sbuf(sn, "oacc%d" % i, [128, 4, 128], F32) for i in range(2)]
                    obf, r_obf = sbuf(sn, "obf", [128, 4, 128], BF16)
                    imp, r_imp = sbuf(sn, "imp", [128, 32], F32)
                    scr_, r_scr_ = sbuf(sn, "scr_", [128, 32], F32)
                    wk_, r_wk_ = sbuf(sn, "wk_", [128, 32], F32)
                    m8, r_m8 = sbuf(sn, "m8", [128, 16], F32)
                    negb, r_negb = sbuf(sn, "negb", [128, 32], BF16)
                    negT, r_negT = sbuf(sn, "negT", [32, 128], BF16)
                    smh, r_smh = sbuf(sn, "smh", [128, 16], F32)
                    smt, r_smt = sbuf(sn, "smt", [128, 16], F32)
                    r_pst_lo = K.res("pst_lo")
                    r_pst_hi = K.res("pst_hi")
                    impt, r_impt = sbuf(sn, "impt", [128, 4, 32], F32)
                    thr, r_thr = sbuf(sn, "thr", [128, 1], F32)
                    tmp2h = [sbuf(sn, "tmp2h%d" % i, [128, 2, 128], F32) for i in range(1)] * 2
                    tmp2t = [sbuf(sn, "tmp2t%d" % i, [128, 2, 128], F32) for i in range(1)] * 2
                    for i in range(2):
                        K.op(pool, lambda: G.memset(vaugn[i][0][:, :, 128:130], 1.0), writes=[vaugn[i][1]])

                    for g in range(2):
                        def kvcol(br, kv):
                            return OFF_NSA_KV + ((br * 2 + kv) * 2 + g) * 128
                        K.dma(sp, bn[:], _dram_ap(u_scr, (4 * g) * UL + 2048 - 127, [[1, 128], [UL, 4], [1, 256]]), reads=[r_uscr],
                              writes=[r_bn])
                        if g == 1:
                            K.barrier()
                            ps_rot[:] = list(ps_a) + [ps_b[2], ps_b[3]]
                        for i in range(2):
                            K.dma(pool, w1t[i][0], cmp_w1[i].rearrange("(l d) j -> d l j", d=128), writes=[w1t[i][1]])
                        if g == 0:
                            for i in range(2):
                                pt_, r_pt_ = ps_next()
                                for l in range(32):
                                    K.op(pe, lambda: PE.matmul(pt_[:, 0:1], lhsT=w1t[i][0][:, l, :], rhs=pet[i][0][:, l:l + 1],
                                                               start=(l == 0), stop=(l == 31)), reads=[w1t[i][1], pet[i][1]],
                                         writes=[r_pt_], inc=(l == 31))
                                K.op(dve, lambda: V.tensor_copy(out=cst[:, i:i + 1], in_=pt_[:, 0:1]), reads=[r_pt_], writes=[r_cst])
                        for i in range(2):
                            wz3, r_wz = ws_next(kvcol(0, i))
                            for tcn in range(4):
                                pj, r_pj = ps_next()
                                projT(pj[:], r_pj, wz3, r_wz, 0, tcn * 512, 512)
                                if tcn % 2 == 0:
                                    K.op(dve, lambda: V.tensor_copy(out=zT[i][0][:, tcn * 512:(tcn + 1) * 512], in_=pj[:]), reads=[r_pj],
                                         writes=[zT[i][1]])
                                else:
                                    K.op(act, lambda: A.copy(out=zT[i][0][:, tcn * 512:(tcn + 1) * 512], in_=pj[:]), reads=[r_pj],
                                         writes=[zT[i][1]])
                            if tcn == 3:
                                ws_release()
                        for i in range(2):
                            pj, r_pj = ps_next()
                            for l in range(32):
                                K.op(pe, lambda: PE.matmul(pj[:, 0:127], lhsT=w1t[i][0][:, l, :], rhs=zT[i][0][:, l:l + 16 * 126 + 1:16],
                                                           start=(l == 0), stop=(l == 31)), reads=[w1t[i][1], zT[i][1]], writes=[r_pj],
                                     inc=(l == 31))
                            K.op(dve, lambda: V.tensor_scalar(out=xg[:, 0:127], in0=pj[:, 0:127], scalar1=cst[:, i:i + 1], scalar2=None,
                                                              op0=ALU.add), reads=[r_pj, r_cst], writes=[r_xg])
                            K.op(dve, lambda: V.tensor_tensor(out=tg[:, 0:127], in0=xg[:, 0:127], in1=xg[:, 0:127], op=ALU.mult),
                                 reads=[r_xg], writes=[r_tg])
                            K.op(dve, lambda: V.tensor_scalar(out=tg[:, 0:127], in0=tg[:, 0:127], scalar1=0.044715, scalar2=1.0,
                                                              op0=ALU.mult, op1=ALU.add), reads=[r_tg], writes=[r_tg])
                            K.op(dve, lambda: V.tensor_tensor(out=tg[:, 0:127], in0=tg[:, 0:127], in1=xg[:, 0:127], op=ALU.mult),
                                 reads=[r_tg, r_xg], writes=[r_tg])
                            K.op(act, lambda: A.activation(out=tg[:, 0:127], in_=tg[:, 0:127], func=AF.Exp, scale=-1.5957691216),
                                 reads=[r_tg], writes=[r_tg])
                            K.op(dve, lambda: V.tensor_scalar(out=tg[:, 0:127], in0=tg[:, 0:127], scalar1=1.0, scalar2=None,
                                                              op0=ALU.add), reads=[r_tg], writes=[r_tg])
                            K.op(dve, lambda: V.reciprocal(out=tg[:, 0:127], in_=tg[:, 0:127]), reads=[r_tg], writes=[r_tg])
                            K.op(dve, lambda: V.tensor_tensor(out=glu[i][0][:, 0:127], in0=tg[:, 0:127], in1=xg[:, 0:127], op=ALU.mult),
                                 reads=[r_tg, r_xg], writes=[glu[i][1]])
                        pj, r_pj = ps_next()
                        K.op(pe, lambda: PE.matmul(pj[:, 0:127], lhsT=w2t[0][0][:], rhs=glu[0][0][:, 0:127], start=True, stop=True),
                             reads=[w2t[0][1], glu[0][1]], writes=[r_pj])
                        qknorm(pj[:, 0:127], r_pj, 127, qkg_t[:, 1:2], kcT[:, 0:127], r_kcT)
                        pj, r_pj = ps_next()
                        K.op(pe, lambda: PE.matmul(pj[0:127, 0:128], lhsT=glu[1][0][:, 0:127], rhs=w2t[1][0][:], start=True, stop=True),
                             reads=[w2t[1][1], glu[1][1]], writes=[r_pj])
                        K.op(dve, lambda: V.tensor_copy(out=vcaug[0:127, 0:128], in_=pj[0:127, 0:128]), reads=[r_pj], writes=[r_vcaug])
                        for bi, br in enumerate((1, 2)):
                            wz3, r_wz = ws_next(kvcol(br, 0))
                            for tcn in range(4):
                                pj, r_pj = ps_next()
                                projT(pj[:], r_pj, wz3, r_wz, 0, tcn * 512, 512)
                                qknorm(pj[:], r_pj, 512, qkg_t[:, 1 + br:2 + br], kTn[bi][0][:, tcn * 512:(tcn + 1) * 512], kTn[bi][1])
                            ws_release()
                            wz3, r_wz = ws_next(kvcol(br, 1))
                            for i in range(16):
                                pj, r_pj = ps_next()
                                for kc in range(16):
                                    K.op(pe, lambda: PE.matmul(pj[:, 0:128], lhsT=hT[:, kc, i * 128:(i + 1) * 128], rhs=wz3[:, kc, :],
                                                               start=(kc == 0), stop=(kc == 15)), reads=[r_hT, r_wz], writes=[r_pj],
                                         inc=(kc == 15))
                                if i % 2 == 0:
                                    K.op(dve, lambda: V.tensor_copy(out=vaugn[bi][0][:, i, 0:128], in_=pj[:, 0:128]), reads=[r_pj],
                                         writes=[vaugn[bi][1]])
                                else:
                                    K.op(act, lambda: A.copy(out=vaugn[bi][0][:, i, 0:128], in_=pj[:, 0:128]), reads=[r_pj],
                                         writes=[vaugn[bi][1]])
                            ws_release()
                        for hp in range(2):
                            wq3, r_wq = ws_next((4 * g + 2 * hp) * 128)
                            for hh in range(2):
                                hl = 2 * hp + hh
                                for (t0, n) in QCH:
                                    pj, r_pj = ps_next()
                                    projT(pj[:, 0:n], r_pj, wq3, r_wq, hh * 128, t0, n)
                                    q0 = (t0 - Q0) // 128
                                    qknorm(pj[:, 0:n], r_pj, n, qkg_t[:, 0:1], qTn[:, q0:q0 + n // 128, hl, :], r_qTn,
                                           view=lambda a: a.rearrange("p (a b) -> p a b", b=128))
                            ws_release()
                        K.barrier()
                        ps_rot[:] = list(ps_a) + [ps_b[4]]
                        def nsa_head(qi):
                            oacc_, r_oacc_ = oaccs[qi % 2]
                            qb = QT0 + qi
                            qrhs = qTn[:, qi].rearrange("p a b -> p (a b)")
                            cb, r_cb = cbt[qi % 2]
                            K.dma(sp, cb[0:127], _dram_ap(u_scr, (4 * g) * UL + 128 * qb + 1, [[16, 127], [UL, 4], [1, 128]]),
                                  reads=[r_uscr], writes=[r_cb])
                            pS, r_pS = ps_next()
                            K.op(pe, lambda: PE.matmul(pS[0:127, :], lhsT=kcT[:, 0:127], rhs=qrhs, start=True, stop=False),
                                 reads=[r_kcT, r_qTn], writes=[r_pS], inc=False)
                            K.op(pe, lambda: PE.matmul(pS[0:127, :], lhsT=j127[0:127, 0:127],
                                                       rhs=cb[0:127].rearrange("p a b -> p (a b)"), start=False, stop=True),
                                 reads=[r_j127, r_cb], writes=[r_pS])
                            K.op(act, lambda: A.activation(out=ptc[0:127, :], in_=pS[0:127, :], func=AF.Exp, bias=cm[0:127, 2:3]),
                                 reads=[r_pS, r_cm], writes=[r_ptc])
                            yield
                            pc_ = [ps_b[0], ps_b[1]]
                            for hl in range(4):
                                pcb, r_pcb = pc_[hl // 2]
                                c0 = (hl % 2) * 161
                                K.op(pe, lambda: PE.matmul(pcb[:, c0:c0 + 161], lhsT=ptc[0:127, hl * 128:(hl + 1) * 128],
                                                           rhs=vcaug[0:127, 0:161], start=True, stop=True), reads=[r_ptc, r_vcaug],
                                     writes=[r_pcb])
                            for b_ in range(2):
                                pcb, r_pcb = pc_[b_]
                                K.op(dve, lambda: V.tensor_scalar(out=smh[:, 2 * b_:2 * b_ + 2], in0=pcb[:, 128:290:161], scalar1=1e-20,
                                                                  scalar2=None, op0=ALU.max), reads=[r_pcb], writes=[r_smh])
                            yield
                            K.op(dve, lambda: V.reciprocal(out=smh[:, 0:4], in_=smh[:, 0:4]), reads=[r_smh], writes=[r_smh])
                            K.op(dve, lambda: V.tensor_tensor(out=smh[:, 4:8], in0=smh[:, 0:4], in1=gates[:, qi, 12 * g:12 * g + 10:3],
                                                              op=ALU.mult), reads=[r_smh, r_gates], writes=[r_smh])
                            for b_ in range(2):
                                pcb, r_pcb = pc_[b_]
                                pv = pcb[:, 0:322].rearrange("p (h c) -> p h c", c=161)
                                K.op(dve, lambda: V.tensor_tensor(out=oacc_[:, 2 * b_:2 * b_ + 2, :], in0=pv[:, :, 0:128],
                                                                  in1=smh[:, 4 + 2 * b_:6 + 2 * b_].unsqueeze(2).to_broadcast([128, 2, 128]),
                                                                  op=ALU.mult), reads=[r_pcb, r_smh], writes=[r_oacc_])
                                K.op(dve, lambda: V.tensor_tensor(out=impt[:, 2 * b_:2 * b_ + 2, :], in0=pv[:, :, 129:161],
                                                                  in1=smh[:, 2 * b_:2 * b_ + 2].unsqueeze(2).to_broadcast([128, 2, 32]),
                                                                  op=ALU.mult), reads=[r_pcb, r_smh], writes=[r_impt])
                            yield
                            K.op(dve, lambda: V.tensor_reduce(out=imp[:], in_=impt[:].rearrange("p h s -> p s h"), axis=AX.X, op=ALU.add),
                                 reads=[r_impt], writes=[r_imp])
                            for jj in range(5):
                                j = qb - 4 + jj
                                pS, r_pS = ps_next()
                                hasb = jj in (0, 3, 4)
                                K.op(pe, lambda: PE.matmul(pS[:], lhsT=kTn[1][0][:, j * 128:(j + 1) * 128], rhs=qrhs, start=True,
                                                           stop=not hasb), reads=[kTn[1][1], r_qTn], writes=[r_pS], inc=not hasb)
                                if jj == 0:
                                    K.op(pe, lambda: PE.matmul(pS[:], lhsT=jmat[:], rhs=w4t[:, 0:128].unsqueeze(1).to_broadcast([128, 4, 128]),
                                                               start=False, stop=True), reads=[r_jmat, r_w4t], writes=[r_pS])
                                elif jj >= 3:
                                    off = 0 if jj == 4 else 128
                                    K.op(pe, lambda: PE.matmul(pS[:], lhsT=jmat[:], rhs=bn[:, :, off:off + 128], start=False, stop=True),
                                         reads=[r_jmat, r_bn], writes=[r_pS])
                                bias_ = cm[:, 0:1] if j < 8 else 0.0
                                rd = [r_pS, r_cm] if j < 8 else [r_pS]
                                K.op(act, lambda: A.activation(out=ptw[:, jj, :], in_=pS[:], func=AF.Exp, bias=bias_), reads=rd,
                                     writes=[r_ptw])
                                yield
                            pw_ = [ps_b[0], ps_b[1]]
                            for hl in range(4):
                                pwb, r_pwb = pw_[hl // 2]
                                c0 = (hl % 2) * 129
                                for jj in range(5):
                                    j = qb - 4 + jj
                                    K.op(pe, lambda: PE.matmul(pwb[:, c0:c0 + 129], lhsT=ptw[:, jj, hl * 128:(hl + 1) * 128],
                                                               rhs=vaugn[1][0][:, j, 0:129], start=(jj == 0), stop=(jj == 4)),
                                         reads=[r_ptw, vaugn[1][1]], writes=[r_pwb], inc=(jj == 4))
                                yield
                            K.op(dve, lambda: V.tensor_tensor(out=scr_[:], in0=imp[:], in1=smk[:, qi, 0:32], op=ALU.mult),
                                 reads=[r_imp, r_smk], writes=[r_scr_])
                            K.op(dve, lambda: V.tensor_tensor(out=scr_[:], in0=scr_[:], in1=smk[:, qi, 32:64], op=ALU.add),
                                 reads=[r_scr_, r_smk], writes=[r_scr_])
                            yield
                            K.op(dve, lambda: V.max(out=m8[:, 0:8], in_=scr_[:]), reads=[r_scr_], writes=[r_m8])
                            K.op(dve, lambda: V.match_replace(out=wk_[:], in_to_replace=m8[:, 0:8], in_values=scr_[:], imm_value=-3e4),
                                 reads=[r_scr_, r_m8], writes=[r_wk_])
                            K.op(dve, lambda: V.max(out=m8[:, 8:16], in_=wk_[:]), reads=[r_wk_, r_m8], writes=[r_m8])
                            yield
                            K.op(dve, lambda: V.tensor_reduce(out=thr[:], in_=m8[:, 8:16], axis=AX.X, op=ALU.min), reads=[r_m8],
                                 writes=[r_thr])
                            K.op(dve, lambda: V.tensor_scalar(out=scr_[:], in0=scr_[:], scalar1=thr[:], scalar2=None, op0=ALU.is_ge),
                                 reads=[r_scr_, r_thr], writes=[r_scr_])
                            K.op(dve, lambda: V.tensor_scalar(out=negb[:], in0=scr_[:], scalar1=-NEGV, scalar2=NEGV, op0=ALU.mult,
                                                              op1=ALU.add), reads=[r_scr_], writes=[r_negb])
                            yield
                            ptnb, r_ptn = ps_t[:, 512:1024], r_pst_hi
                            K.op(pe, lambda: PE.transpose(out=ptnb[0:32, 0:128], in_=negb[:, 0:32], identity=ident[:]),
                                 reads=[r_negb, r_ident], writes=[r_ptn])
                            K.op(dve, lambda: V.tensor_copy(out=negT[:], in_=ptnb[0:32, 0:128]), reads=[r_ptn], writes=[r_negT])
                            yield
                            for b_ in range(2):
                                K.op(dve, lambda: V.tensor_scalar(out=smh[:, 8 + 2 * b_:10 + 2 * b_], in0=pw_[b_][0][:, 128:258:129],
                                                                  scalar1=1e-20, scalar2=None, op0=ALU.max), reads=[pw_[b_][1]],
                                     writes=[r_smh])
                            K.op(dve, lambda: V.reciprocal(out=smh[:, 8:12], in_=smh[:, 8:12]), reads=[r_smh], writes=[r_smh])
                            K.op(dve, lambda: V.tensor_tensor(out=smh[:, 8:12], in0=smh[:, 8:12], in1=gates[:, qi, 12 * g + 2:12 * g + 12:3],
                                                              op=ALU.mult), reads=[r_smh, r_gates], writes=[r_smh])
                            yield
                            for b_ in range(2):
                                pb_, r_pb_ = pw_[b_]
                                pv = pb_[:, 0:258].rearrange("p (h c) -> p h c", c=129)
                                t2, r_t2 = tmp2h[b_]
                                K.op(dve, lambda: V.tensor_tensor(out=t2[:], in0=pv[:, :, 0:128],
                                                                  in1=smh[:, 8 + 2 * b_:10 + 2 * b_].unsqueeze(2).to_broadcast([128, 2, 128]),
                                                                  op=ALU.mult), reads=[r_pb_, r_smh], writes=[r_t2])
                                K.op(dve, lambda: V.tensor_tensor(out=oacc_[:, 2 * b_:2 * b_ + 2, :], in0=oacc_[:, 2 * b_:2 * b_ + 2, :],
                                                                  in1=t2[:], op=ALU.add), reads=[r_oacc_, r_t2], writes=[r_oacc_])
                                yield
                            nj = qb + 1
                            for j in range(nj):
                                pS, r_pS = ps_next()
                                near = j >= qb - 1
                                K.op(pe, lambda: PE.matmul(pS[:], lhsT=kTn[0][0][:, j * 128:(j + 1) * 128], rhs=qrhs, start=True,
                                                           stop=False), reads=[kTn[0][1], r_qTn], writes=[r_pS], inc=False)
                                if j != qb:
                                    K.op(pe, lambda: PE.matmul(pS[:], lhsT=ebf[:, j * 128:(j + 1) * 128],
                                                               rhs=negT[:, 0:128].unsqueeze(1).to_broadcast([32, 4, 128]), start=False,
                                                               stop=not near), reads=[r_ebf, r_negT], writes=[r_pS], inc=not near)
                                if near:
                                    off = 0 if j == qb else 128
                                    K.op(pe, lambda: PE.matmul(pS[:], lhsT=jmat[:], rhs=bn[:, :, off:off + 128], start=False, stop=True),
                                         reads=[r_jmat, r_bn], writes=[r_pS])
                                bias_ = cm[:, 0:1] if j < 8 else 0.0
                                rd = [r_pS, r_cm] if j < 8 else [r_pS]
                                K.op(act, lambda: A.activation(out=pts[:, j, :], in_=pS[:], func=AF.Exp, bias=bias_), reads=rd,
                                     writes=[r_pts])
                                yield
                            px_ = [ps_b[2], ps_b[3]]
                            for hl in range(4):
                                pxb, r_pxb = px_[hl // 2]
                                c0 = (hl % 2) * 129
                                for j in range(nj):
                                    K.op(pe, lambda: PE.matmul(pxb[:, c0:c0 + 129], lhsT=pts[:, j, hl * 128:(hl + 1) * 128],
                                                               rhs=vaugn[0][0][:, j, 0:129], start=(j == 0), stop=(j == nj - 1)),
                                         reads=[r_pts, vaugn[0][1]], writes=[r_pxb], inc=(j == nj - 1))
                        def nsa_tail(qi):
                            oacc_, r_oacc_ = oaccs[qi % 2]
                            qb = QT0 + qi
                            px_ = [ps_b[2], ps_b[3]]
                            for b_ in range(2):
                                K.op(dve, lambda: V.tensor_scalar(out=smt[:, 12 + 2 * b_:14 + 2 * b_], in0=px_[b_][0][:, 128:258:129],
                                                                  scalar1=1e-20, scalar2=None, op0=ALU.max), reads=[px_[b_][1]],
                                     writes=[r_smt])
                            K.op(dve, lambda: V.reciprocal(out=smt[:, 12:16], in_=smt[:, 12:16]), reads=[r_smt], writes=[r_smt])
                            yield
                            K.op(dve, lambda: V.tensor_tensor(out=smt[:, 12:16], in0=smt[:, 12:16], in1=gates[:, qi, 12 * g + 1:12 * g + 11:3],
                                                              op=ALU.mult), reads=[r_smt, r_gates], writes=[r_smt])
                            yield
                            for bi, pp_ in ((1, px_),):
                                for b_ in range(2):
                                    pb_, r_pb_ = pp_[b_]
                                    pv = pb_[:, 0:258].rearrange("p (h c) -> p h c", c=129)
                                    t2, r_t2 = tmp2t[b_]
                                    cc0 = 8 + 4 * bi + 2 * b_
                                    K.op(dve, lambda: V.tensor_tensor(out=t2[:], in0=pv[:, :, 0:128],
                                                                      in1=smt[:, cc0:cc0 + 2].unsqueeze(2).to_broadcast([128, 2, 128]),
                                                                      op=ALU.mult), reads=[r_pb_, r_smt], writes=[r_t2])
                                    if bi == 0:
                                        K.op(dve, lambda: V.tensor_tensor(out=oacc_[:, 2 * b_:2 * b_ + 2, :], in0=oacc_[:, 2 * b_:2 * b_ + 2, :],
                                                                          in1=t2[:], op=ALU.add), reads=[r_oacc_, r_t2], writes=[r_oacc_])
                                    else:
                                        K.op(dve, lambda: V.tensor_tensor(out=obf[:, 2 * b_:2 * b_ + 2, :], in0=oacc_[:, 2 * b_:2 * b_ + 2, :],
                                                                          in1=t2[:], op=ALU.add), reads=[r_oacc_, r_t2], writes=[r_obf])
                                    yield
                            yield
                            ptrb, r_ptr = ps_t[:, 0:512], r_pst_lo
                            for hl in range(4):
                                K.op(pe, lambda: PE.transpose(out=ptrb[:, hl * 128:(hl + 1) * 128], in_=obf[:, hl, :], identity=ident[:]),
                                     reads=[r_obf, r_ident], writes=[r_ptr], inc=(hl == 3))
                            yield
                            K.op(act, lambda: A.copy(out=oT_nsa[:, 4 * g:4 * g + 4, qi * 128:(qi + 1) * 128],
                                                     in_=ptrb[:, 0:512].rearrange("p (a b) -> p a b", b=128)),
                                 reads=[r_ptr], writes=[r_oTn])
                        prev_tail = None
                        for qi in range(NQT):
                            hg = nsa_head(qi)
                            if prev_tail is not None:
                                t_alive = True
                                h_alive = True
                                while t_alive:
                                    if h_alive:
                                        try:
                                            next(hg)
                                        except StopIteration:
                                            h_alive = False
                                    try:
                                        next(prev_tail)
                                    except StopIteration:
                                        t_alive = False
                                if h_alive:
                                    for _ in hg:
                                        pass
                            else:
                                for _ in hg:
                                    pass
                            prev_tail = nsa_tail(qi)
                        for _ in prev_tail:
                            pass

                    K.barrier()
                K.barrier()

            if debug:
                of, r_of = sbuf(stB, "of", [128, 16 * NQ], F32)
                d = dbg_out("oT", [128, 16 * NQ])
                K.op(dve, lambda: V.tensor_copy(out=of[:, 0:8 * NQ], in_=oT_nsa[:].rearrange("p a b -> p (a b)")), reads=[r_oTn],
                     writes=[r_of])
                K.op(dve, lambda: V.tensor_copy(out=of[:, 8 * NQ:16 * NQ], in_=oT_diff[:].rearrange("p a b -> p (a b)")),
                     reads=[r_oTd], writes=[r_of])
                K.dma(sp, d, of[:], reads=[r_of], writes=[r_out], semres=r_of)
            if stage <= 3:
                K.barrier()
                return nc, dbg

            with ExitStack() as st:
                mergedT, r_mT = sbuf(st, "mergedT", [128, 16, NQ], BF16)
                wbufs = [sbuf(st, "wc%d" % i, [128, 4096], BF16) for i in range(5)]
                g1bc, r_g1bc = sbuf(st, "g1bc", [128, D], F32)
                sg = [sbuf(st, "sg%d" % i, [128, 512], F32) for i in range(4)]
                xts = [sbuf(st, "xts%d" % i, [128, 256], F32) for i in range(3)]
                x1s = [sbuf(st, "x1s%d" % i, [128, 256], F32) for i in range(3)]
                pss_ = [psum(st, "pcc%d" % i, [128, 512], F32) for i in range(8)]
                K.dma(sp, g1bc[:], _dram_ap(g_scr, 0, [[0, 128], [1, D]]), reads=[r_gscr], writes=[r_g1bc])
                wno_v = w_nsa_out.rearrange("(c p) n -> p c n", p=128)
                wdo_v = w_diff_out.rearrange("(c p) n -> p c n", p=128)
                TCH = [(0, 512), (512, 512), (1024, 128)]
                unit = 0
                wi = 0
                cload = {}

                def c_load(nch_):
                    wA_, r_wA_ = wbufs[(2 * nch_) % 5]
                    wa3_ = wA_[:, 0:1024].rearrange("p (c n) -> p c n", n=128)
                    wd3_ = wA_[:, 1024:2048].rearrange("p (c n) -> p c n", n=128)
                    K.dma(pool, wa3_, wno_v[:, :, nch_ * 128:(nch_ + 1) * 128], writes=[r_wA_])
                    K.dma(pool, wd3_, wdo_v[:, :, nch_ * 128:(nch_ + 1) * 128], writes=[r_wA_])
                    wB_, r_wB_ = wbufs[(2 * nch_ + 1) % 5]
                    wm0_ = wB_[:, 0:2048].rearrange("p (c n) -> p c n", n=128)
                    wm1_ = wB_[:, 2048:4096].rearrange("p (c n) -> p c n", n=128)
                    K.dma(pool, wm0_, w_in_v[:, :, OFF_MG + nch_ * 128:OFF_MG + (nch_ + 1) * 128], writes=[r_wB_])
                    K.dma(pool, wm1_, w_in_v[:, :, OFF_MG + D + nch_ * 128:OFF_MG + D + (nch_ + 1) * 128], writes=[r_wB_])
                    cload[nch_] = (wa3_, wd3_, r_wA_, wm0_, wm1_, r_wB_)
                c_load(0)
                for nch in range(16):
                    if nch + 1 < 16:
                        c_load(nch + 1)
                    wa3, wd3, r_wA, wm0, wm1, r_wB = cload.pop(nch)
                    for (t0, n) in TCH:
                        pset = pss_[(unit % 2) * 4:(unit % 2) * 4 + 4]
                        (pn, r_pn), (pd, r_pd), (p0, r_p0), (p1, r_p1) = pset
                        s0, r_s0 = sg[(unit % 2) * 2]
                        s1, r_s1 = sg[(unit % 2) * 2 + 1]
                        unit += 1
                        for c in range(8):
                            K.op(pe, lambda: PE.matmul(pn[:, 0:n], lhsT=wa3[:, c, :], rhs=oT_nsa[:, c, t0:t0 + n], start=(c == 0),
                                                       stop=(c == 7)), reads=[r_wA, r_oTn], writes=[r_pn], inc=(c == 7))
                        for c in range(8):
                            K.op(pe, lambda: PE.matmul(pd[:, 0:n], lhsT=wd3[:, c, :], rhs=oT_diff[:, c, t0:t0 + n], start=(c == 0),
                                                       stop=(c == 7)), reads=[r_wA, r_oTd], writes=[r_pd], inc=(c == 7))
                        for kc in range(16):
                            K.op(pe, lambda: PE.matmul(p0[:, 0:n], lhsT=wm0[:, kc, :], rhs=hT[:, kc, Q0 + t0:Q0 + t0 + n],
                                                       start=(kc == 0), stop=(kc == 15)), reads=[r_wB, r_hT], writes=[r_p0],
                                 inc=(kc == 15))
                        for kc in range(16):
                            K.op(pe, lambda: PE.matmul(p1[:, 0:n], lhsT=wm1[:, kc, :], rhs=hT[:, kc, Q0 + t0:Q0 + t0 + n],
                                                       start=(kc == 0), stop=(kc == 15)), reads=[r_wB, r_hT], writes=[r_p1],
                                 inc=(kc == 15))
                        K.op(act, lambda: A.activation(out=s0[:, 0:n], in_=p0[:, 0:n], func=AF.Sigmoid), reads=[r_p0], writes=[r_s0])
                        K.op(act, lambda: A.activation(out=s1[:, 0:n], in_=p1[:, 0:n], func=AF.Sigmoid), reads=[r_p1], writes=[r_s1])
                        K.op(dve, lambda: V.tensor_tensor(out=s0[:, 0:n], in0=pn[:, 0:n], in1=s0[:, 0:n], op=ALU.mult),
                             reads=[r_pn, r_s0], writes=[r_s0])
                        K.op(dve, lambda: V.tensor_tensor(out=s1[:, 0:n], in0=pd[:, 0:n], in1=s1[:, 0:n], op=ALU.mult),
                             reads=[r_pd, r_s1], writes=[r_s1])
                        K.op(dve, lambda: V.tensor_tensor(out=mergedT[:, nch, t0:t0 + n], in0=s0[:, 0:n], in1=s1[:, 0:n], op=ALU.add),
                             reads=[r_s0, r_s1], writes=[r_mT])
                w_o_v = w_o.rearrange("(kc p) n -> p kc n", p=128)
                u2 = 0
                oload = {}

                def o_load(cc_):
                    wO_, r_wO_ = wbufs[(32 + cc_) % 5]
                    wo3_ = wO_[:].rearrange("p (c n) -> p c n", n=256)
                    K.dma(pool, wo3_, w_o_v[:, :, cc_ * 256:(cc_ + 1) * 256], writes=[r_wO_])
                    oload[cc_] = (wo3_, r_wO_)
                o_load(0)
                o_load(1)
                for cc in range(8):
                    if cc + 2 < 8:
                        o_load(cc + 2)
                    wo3, r_wO = oload.pop(cc)
                    for qi in range(NQT):
                        xt, r_xt = xts[u2 % 3]
                        x1t, r_x1t = x1s[u2 % 3]
                        pq, r_pq = pss_[u2 % 8]
                        u2 += 1
                        K.dma(act, xt[:], xctx[Q0 + qi * 128:Q0 + (qi + 1) * 128, cc * 256:(cc + 1) * 256], writes=[r_xt])
                        for nch in range(16):
                            K.op(pe, lambda: PE.matmul(pq[:, 0:256], lhsT=mergedT[:, nch, qi * 128:(qi + 1) * 128], rhs=wo3[:, nch, :],
                                                       start=(nch == 0), stop=(nch == 15)), reads=[r_mT, r_wO], writes=[r_pq],
                                 inc=(nch == 15))
                        K.op(dve, lambda: V.tensor_tensor(out=x1t[:], in0=pq[:, 0:256], in1=g1bc[:, cc * 256:(cc + 1) * 256], op=ALU.mult),
                             reads=[r_pq, r_g1bc], writes=[r_x1t])
                        K.op(dve, lambda: V.tensor_tensor(out=x1t[:], in0=x1t[:], in1=xt[:], op=ALU.add), reads=[r_x1t, r_xt],
                             writes=[r_x1t])
                        K.dma(sp, x1_scr[qi * 128:(qi + 1) * 128, cc * 256:(cc + 1) * 256], x1t[:], reads=[r_x1t], writes=[r_x1scr],
                              semres=r_x1t)
                K.barrier()
        if stage <= 4:
            K.barrier()
            return nc, dbg

        with ExitStack() as stE:
            h2T, r_h2T = sbuf(stE, "h2T", [128, 16, NQ], BF16)
            K.op(dve, lambda: V.scalar_tensor_tensor(out=gcol[:, 16:32], in0=modcol[:, 64:80], scalar=1.0, in1=gl[:, 16:32],
                                                     op0=ALU.add, op1=ALU.mult), reads=[r_modcol, r_gl], writes=[r_gcol])
            with ExitStack() as st:
                norm_to_featmajor(st, lambda i: x1_scr[i * 128:(i + 1) * 128, :], NQT, h2T, r_h2T, 16, 48, 0, "n2",
                                  src_reads=[r_x1scr])
                K.barrier()
            with ExitStack() as st:
                acc, r_acc = sbuf(st, "acc", [128, 8, D], F32)
                g2bc, r_g2bc = sbuf(st, "g2bc", [128, D], F32)
                cvt, r_cvt = sbuf(st, "cvt", [128, 88, 4], F32)
                gT, r_gT = sbuf(st, "gT", [128, 11, 1024], BF16)
                wbufs = [sbuf(st, "we%d" % i, [128, 4096], BF16) for i in range(3)]
                uu = [sbuf(st, "uu%d" % i, [128, 1026], F32) for i in range(2)]
                cc_ = [sbuf(st, "cc%d" % i, [128, 1024], F32) for i in range(2)]
                sa, r_sa = sbuf(st, "sa", [128, 1024], F32)
                tmpd = [sbuf(st, "tmpd%d" % i, [128, 256], F32) for i in range(2)]
                pss_ = [psum(st, "pee%d" % i, [128, 512], F32) for i in range(8)]
                racc = [K.res("acc%d" % i) for i in range(8)]
                for i in range(8):
                    K.dma(sp, acc[:, i, :], x1_scr[128 + i * 128:128 + (i + 1) * 128, :], reads=[r_x1scr], writes=[racc[i]])
                K.dma(sp, g2bc[:], _dram_ap(g_scr, D, [[0, 128], [1, D]]), reads=[r_gscr], writes=[r_g2bc])
                K.dma(sp, cvt[:], conv_l, writes=[r_cvt])
                w_up_v = w_up.rearrange("(kc p) n -> p kc n", p=128)
                w_down_v = w_down.rearrange("(fc p) n -> p fc n", p=128)
                UCH = [(126, 342, 0), (468, 342, 342), (810, 342, 684)]
                pi = [0]
                estages = []
                for fp_ in range(4):
                    for fl_ in range(11):
                        estages.append(("up", fp_, fl_))
                    for cx_ in range(8):
                        estages.append(("down", fp_, cx_))
                eload = {}

                def e_load(si):
                    kind, fp_, x_ = estages[si]
                    wt_, r_wt_ = wbufs[si % 3]
                    if kind == "up":
                        fc_ = fp_ * 11 + x_
                        v0 = wt_[:, 0:2048].rearrange("p (c n) -> p c n", n=128)
                        v1 = wt_[:, 2048:4096].rearrange("p (c n) -> p c n", n=128)
                        K.dma(pool, v0, w_up_v[:, :, fc_ * 128:(fc_ + 1) * 128], writes=[r_wt_])
                        K.dma(pool, v1, w_up_v[:, :, DFF + fc_ * 128:DFF + (fc_ + 1) * 128], writes=[r_wt_])
                        eload[si] = ([v0, v1], r_wt_)
                    else:
                        v0 = wt_[:, 0:11 * 256].rearrange("p (c n) -> p c n", n=256)
                        K.dma(pool, v0, w_down_v[:, fp_ * 11:(fp_ + 1) * 11, x_ * 256:(x_ + 1) * 256], writes=[r_wt_])
                        eload[si] = (v0, r_wt_)

                def e_up(fp, fl, wu3, r_wU):
                    fc = fp * 11 + fl
                    for part in range(2):
                        u, r_u = uu[part]
                        cv_, r_cv = cc_[part]
                        ch = fc + 44 * part
                        for (ti, n, dc) in UCH:
                            pu, r_pu = pss_[pi[0] % 8]; pi[0] += 1
                            for kc in range(16):
                                K.op(pe, lambda: PE.matmul(pu[:, 0:n], lhsT=wu3[part][:, kc, :], rhs=h2T[:, kc, ti:ti + n],
                                                           start=(kc == 0), stop=(kc == 15)), reads=[r_wU, r_h2T], writes=[r_pu],
                                     inc=(kc == 15))
                            K.op(act, lambda: A.copy(out=u[:, dc:dc + n], in_=pu[:, 0:n]), reads=[r_pu], writes=[r_u])
                        K.op(dve, lambda: V.tensor_scalar(out=u[:, 0:2], in0=u[:, 0:2], scalar1=cm[:, 1:2], scalar2=None,
                                                          op0=ALU.mult), reads=[r_u, r_cm], writes=[r_u])
                        K.op(dve, lambda: V.tensor_scalar(out=cv_[:], in0=u[:, 2:1026], scalar1=cvt[:, ch, 2:3],
                                                          scalar2=cvt[:, ch, 3:4], op0=ALU.mult, op1=ALU.add),
                             reads=[r_u, r_cvt], writes=[r_cv])
                        K.op(dve, lambda: V.scalar_tensor_tensor(out=cv_[:], in0=u[:, 1:1025], scalar=cvt[:, ch, 1:2], in1=cv_[:],
                                                                 op0=ALU.mult, op1=ALU.add), reads=[r_u, r_cvt, r_cv],
                             writes=[r_cv])
                        K.op(dve, lambda: V.scalar_tensor_tensor(out=cv_[:], in0=u[:, 0:1024], scalar=cvt[:, ch, 0:1], in1=cv_[:],
                                                                 op0=ALU.mult, op1=ALU.add), reads=[r_u, r_cvt, r_cv],
                             writes=[r_cv])
                    K.op(act, lambda: A.activation(out=sa[:], in_=cc_[0][0][:], func=AF.Silu), reads=[cc_[0][1]], writes=[r_sa])
                    K.op(pool, lambda: G.tensor_tensor(out=gT[:, fl, :], in0=sa[:], in1=cc_[1][0][:], op=ALU.mult),
                         reads=[r_sa, cc_[1][1]], writes=[r_gT])

                def e_down(fp, cc, wd3, r_wD):
                    for i in range(8):
                        pq, r_pq = pss_[pi[0] % 8]; pi[0] += 1
                        td, r_td = tmpd[pi[0] % 2]
                        for fl in range(11):
                            K.op(pe, lambda: PE.matmul(pq[:, 0:256], lhsT=gT[:, fl, i * 128:(i + 1) * 128], rhs=wd3[:, fl, :],
                                                       start=(fl == 0), stop=(fl == 10)), reads=[r_gT, r_wD], writes=[r_pq],
                                 inc=(fl == 10))
                        K.op(dve, lambda: V.tensor_tensor(out=td[:], in0=pq[:, 0:256], in1=g2bc[:, cc * 256:(cc + 1) * 256],
                                                          op=ALU.mult), reads=[r_pq, r_g2bc], writes=[r_td])
                        K.op(dve, lambda: V.tensor_tensor(out=acc[:, i, cc * 256:(cc + 1) * 256],
                                                          in0=acc[:, i, cc * 256:(cc + 1) * 256], in1=td[:], op=ALU.add),
                             reads=[r_td, racc[i]], writes=[racc[i]])

                e_load(0)
                e_load(1)
                for si in range(len(estages)):
                    if si + 2 < len(estages):
                        e_load(si + 2)
                    kind, fp, x_ = estages[si]
                    wv_, r_wv_ = eload.pop(si)
                    if kind == "up":
                        e_up(fp, x_, wv_, r_wv_)
                    else:
                        e_down(fp, x_, wv_, r_wv_)
                for i in range(8):
                    K.dma(sp, out[i * 128:(i + 1) * 128, :], acc[:, i, :], reads=[racc[i]], writes=[r_out], semres=racc[i])
                K.barrier()
        print('ninst', K.ninst, 'nwaits', K.nwaits, 'nsem', K.nsem)
    return nc, dbg


def _t5_bucket_np(n):
    n = np.maximum(np.asarray(n, np.int64), 0)
    nf = np.maximum(n, 16).astype(np.float32)
    large = 16 + (np.log(nf / np.float32(16)) / np.float32(math.log(128 / 16)) * np.float32(16)).astype(np.int32)
    large = np.minimum(large, 31)
    return np.where(n < 16, n, large)


def _consts():
    d = np.arange(UL) - 2048
    oh = np.zeros((33, UL), np.float32)
    b = _t5_bucket_np(d)
    for i in range(UL):
        if d[i] < 0:
            oh[32, i] = 1.0
        else:
            oh[b[i], i] = 1.0
    e_mat = np.zeros((32, T), np.float32)
    for s in range(32):
        e_mat[s, s * 64:(s + 1) * 64] = 1.0
    starts = np.arange(127) * 16
    sel_start = np.arange(32) * 64
    ovl = np.clip(np.minimum(starts[:, None] + 32, sel_start[None, :] + 64) - np.maximum(starts[:, None], sel_start[None, :]),
                  0, None).astype(np.float32) / 16.0
    return oh, e_mat, ovl


def _core_masks(half):
    cm = np.zeros((128, 8), np.float32)
    cm[:, 0] = 0.0 if half == 1 else NEGV
    cm[:, 1] = 1.0 if half == 1 else 0.0
    if half == 0:
        cm[:64, 2] = NEGV
    shift = 0 if half == 1 else 1024
    t_loc = Q0 + np.arange(NQ)
    t_real = t_loc - shift
    cur = np.floor_divide(t_real, 64)
    j_real = np.arange(32)[None, :] - shift // 64
    valid = (j_real >= 0) & (j_real <= cur[:, None])
    forced = valid & ((j_real == 0) | (j_real == cur[:, None]) | (j_real == cur[:, None] - 1))
    mul = (valid & ~forced).astype(np.float32)
    add = np.where(forced, 1e4, np.where(valid, 0.0, -1e4)).astype(np.float32)
    sm = np.concatenate([mul, add], axis=1).reshape(NQT, 128, 64).transpose(1, 0, 2)
    return cm, np.ascontiguousarray(sm)


def _col_layout(v):
    return np.ascontiguousarray(np.asarray(v, np.float32).reshape(16, 128).T)


def make_in_maps(inp):
    x = np.asarray(inp["x"], np.float32)
    oh, e_mat, ovl = _consts()
    f = lambda k: np.ascontiguousarray(np.asarray(inp[k], np.float32)[0])
    qkg = np.zeros((128, 8), np.float32)
    qkg[:, 0] = f("nsa_q_gain")
    qkg[:, 1:4] = f("nsa_k_gain").T
    qkg[:, 4] = f("diff_q_gain")
    qkg[:, 5] = f("diff_k_gain")
    lam_qk = np.concatenate([f("diff_lambda_q").reshape(-1), f("diff_lambda_k").reshape(-1)])[None, :]
    cw = f("ffn_conv_w")
    cb = f("ffn_conv_b")
    conv = np.concatenate([cw, cb[None, :]], axis=0)
    conv_l = np.ascontiguousarray(conv.reshape(4, 88, 128).transpose(2, 1, 0))
    shared = {
        "w_ada": f("w_ada"), "b_ada": np.asarray(inp["b_ada"], np.float32).reshape(1, -1),
        "g1_l": _col_layout(f("norm1_gain")), "g2_l": _col_layout(f("norm2_gain")),
        "w_in": f("w_in"), "qkg": qkg, "cmp_pe": np.ascontiguousarray(f("cmp_pe").transpose(0, 2, 1)), "cmp_w1": f("cmp_w1"), "cmp_w2": f("cmp_w2"),
        "lam_qk": np.ascontiguousarray(lam_qk), "subln": f("diff_subln_gain")[None, :],
        "w_nsa_out": f("w_nsa_out"), "w_diff_out": f("w_diff_out"), "w_o": f("w_o"), "w_up": f("w_ffn_up"),
        "conv_l": conv_l, "w_down": f("w_ffn_down"), "rel_bias": np.asarray(inp["rel_bias"], np.float32),
        "oh_tab": oh, "e_mat": e_mat, "ovl": ovl,
    }
    maps = []
    for core in range(8):
        b, half = core // 2, core % 2
        if half == 1:
            xc = np.ascontiguousarray(x[b])
        else:
            xc = np.concatenate([np.zeros((1024, D), np.float32), x[b, :1024]], axis=0)
        cm, sm = _core_masks(half)
        m = dict(shared)
        m.update({"xctx": xc, "c_l": _col_layout(np.asarray(inp["c"], np.float32)[b]), "cmasks": cm, "selmask": sm})
        maps.append(m)
    return maps


_PROG = {}


def kernel(**inputs):
    if "p" not in _PROG:
        _PROG["p"] = build_program()[0]
    nc = _PROG["p"]
    maps = make_in_maps(inputs)
    res = run_bass_kernel_spmd(nc, maps, core_ids=list(range(8)))
    outp = np.zeros((4, T, D), np.float32)
    for core in range(8):
        b, half = core // 2, core % 2
        outp[b, half * 1024:(half + 1) * 1024] = res.results[core]["out"]
    return outp
```

```python
import os
import math
import numpy as np
from contextlib import ExitStack
import concourse.bass as bass
import concourse.mybir as mybir
from concourse.bass_utils import run_bass_kernel_spmd

F32 = mybir.dt.float32
BF16 = mybir.dt.bfloat16
ALU = mybir.AluOpType
AF = mybir.ActivationFunctionType
AX = mybir.AxisListType

D = 2048
T = 2048
DFF = 5632
NQT = 9
QT0 = 7
NQ = NQT * 128
Q0 = QT0 * 128
OFF_NSA_KV = 1024
OFF_NSA_G = 1024 + 1536
OFF_DQ = OFF_NSA_G + 24
OFF_DK = OFF_DQ + 1024
OFF_DV = OFF_DK + 1024
OFF_MG = OFF_DV + 1024
IN_COLS = OFF_MG + 4096
NEGV = -30000.0
EPS = 1e-6
UL = 4096


class Res:
    __slots__ = ("name", "w", "r", "dsem", "dcount")

    def __init__(self, name):
        self.name = name
        self.w = None
        self.r = {}
        self.dsem = None
        self.dcount = 0


class Eng:
    def __init__(self, name, e, sem):
        self.name = name
        self.e = e
        self.sem = sem
        self.count = 0
        self.seen = {}


class Kern:
    def __init__(self, nc, stack):
        self.nc = nc
        self.stack = stack
        self.nsem = 0
        self.pe = Eng("pe", nc.tensor, self.newsem("s_pe"))
        self.act = Eng("act", nc.scalar, self.newsem("s_act"))
        self.dve = Eng("dve", nc.vector, self.newsem("s_dve"))
        self.pool = Eng("pool", nc.gpsimd, self.newsem("s_pool"))
        self.sp = Eng("sp", nc.sync, self.newsem("s_sp"))
        self.engines = [self.pe, self.act, self.dve, self.pool, self.sp]
        self.dres = []
        self.nwaits = 0
        self.ninst = 0
        self.freesems = []

    def newsem(self, name):
        self.nsem += 1
        return self.stack.enter_context(self.nc.semaphore(name))

    def res(self, name):
        return Res(name)

    def _wait(self, E, tok):
        sem, val = tok
        key = id(sem)
        if E.seen.get(key, 0) < val:
            E.e.wait_ge(sem, val)
            E.seen[key] = val
            self.nwaits += 1

    def _needs(self, E, reads, writes, skip_sem=None):
        for r in reads:
            if r.w is not None and r.w[0] is not skip_sem:
                self._wait(E, r.w)
        for w in writes:
            if w.w is not None and w.w[0] is not skip_sem:
                self._wait(E, w.w)
            for sem, val in list(w.r.values()):
                if sem is not skip_sem:
                    self._wait(E, (sem, val))

    def _mark(self, tok, reads, writes):
        sem, val = tok
        for r in reads:
            old = r.r.get(id(sem))
            if old is None or old[1] < val:
                r.r[id(sem)] = (sem, val)
        for w in writes:
            w.w = tok
            w.r = {}

    def op(self, E, fn, reads=(), writes=(), inc=True):
        skip = E.sem if E is self.pe else None
        self._needs(E, reads, writes, skip_sem=skip)
        inst = fn()
        self.ninst += 1
        if inc:
            E.count += 1
            inst.then_inc(E.sem, 1)
            tok = (E.sem, E.count)
        else:
            tok = (E.sem, E.count + 1)
        self._mark(tok, reads, writes)
        return inst

    def dma(self, E, out, in_, reads=(), writes=(), semres=None, **kw):
        if semres is None:
            semres = writes[0] if writes else reads[0]
        if semres.dsem is None:
            semres.dsem = self.newsem("d_" + semres.name)
            self.dres.append(semres)
        self._needs(E, reads, writes, skip_sem=semres.dsem)
        inst = E.e.dma_start(out=out, in_=in_, **kw)
        self.ninst += 1
        semres.dcount += 16
        inst.then_inc(semres.dsem, 16)
        tok = (semres.dsem, semres.dcount)
        self._mark(tok, reads, writes)
        return inst

    def barrier(self):
        toks = []
        for E in self.engines:
            if E.count > 0:
                toks.append((E.sem, E.count))
        for r in self.dres:
            if r.dcount > 0:
                toks.append((r.dsem, r.dcount))
        for E in self.engines:
            for t in toks:
                if t[0] is E.sem and E is self.pe:
                    continue
                self._wait(E, t)


def _dram_ap(t, offset, ap):
    return bass.AP(tensor=t.tensor, offset=offset, ap=ap)


def build_program(stage=99, debug=False):
    nc = bass.Bass("TRN2", target_bir_lowering=False)
    dt_in = lambda name, shape: nc.dram_tensor(name, list(shape), F32, kind="ExternalInput").ap()
    xctx = dt_in("xctx", [T, D])
    c_l = dt_in("c_l", [128, 16])
    w_ada = dt_in("w_ada", [D, 6 * D])
    b_ada = dt_in("b_ada", [1, 6 * D])
    g1_l = dt_in("g1_l", [128, 16])
    g2_l = dt_in("g2_l", [128, 16])
    w_in = dt_in("w_in", [D, IN_COLS])
    qkg = dt_in("qkg", [128, 8])
    cmp_pe = dt_in("cmp_pe", [2, 128, 32])
    cmp_w1 = dt_in("cmp_w1", [2, 4096, 128])
    cmp_w2 = dt_in("cmp_w2", [2, 128, 128])
    lam_qk = dt_in("lam_qk", [1, 512])
    subln = dt_in("subln", [1, 256])
    w_nsa_out = dt_in("w_nsa_out", [1024, D])
    w_diff_out = dt_in("w_diff_out", [1024, D])
    w_o = dt_in("w_o", [D, D])
    w_up = dt_in("w_up", [D, 2 * DFF])
    conv_l = dt_in("conv_l", [128, 88, 4])
    w_down = dt_in("w_down", [DFF, D])
    rel_bias = dt_in("rel_bias", [32, 12])
    oh_tab = dt_in("oh_tab", [33, UL])
    cmasks = dt_in("cmasks", [128, 8])
    selmask = dt_in("selmask", [128, NQT, 64])
    e_mat = dt_in("e_mat", [32, T])
    ovl = dt_in("ovl", [127, 32])
    out = nc.dram_tensor("out", [1024, D], F32, kind="ExternalOutput").ap()
    dbg = {}

    def dbg_out(name, shape):
        dbg[name] = nc.dram_tensor("dbg_" + name, list(shape), F32, kind="ExternalOutput").ap()
        return dbg[name]

    x1_scr = nc.dram_tensor("x1_scr", [NQ, D], F32, kind="Internal").ap()
    u_scr = nc.dram_tensor("u_scr", [12, UL], BF16, kind="Internal").ap()
    g_scr = nc.dram_tensor("g_scr", [2, D], F32, kind="Internal").ap()

    with ExitStack() as top:
        K = Kern(nc, top)
        pe, act, dve, pool, sp = K.pe, K.act, K.dve, K.pool, K.sp
        V = nc.vector
        A = nc.scalar
        G = nc.gpsimd
        PE = nc.tensor

        def sbuf(st, name, shape, dt):
            t = st.enter_context(nc.sbuf_tensor(name, list(shape), dt))
            return t, K.res(name)

        def psum(st, name, shape, dt):
            t = st.enter_context(nc.psum_tensor(name, list(shape), dt))
            return t, K.res(name)

        r_out = K.res("out")
        r_x1scr = K.res("x1scr")
        r_uscr = K.res("uscr")

        ident_f, r_identf = sbuf(top, "ident_f", [128, 128], F32)
        ident, r_ident = sbuf(top, "ident", [128, 128], BF16)
        jmat, r_jmat = sbuf(top, "jmat", [128, 128], BF16)
        j127, r_j127 = sbuf(top, "j127", [128, 128], BF16)
        ones_bf, r_onesbf = sbuf(top, "ones_bf", [128, 128], BF16)
        ones_f, r_onesf = sbuf(top, "ones_f", [128, 128], F32)
        modcol, r_modcol = sbuf(top, "modcol", [128, 96], F32)
        gcol, r_gcol = sbuf(top, "gcol", [128, 32], F32)
        r_gscr = K.res("gscr")
        cm, r_cm = sbuf(top, "cm", [128, 8], F32)
        qkg_t, r_qkg = sbuf(top, "qkg_t", [128, 8], F32)
        tmpf, r_tmpf = sbuf(top, "tmpf", [128, 128], F32)
        scb, r_scb = sbuf(top, "scb", [128, 16], BF16)
        nlam, r_nlam = sbuf(top, "nlam", [128, 1], F32)
        sublbc, r_sublbc = sbuf(top, "sublbc", [128, 256], F32)
        gl, r_gl = sbuf(top, "gl", [128, 32], F32)

        K.dma(sp, cm[:], cmasks, writes=[r_cm])
        K.dma(sp, qkg_t[:], qkg, writes=[r_qkg])
        K.op(pool, lambda: G.memset(ident_f[:], 0.0), writes=[r_identf])
        K.op(pool, lambda: G.affine_select(out=ident_f[:], in_=ident_f[:], pattern=[[-1, 128]], compare_op=ALU.not_equal,
                                           fill=1.0, base=0, channel_multiplier=1), reads=[r_identf], writes=[r_identf])
        K.op(dve, lambda: V.tensor_copy(out=ident[:], in_=ident_f[:]), reads=[r_identf], writes=[r_ident])
        K.op(pool, lambda: G.memset(tmpf[:], 0.0), writes=[r_tmpf])
        K.op(pool, lambda: G.affine_select(out=tmpf[:], in_=tmpf[:], pattern=[[1, 128]], compare_op=ALU.not_equal,
                                           fill=1.0, base=-127, channel_multiplier=1), reads=[r_tmpf], writes=[r_tmpf])
        K.op(dve, lambda: V.tensor_copy(out=jmat[:], in_=tmpf[:]), reads=[r_tmpf], writes=[r_jmat])
        K.op(pool, lambda: G.memset(tmpf[:], 0.0), reads=[r_tmpf], writes=[r_tmpf])
        K.op(pool, lambda: G.affine_select(out=tmpf[:], in_=tmpf[:], pattern=[[1, 128]], compare_op=ALU.not_equal,
                                           fill=1.0, base=-126, channel_multiplier=1), reads=[r_tmpf], writes=[r_tmpf])
        K.op(dve, lambda: V.tensor_copy(out=j127[:], in_=tmpf[:]), reads=[r_tmpf], writes=[r_j127])
        K.op(dve, lambda: V.memset(ones_bf[:], 1.0), writes=[r_onesbf])
        K.op(dve, lambda: V.memset(ones_f[:], 1.0), writes=[r_onesf])
        sc_ = 128.0 ** -0.5
        K.op(dve, lambda: V.tensor_scalar(out=qkg_t[:, 0:1], in0=qkg_t[:, 0:1], scalar1=sc_, scalar2=None, op0=ALU.mult),
             reads=[r_qkg], writes=[r_qkg])
        K.op(dve, lambda: V.tensor_scalar(out=qkg_t[:, 4:5], in0=qkg_t[:, 4:5], scalar1=sc_, scalar2=None, op0=ALU.mult),
             reads=[r_qkg], writes=[r_qkg])

        with ExitStack() as st:
            cl, r_cl = sbuf(st, "cl", [128, 16], F32)
            wb = [sbuf(st, "wada%d" % i, [128, 16, 512], BF16) for i in range(2)]
            brow, r_brow = sbuf(st, "brow", [1, 512], F32)
            mrow = [sbuf(st, "mrow%d" % i, [1, 512], F32) for i in range(2)]
            pm = [psum(st, "pm%d" % i, [1, 512], F32) for i in range(2)]
            pc = [psum(st, "pc%d" % i, [128, 4], F32) for i in range(2)]
            K.dma(sp, cl[:], c_l, writes=[r_cl])
            K.dma(sp, gl[:, 0:16], g1_l, writes=[r_gl])
            K.dma(sp, gl[:, 16:32], g2_l, writes=[r_gl])
            K.op(act, lambda: A.activation(out=scb[:], in_=cl[:], func=AF.Silu), reads=[r_cl], writes=[r_scb])
            pu0 = [psum(st, "pu0_%d" % i, [128, 512], F32) for i in range(2)]
            pu0c = [0]

            def pu0_next():
                t_ = pu0[pu0c[0] % 2]
                pu0c[0] += 1
                return t_
            su = st
            rba, r_rba = sbuf(su, "rba", [33, 12], F32)
            b31, r_b31 = sbuf(su, "b31", [33, 12], F32)
            oht, r_oht = sbuf(su, "oht", [33, UL], F32)
            ub, r_ub = sbuf(su, "ub", [12, UL], BF16)
            K.op(dve, lambda: V.memset(rba[:], NEGV), writes=[r_rba])
            K.dma(sp, rba[0:32, :], rel_bias, writes=[r_rba])
            K.dma(sp, b31[0:32, :], _dram_ap(rel_bias, 31 * 12, [[0, 32], [1, 12]]), writes=[r_b31])
            K.dma(sp, oht[:], oh_tab, writes=[r_oht])
            K.op(dve, lambda: V.tensor_tensor(out=rba[0:32, :], in0=rba[0:32, :], in1=b31[0:32, :], op=ALU.subtract),
                 reads=[r_rba, r_b31], writes=[r_rba])
            for ch in range(UL // 512):
                pt_, r_pt_ = pu0_next()
                K.op(pe, lambda: PE.matmul(pt_[0:12, :], lhsT=rba[:, :], rhs=oht[:, ch * 512:(ch + 1) * 512], start=True,
                                           stop=True), reads=[r_rba, r_oht], writes=[r_pt_])
                K.op(dve, lambda: V.tensor_copy(out=ub[:, ch * 512:(ch + 1) * 512], in_=pt_[0:12, :]), reads=[r_pt_],
                     writes=[r_ub])
            K.dma(sp, u_scr, ub[:], reads=[r_ub], writes=[r_uscr], semres=r_ub)

            lamt, r_lamt = sbuf(st, "lamt", [1, 512], F32)
            lamw, r_lamw = sbuf(st, "lamw", [1, 8], F32)
            subl, r_subl = sbuf(st, "subl", [1, 256], F32)
            K.dma(sp, lamt[:], lam_qk, writes=[r_lamt])
            K.dma(sp, subl[:], subln, writes=[r_subl])
            K.op(dve, lambda: V.tensor_tensor(out=lamt[:, 0:256], in0=lamt[:, 0:256], in1=lamt[:, 256:512], op=ALU.mult),
                 reads=[r_lamt], writes=[r_lamt])
            K.op(dve, lambda: V.tensor_reduce(out=lamw[:, 0:2], in_=lamt[:, 0:256].rearrange("p (a b) -> p a b", b=128),
                                              axis=AX.X, op=ALU.add), reads=[r_lamt], writes=[r_lamw])
            K.op(act, lambda: A.activation(out=lamw[:, 2:4], in_=lamw[:, 0:2], func=AF.Exp), reads=[r_lamw], writes=[r_lamw])
            K.op(dve, lambda: V.tensor_tensor(out=lamw[:, 4:5], in0=lamw[:, 3:4], in1=lamw[:, 2:3], op=ALU.subtract),
                 reads=[r_lamw], writes=[r_lamw])
            K.op(dve, lambda: V.tensor_scalar(out=lamw[:, 4:5], in0=lamw[:, 4:5], scalar1=-0.2, scalar2=None, op0=ALU.add),
                 reads=[r_lamw], writes=[r_lamw])
            pt_, r_pt_ = pu0_next()
            K.op(pe, lambda: PE.matmul(pt_[:, 0:1], lhsT=ones_f[0:1, :], rhs=lamw[0:1, 4:5], start=True, stop=True),
                 reads=[r_onesf, r_lamw], writes=[r_pt_])
            K.op(dve, lambda: V.tensor_copy(out=nlam[:], in_=pt_[:, 0:1]), reads=[r_pt_], writes=[r_nlam])
            pt_, r_pt_ = pu0_next()
            K.op(pe, lambda: PE.matmul(pt_[:, 0:256], lhsT=ones_f[0:1, :], rhs=subl[0:1, :], start=True, stop=True),
                 reads=[r_onesf, r_subl], writes=[r_pt_])
            K.op(dve, lambda: V.tensor_scalar(out=sublbc[:], in0=pt_[:, 0:256], scalar1=0.8, scalar2=None, op0=ALU.mult),
                 reads=[r_pt_], writes=[r_sublbc])

            w_ada_v = w_ada.rearrange("(kc p) n -> p kc n", p=128)
            for ch in range(8):
                wt, r_wt = wb[ch % 2]
                K.dma(pool, wt[:], w_ada_v[:, :, ch * 512:(ch + 1) * 512], writes=[r_wt])
                K.dma(sp, brow[:], b_ada[:, ch * 512:(ch + 1) * 512], writes=[r_brow])
                pmt, r_pm = pm[ch % 2]
                for kc in range(16):
                    K.op(pe, lambda kc=kc: PE.matmul(pmt[:], lhsT=scb[:, kc:kc + 1], rhs=wt[:, kc, :], start=(kc == 0),
                                                     stop=(kc == 15)), reads=[r_scb, r_wt], writes=[r_pm], inc=(kc == 15))
                mr, r_mr = mrow[ch % 2]
                K.op(dve, lambda: V.tensor_tensor(out=mr[:], in0=pmt[:], in1=brow[:], op=ALU.add),
                     reads=[r_pm, r_brow], writes=[r_mr])
                pct, r_pc = pc[ch % 2]
                for j in range(4):
                    K.op(pe, lambda j=j: PE.matmul(pct[:, j:j + 1], lhsT=mr[0:1, j * 128:(j + 1) * 128], rhs=ones_f[0:1, 0:1],
                                                   start=True, stop=True), reads=[r_mr, r_onesf], writes=[r_pc], inc=(j == 3))
                K.op(dve, lambda: V.tensor_copy(out=modcol[:, ch * 4:(ch + 1) * 4], in_=pct[:]), reads=[r_pc], writes=[r_modcol])
            K.op(dve, lambda: V.scalar_tensor_tensor(out=gcol[:, 0:16], in0=modcol[:, 16:32], scalar=1.0, in1=gl[:, 0:16],
                                                     op0=ALU.add, op1=ALU.mult), reads=[r_modcol, r_gl], writes=[r_gcol])
            K.barrier()
        if debug:
            d = dbg_out("modcol", [128, 96])
            K.dma(sp, d, modcol[:], reads=[r_modcol], writes=[r_out], semres=r_modcol)

        def norm_to_featmajor(st, src_ap_fn, ntiles, dstT, r_dstT, gc0, shc0, tok0, tag, src_reads=()):
            xs = [sbuf(st, "%s_x%d" % (tag, i), [128, D], F32) for i in range(8)]
            xn = [sbuf(st, "%s_xn%d" % (tag, i), [128, D], BF16) for i in range(8)]
            junk, r_junk = sbuf(st, tag + "_junk", [128, D], BF16)
            sss = [sbuf(st, "%s_ss%d" % (tag, i), [128, 8], F32) for i in range(2)]
            pt = [psum(st, "%s_pt%d" % (tag, i), [128, 512], BF16) for i in range(4)]
            ngroups = (ntiles + 3) // 4
            ev = [0]

            def load_a(g):
                nt = min(4, ntiles - g * 4)
                for i in range(nt):
                    xt, r_xt = xs[(g % 2) * 4 + i]
                    K.dma(sp, xt[:], src_ap_fn(g * 4 + i), reads=list(src_reads), writes=[r_xt])

            def stage_a(g):
                nt = min(4, ntiles - g * 4)
                ss, r_ss = sss[g % 2]
                for i in range(nt):
                    xt, r_xt = xs[(g % 2) * 4 + i]
                    K.op(act, lambda: A.activation(out=junk[:], in_=xt[:], func=AF.Square, accum_out=ss[:, i:i + 1]),
                         reads=[r_xt], writes=[r_junk, r_ss])
                K.op(act, lambda: A.activation(out=ss[:, 4:4 + nt], in_=ss[:, 0:nt], func=AF.Sqrt, scale=1.0 / D, bias=EPS),
                     reads=[r_ss], writes=[r_ss])
                K.op(dve, lambda: V.reciprocal(out=ss[:, 4:4 + nt], in_=ss[:, 4:4 + nt]), reads=[r_ss], writes=[r_ss])
                for i in range(nt):
                    xt, r_xt = xs[(g % 2) * 4 + i]
                    xnt, r_xnt = xn[(g % 2) * 4 + i]
                    K.op(dve, lambda: V.tensor_scalar(out=xnt[:], in0=xt[:], scalar1=ss[:, 4 + i:5 + i], scalar2=None,
                                                      op0=ALU.mult), reads=[r_xt, r_ss], writes=[r_xnt])

            def stage_b(g):
                nt = min(4, ntiles - g * 4)
                for kc in range(16):
                    ptt, r_pt = pt[kc % 4]
                    for i in range(nt):
                        xnt, r_xnt = xn[(g % 2) * 4 + i]
                        K.op(pe, lambda: PE.transpose(out=ptt[:, i * 128:(i + 1) * 128], in_=xnt[:, kc * 128:(kc + 1) * 128],
                                                      identity=ident[:]), reads=[r_xnt, r_ident], writes=[r_pt],
                             inc=(i == nt - 1))
                    t0 = tok0 + g * 512
                    K.op(dve, lambda: V.tensor_scalar(out=dstT[:, kc, t0:t0 + nt * 128], in0=ptt[:, 0:nt * 128],
                                                      scalar1=gcol[:, gc0 + kc:gc0 + kc + 1],
                                                      scalar2=modcol[:, shc0 + kc:shc0 + kc + 1], op0=ALU.mult, op1=ALU.add),
                         reads=[r_pt, r_gcol, r_modcol], writes=[r_dstT])
                    ev[0] += 1

            load_a(0)
            if ngroups > 1:
                load_a(1)
            stage_a(0)
            for g in range(ngroups):
                stage_b(g)
                if g + 1 < ngroups:
                    stage_a(g + 1)
                if g + 2 < ngroups:
                    load_a(g + 2)

        with ExitStack() as stB:
            hT, r_hT = sbuf(stB, "hT", [128, 16, T], BF16)
            with ExitStack() as st:
                norm_to_featmajor(st, lambda i: xctx[i * 128:(i + 1) * 128, :], 16, hT, r_hT, 0, 0, 0, "n1")
                K.barrier()
            if debug and stage <= 1:
                d = dbg_out("hT", [128, 16 * T])
                hf, r_hf = sbuf(stB, "hf", [128, T], F32)
                for kc in range(16):
                    K.op(dve, lambda: V.tensor_copy(out=hf[:], in_=hT[:, kc, :]), reads=[r_hT], writes=[r_hf])
                    K.dma(sp, d[:, kc * T:(kc + 1) * T], hf[:], reads=[r_hf], writes=[r_out], semres=r_hf)
            if stage <= 1:
                K.barrier()
                return nc, dbg

            oT_nsa, r_oTn = sbuf(stB, "oT_nsa", [128, 8, NQ], BF16)
            oT_diff, r_oTd = sbuf(stB, "oT_diff", [128, 8, NQ], BF16)
            w_in_v = w_in.rearrange("(kc p) n -> p kc n", p=128)

            with ExitStack() as st:
                wbufs = [sbuf(st, "wb%d" % i, [128, 4096], BF16) for i in range(3)]
                wctr = [0]

                def wnext():
                    t = wbufs[wctr[0] % 3]
                    wctr[0] += 1
                    return t

                def wload_in(c0, ncols, dst3, r_w):
                    K.dma(pool, dst3, w_in_v[:, :, c0:c0 + ncols], writes=[r_w])

                def _kvcol(g_, br, kv):
                    return OFF_NSA_KV + ((br * 2 + kv) * 2 + g_) * 128
                wlist = []
                if stage >= 2:
                    for h_ in range(4):
                        wlist += [(OFF_DQ + h_ * 256, 256), (OFF_DK + h_ * 256, 256), (OFF_DV + h_ * 256, 256)]
                if stage >= 3:
                    wlist.append((OFF_NSA_G, 24))
                    for g_ in range(2):
                        wlist += [(_kvcol(g_, 0, 0), 128), (_kvcol(g_, 0, 1), 128)]
                        for br_ in (1, 2):
                            wlist += [(_kvcol(g_, br_, 0), 128), (_kvcol(g_, br_, 1), 128)]
                        wlist += [((4 * g_) * 128, 256), ((4 * g_ + 2) * 128, 256)]
                wstate = {"emitted": 0, "cur": 0, "released": -1}
                wloaded = {}

                def _ws_emit():
                    while wstate["emitted"] < len(wlist) and wstate["emitted"] - 3 <= wstate["released"]:
                        j = wstate["emitted"]
                        c0_, nc_ = wlist[j]
                        t_, r_ = wbufs[j % 3]
                        v3 = t_[:, 0:16 * nc_].rearrange("p (k c) -> p k c", c=nc_)
                        wload_in(c0_, nc_, v3, r_)
                        wloaded[j] = (v3, r_)
                        wstate["emitted"] += 1

                def ws_next(c0_expect):
                    i = wstate["cur"]
                    _ws_emit()
                    assert i in wloaded, (i, wstate)
                    assert wlist[i][0] == c0_expect, (i, wlist[i], c0_expect)
                    wstate["cur"] += 1
                    return wloaded.pop(i)

                def ws_release():
                    wstate["released"] = wstate["cur"] - 1
                    _ws_emit()

                sqb = [sbuf(st, "sqb%d" % i, [128, 512], BF16) for i in range(2)]
                rsb = [sbuf(st, "rsb%d" % i, [128, 512], F32) for i in range(2)]
                nctr = [0]
                ps_a = [psum(st, "ps_a%d" % i, [128, 512], F32) for i in range(2)]
                ps_b = [psum(st, "ps_b%d" % i, [128, 512], F32) for i in range(5)]
                ps_t, r_ps_t = psum(st, "ps_t", [128, 1024], BF16)
                pctr = [0]

                ps_rot = list(ps_a) + [ps_b[2], ps_b[3]]

                def ps_next():
                    t = ps_rot[pctr[0] % len(ps_rot)]
                    pctr[0] += 1
                    return t

                def qknorm(ps_ap, r_ps, n, gcol, dst_ap, r_dst, view=None):
                    i = nctr[0] % 2
                    nctr[0] += 1
                    sq, r_sq = sqb[i]
                    rs, r_rs = rsb[i]
                    pss, r_pss = ps_b[4]
                    K.op(act, lambda: A.activation(out=sq[:, 0:n], in_=ps_ap, func=AF.Square), reads=[r_ps], writes=[r_sq])
                    K.op(pe, lambda: PE.matmul(pss[:, 0:n], lhsT=ones_bf[:], rhs=sq[:, 0:n], start=True, stop=True),
                         reads=[r_sq, r_onesbf], writes=[r_pss])
                    K.op(act, lambda: A.activation(out=rs[:, 0:n], in_=pss[:, 0:n], func=AF.Ln, scale=1.0 / 128, bias=EPS),
                         reads=[r_pss], writes=[r_rs])
                    K.op(act, lambda: A.activation(out=rs[:, 0:n], in_=rs[:, 0:n], func=AF.Exp, scale=-0.5),
                         reads=[r_rs], writes=[r_rs])
                    vw = view if view is not None else (lambda a: a)
                    K.op(dve, lambda: V.scalar_tensor_tensor(out=dst_ap, in0=vw(ps_ap), scalar=gcol, in1=vw(rs[:, 0:n]),
                                                             op0=ALU.mult, op1=ALU.mult),
                         reads=[r_ps, r_rs, r_qkg], writes=[r_dst])

                def projT(ps_ap, r_ps, w3, r_w, c0, t0, n):
                    for kc in range(16):
                        K.op(pe, lambda: PE.matmul(ps_ap, lhsT=w3[:, kc, c0:c0 + 128], rhs=hT[:, kc, t0:t0 + n],
                                                   start=(kc == 0), stop=(kc == 15)), reads=[r_w, r_hT], writes=[r_ps],
                             inc=(kc == 15))

                QCH = [(Q0, 512), (Q0 + 512, 512), (Q0 + 1024, 128)]

                if stage >= 2:
                  with ExitStack() as sd:
                    kTd = [sbuf(sd, "kTd%d" % m, [128, T], BF16) for m in range(2)]
                    qTd = [sbuf(sd, "qTd%d" % m, [128, NQ], BF16) for m in range(2)]
                    vaug, r_vaug = sbuf(sd, "vaugd", [128, 16, 258], BF16)
                    bd, r_bd = sbuf(sd, "bd", [128, 256], BF16)
                    dslots = []
                    for s_ in range(2):
                        dslots.append({
                            "ptd": sbuf(sd, "ptd%d" % s_, [128, 16 * 256], BF16),
                            "od": sbuf(sd, "od%d" % s_, [128, 256], F32),
                            "odb": sbuf(sd, "odb%d" % s_, [128, 256], BF16),
                            "junk": sbuf(sd, "junkd%d" % s_, [128, 256], F32),
                            "sm": sbuf(sd, "smd%d" % s_, [128, 8], F32),
                            "po": [ps_b[2 * s_], ps_b[2 * s_ + 1]],
                            "r_pst": K.res("pst%d" % s_),
                            "tcol": s_ * 256,
                        })
                    s_tiles = [ps_a[0], ps_a[1], ps_b[4]]
                    sctr = [0]
                    bgw = [sbuf(sd, "bgw%d" % i, [128, 16, 256], BF16) for i in range(2)]
                    bgb = [sbuf(sd, "bgb%d" % i, [1, 256], F32) for i in range(2)]
                    bgm = [sbuf(sd, "bgm%d" % i, [1, 256], F32) for i in range(2)]
                    w_ada_v2 = w_ada.rearrange("(kc p) n -> p kc n", p=128)
                    bgst = {"loaded": 0, "done": 0}
                    bgps = [ps_b[0], ps_b[1]]
                    NBG = 32

                    def bg_load():
                        c = bgst["loaded"]
                        if c >= NBG:
                            return
                        col0 = 4096 + c * 256
                        K.dma(pool, bgw[c % 2][0][:], w_ada_v2[:, :, col0:col0 + 256], writes=[bgw[c % 2][1]])
                        K.dma(sp, bgb[c % 2][0][:], b_ada[:, col0:col0 + 256], writes=[bgb[c % 2][1]])
                        bgst["loaded"] += 1

                    def bg_finish(c):
                        mr_, r_mr_ = bgm[c % 2]
                        pS, r_pS = bgps[c % 2]
                        for j in range(2):
                            K.op(pe, lambda: PE.matmul(pS[:, 256 + j:257 + j], lhsT=mr_[0:1, j * 128:(j + 1) * 128], rhs=ones_f[0:1, 0:1],
                                                       start=True, stop=True), reads=[r_mr_, r_onesf], writes=[r_pS], inc=(j == 1))
                        K.op(dve, lambda: V.tensor_copy(out=modcol[:, 32 + 2 * c:34 + 2 * c], in_=pS[:, 256:258]), reads=[r_pS],
                             writes=[r_modcol])

                    def bg_step():
                        c = bgst["done"]
                        if c > NBG:
                            return
                        bgst["done"] += 1
                        if c >= 1:
                            bg_finish(c - 1)
                        if c == NBG:
                            return
                        wt_, r_wt_ = bgw[c % 2]
                        br_, r_br_ = bgb[c % 2]
                        mr_, r_mr_ = bgm[c % 2]
                        pS, r_pS = bgps[c % 2]
                        for kc in range(16):
                            K.op(pe, lambda: PE.matmul(pS[0:1, 0:256], lhsT=scb[:, kc:kc + 1], rhs=wt_[:, kc, :], start=(kc == 0),
                                                       stop=(kc == 15)), reads=[r_scb, r_wt_], writes=[r_pS], inc=(kc == 15))
                        K.op(dve, lambda: V.tensor_tensor(out=mr_[:], in0=pS[0:1, 0:256], in1=br_[:], op=ALU.add),
                             reads=[r_pS, r_br_], writes=[r_mr_])
                        if c < 8:
                            K.dma(sp, g_scr[0:1, c * 256:(c + 1) * 256], mr_[:], reads=[r_mr_], writes=[r_gscr], semres=r_mr_)
                        elif c >= 24:
                            K.dma(sp, g_scr[1:2, (c - 24) * 256:(c - 23) * 256], mr_[:], reads=[r_mr_], writes=[r_gscr], semres=r_mr_)
                        bg_load()

                    bg_load()
                    bg_load()
                    K.op(pool, lambda: G.memset(vaug[:, :, 256:258], 1.0), writes=[r_vaug])
                    for h in range(4):
                        wq3, r_wq = ws_next(OFF_DQ + h * 256)
                        wk3, r_wk = ws_next(OFF_DK + h * 256)
                        wv3, r_wv = ws_next(OFF_DV + h * 256)
                        K.dma(sp, bd[:], _dram_ap(u_scr, (8 + h) * UL + 2048 - 127, [[1, 128], [1, 256]]), reads=[r_uscr],
                              writes=[r_bd])
                        pslot = 0
                        for m in range(2):
                            for tcn in range(4):
                                pj, r_pj = ps_next()
                                projT(pj[:], r_pj, wk3, r_wk, m * 128, tcn * 512, 512)
                                qknorm(pj[:], r_pj, 512, qkg_t[:, 5:6], kTd[m][0][:, tcn * 512:(tcn + 1) * 512], kTd[m][1])
                                pslot += 1
                                if pslot % 2 == 0:
                                    bg_step()
                            for (t0, n) in QCH:
                                pj, r_pj = ps_next()
                                projT(pj[:, 0:n], r_pj, wq3, r_wq, m * 128, t0, n)
                                qknorm(pj[:, 0:n], r_pj, n, qkg_t[:, 4:5], qTd[m][0][:, t0 - Q0:t0 - Q0 + n], qTd[m][1])
                                pslot += 1
                                if pslot % 2 == 0:
                                    bg_step()
                        for i in range(16):
                            pj, r_pj = ps_next()
                            for kc in range(16):
                                K.op(pe, lambda: PE.matmul(pj[:, 0:256], lhsT=hT[:, kc, i * 128:(i + 1) * 128], rhs=wv3[:, kc, :],
                                                           start=(kc == 0), stop=(kc == 15)), reads=[r_hT, r_wv], writes=[r_pj],
                                     inc=(kc == 15))
                            if i % 2 == 0:
                                K.op(dve, lambda: V.tensor_copy(out=vaug[:, i, 0:256], in_=pj[:, 0:256]), reads=[r_pj],
                                     writes=[r_vaug])
                            else:
                                K.op(act, lambda: A.copy(out=vaug[:, i, 0:256], in_=pj[:, 0:256]), reads=[r_pj], writes=[r_vaug])
                            if i % 4 == 3:
                                bg_step()
                        ws_release()
                        def diff_unit(qi, sl):
                            S_ = dslots[sl]
                            ptd, r_ptd = S_["ptd"]
                            od, r_od = S_["od"]
                            odb, r_odb = S_["odb"]
                            junkd, r_junkd = S_["junk"]
                            sm, r_sm = S_["sm"]
                            po = S_["po"]
                            r_ptr = S_["r_pst"]
                            tc0 = S_["tcol"]
                            qb = QT0 + qi
                            nj = qb + 1
                            for jp in range(0, nj, 2):
                                pS, r_pS = s_tiles[sctr[0] % len(s_tiles)]
                                sctr[0] += 1
                                js = [j for j in (jp, jp + 1) if j < nj]
                                for jj, j in enumerate(js):
                                    near = j >= qb - 1
                                    for m in range(2):
                                        col = jj * 256 + m * 128
                                        K.op(pe, lambda: PE.matmul(pS[:, col:col + 128], lhsT=kTd[m][0][:, j * 128:(j + 1) * 128],
                                                                   rhs=qTd[m][0][:, qi * 128:(qi + 1) * 128], start=True,
                                                                   stop=not near), reads=[kTd[m][1], qTd[m][1]], writes=[r_pS],
                                             inc=not near)
                                        if near:
                                            off = 0 if j == qb else 128
                                            K.op(pe, lambda: PE.matmul(pS[:, col:col + 128], lhsT=jmat[:], rhs=bd[:, off:off + 128],
                                                                       start=False, stop=True), reads=[r_jmat, r_bd],
                                                 writes=[r_pS])
                                ncol = len(js) * 256
                                bias_ = cm[:, 0:1] if jp < 8 else 0.0
                                rd = [r_pS, r_cm] if jp < 8 else [r_pS]
                                K.op(act, lambda: A.activation(out=ptd[:, jp * 256:jp * 256 + ncol], in_=pS[:, 0:ncol], func=AF.Exp,
                                                               bias=bias_), reads=rd, writes=[r_ptd])
                                yield "s"
                            for m in range(2):
                                for j in range(nj):
                                    K.op(pe, lambda: PE.matmul(po[m][0][:, 0:257], lhsT=ptd[:, j * 256 + m * 128:j * 256 + m * 128 + 128],
                                                               rhs=vaug[:, j, 0:257], start=(j == 0), stop=(j == nj - 1)),
                                         reads=[r_ptd, r_vaug], writes=[po[m][1]], inc=(j == nj - 1))
                                    if j % 4 == 3:
                                        yield "av"
                                yield "av"
                            K.op(dve, lambda: V.tensor_scalar(out=sm[:, 0:1], in0=po[0][0][:, 256:257], scalar1=1e-20,
                                                              scalar2=None, op0=ALU.max), reads=[po[0][1]], writes=[r_sm])
                            K.op(dve, lambda: V.tensor_scalar(out=sm[:, 1:2], in0=po[1][0][:, 256:257], scalar1=1e-20,
                                                              scalar2=None, op0=ALU.max), reads=[po[1][1]], writes=[r_sm])
                            yield "e"
                            K.op(dve, lambda: V.reciprocal(out=sm[:, 0:2], in_=sm[:, 0:2]), reads=[r_sm], writes=[r_sm])
                            yield "e"
                            K.op(dve, lambda: V.tensor_tensor(out=sm[:, 2:3], in0=sm[:, 1:2], in1=nlam[:], op=ALU.mult),
                                 reads=[r_sm, r_nlam], writes=[r_sm])
                            yield "e"
                            K.op(dve, lambda: V.tensor_scalar(out=od[:], in0=po[0][0][:, 0:256], scalar1=sm[:, 0:1], scalar2=None,
                                                              op0=ALU.mult), reads=[po[0][1], r_sm], writes=[r_od])
                            yield "e"
                            K.op(dve, lambda: V.scalar_tensor_tensor(out=od[:], in0=po[1][0][:, 0:256], scalar=sm[:, 2:3], in1=od[:],
                                                                     op0=ALU.mult, op1=ALU.add), reads=[po[1][1], r_sm, r_od],
                                 writes=[r_od])
                            yield "e"
                            K.op(act, lambda: A.activation(out=junkd[:], in_=od[:], func=AF.Square, accum_out=sm[:, 3:4]),
                                 reads=[r_od], writes=[r_junkd, r_sm])
                            yield "e"
                            K.op(act, lambda: A.activation(out=sm[:, 4:5], in_=sm[:, 3:4], func=AF.Ln, scale=1.0 / 256, bias=EPS),
                                 reads=[r_sm], writes=[r_sm])
                            yield "e"
                            K.op(act, lambda: A.activation(out=sm[:, 4:5], in_=sm[:, 4:5], func=AF.Exp, scale=-0.5),
                                 reads=[r_sm], writes=[r_sm])
                            yield "e"
                            K.op(dve, lambda: V.scalar_tensor_tensor(out=odb[:], in0=od[:], scalar=sm[:, 4:5], in1=sublbc[:],
                                                                     op0=ALU.mult, op1=ALU.mult), reads=[r_od, r_sm, r_sublbc],
                                 writes=[r_odb])
                            yield "e"
                            for hf in range(2):
                                K.op(pe, lambda: PE.transpose(out=ps_t[:, tc0 + hf * 128:tc0 + (hf + 1) * 128],
                                                              in_=odb[:, hf * 128:(hf + 1) * 128], identity=ident[:]),
                                     reads=[r_odb, r_ident], writes=[r_ptr], inc=(hf == 1))
                            yield "e"
                            K.op(act, lambda: A.copy(out=oT_diff[:, 2 * h:2 * h + 2, qi * 128:(qi + 1) * 128],
                                                     in_=ps_t[:, tc0:tc0 + 256].rearrange("p (a b) -> p a b", b=128)),
                                 reads=[r_ptr], writes=[r_oTd])

                        active = []
                        free_slots = [0, 1]
                        nxt = 0
                        primed = False
                        while active or nxt < NQT:
                            while len(active) < 2 and nxt < NQT:
                                sl = free_slots.pop(0)
                                gnr = diff_unit(nxt, sl)
                                nxt += 1
                                if not primed and active == []:
                                    primed = True
                                    live = True
                                    try:
                                        while next(gnr) == "s":
                                            pass
                                    except StopIteration:
                                        live = False
                                    if live:
                                        active.append((gnr, sl))
                                    else:
                                        free_slots.append(sl)
                                else:
                                    active.append((gnr, sl))
                            for ent in list(active):
                                try:
                                    next(ent[0])
                                except StopIteration:
                                    active.remove(ent)
                                    free_slots.append(ent[1])
                    while bgst["done"] <= NBG:
                        bg_step()
                    K.barrier()

                if stage >= 3:
                  with ExitStack() as sn:
                    arena, _ = sbuf(sn, "arena", [128, 12288], BF16)
                    w1t = [(arena[:, 4096 + i * 4096:8192 + i * 4096].rearrange("p (l j) -> p l j", j=128), K.res("w1t%d" % i))
                           for i in range(2)]
                    w2t = [sbuf(sn, "w2t%d" % i, [128, 128], BF16) for i in range(2)]
                    pet = [sbuf(sn, "pet%d" % i, [128, 32], BF16) for i in range(2)]
                    cst, r_cst = sbuf(sn, "cst", [128, 2], F32)
                    ebf, r_ebf = sbuf(sn, "ebf", [32, T], BF16)
                    vcaug, r_vcaug = sbuf(sn, "vcaug", [128, 162], BF16)
                    gates, r_gates = sbuf(sn, "gates", [128, NQT, 24], F32)
                    smk, r_smk = sbuf(sn, "smk", [128, NQT, 64], F32)
                    w4t, r_w4t = sbuf(sn, "w4t", [128, 128], BF16)
                    K.dma(sp, smk[:], selmask, writes=[r_smk])
                    K.dma(pool, ebf[:], e_mat, writes=[r_ebf])
                    K.op(dve, lambda: V.memset(vcaug[:], 1.0), writes=[r_vcaug])
                    K.dma(pool, vcaug[0:127, 129:161], ovl, writes=[r_vcaug])
                    K.op(pool, lambda: G.memset(tmpf[:], 0.0), reads=[r_tmpf], writes=[r_tmpf])
                    K.op(pool, lambda: G.affine_select(out=tmpf[:], in_=tmpf[:], pattern=[[-1, 128]], compare_op=ALU.is_ge,
                                                       fill=NEGV, base=126, channel_multiplier=-1), reads=[r_tmpf], writes=[r_tmpf])
                    K.op(dve, lambda: V.tensor_copy(out=w4t[:], in_=tmpf[:]), reads=[r_tmpf], writes=[r_w4t])
                    for i in range(2):
                        K.dma(pool, w2t[i][0][:], cmp_w2[i], writes=[w2t[i][1]])
                        K.dma(pool, pet[i][0][:], cmp_pe[i], writes=[pet[i][1]])
                    wg3, r_wg = ws_next(OFF_NSA_G)
                    for qi in range(NQT):
                        pj, r_pj = ps_next()
                        for kc in range(16):
                            K.op(pe, lambda: PE.matmul(pj[:, 0:24], lhsT=hT[:, kc, Q0 + qi * 128:Q0 + (qi + 1) * 128], rhs=wg3[:, kc, :],
                                                       start=(kc == 0), stop=(kc == 15)), reads=[r_hT, r_wg], writes=[r_pj],
                                 inc=(kc == 15))
                        K.op(act, lambda: A.activation(out=gates[:, qi, :], in_=pj[:, 0:24], func=AF.Exp, scale=-1.0),
                             reads=[r_pj], writes=[r_gates])
                    ws_release()
                    K.op(dve, lambda: V.tensor_scalar(out=gates[:], in0=gates[:], scalar1=1.0, scalar2=None, op0=ALU.add),
                         reads=[r_gates], writes=[r_gates])
                    K.op(dve, lambda: V.reciprocal(out=gates[:], in_=gates[:]), reads=[r_gates], writes=[r_gates])

                    zT = [(arena[:, i * 2048:(i + 1) * 2048], K.res("zT%d" % i)) for i in range(2)]
                    kTn = [sbuf(sn, "kTn%d" % i, [128, T], BF16) for i in range(2)]
                    vaugn = [sbuf(sn, "vaugn%d" % i, [128, 16, 130], BF16) for i in range(2)]
                    qTn, r_qTn = sbuf(sn, "qTn", [128, NQT, 4, 128], BF16)
                    kcT, r_kcT = sbuf(sn, "kcT", [128, 128], BF16)
                    glu = [sbuf(sn, "glu%d" % i, [128, 128], BF16) for i in range(2)]
                    xg, r_xg = sbuf(sn, "xg", [128, 128], F32)
                    tg, r_tg = sbuf(sn, "tg", [128, 128], F32)
                    bn, r_bn = sbuf(sn, "bn", [128, 4, 256], BF16)
                    cbt = [sbuf(sn, "cbt%d" % i, [128, 4, 128], BF16) for i in range(2)]
                    pts, r_pts = arena[:, 0:8192].rearrange("p (a b) -> p a b", b=512), K.res("pts")
                    ptw, r_ptw = arena[:, 8192:8192 + 2560].rearrange("p (a b) -> p a b", b=512), K.res("ptw")
                    ptc, r_ptc = arena[:, 11264:11776], K.res("ptc")
                    oaccs = [sbuf(sn, "oacc%d" % i, [128, 4, 128], F32) for i in range(2)]
                    obf, r_obf = sbuf(sn, "obf", [128, 4, 128], BF16)
                    imp, r_imp = sbuf(sn, "imp", [128, 32], F32)
                    scr_, r_scr_ = sbuf(sn, "scr_", [128, 32], F32)
                    wk_, r_wk_ = sbuf(sn, "wk_", [128, 32], F32)
                    m8, r_m8 = sbuf(sn, "m8", [128, 16], F32)
                    negb, r_negb = sbuf(sn, "negb", [128, 32], BF16)
                    negT, r_negT = sbuf(sn, "negT", [32, 128], BF16)
                    smh, r_smh = sbuf(sn, "smh", [128, 16], F32)
                    smt, r_smt = sbuf(sn, "smt", [128, 16], F32)
                    r_pst_lo = K.res("pst_lo")
                    r_pst_hi = K.res("pst_hi")
                    impt, r_impt = sbuf(sn, "impt", [128, 4, 32], F32)
                    thr, r_thr = sbuf(sn, "thr", [128, 1], F32)
                    tmp2h = [sbuf(sn, "tmp2h%d" % i, [128, 2, 128], F32) for i in range(1)] * 2
                    tmp2t = [sbuf(sn, "tmp2t%d" % i, [128, 2, 128], F32) for i in range(1)] * 2
                    for i in range(2):
                        K.op(pool, lambda: G.memset(vaugn[i][0][:, :, 128:130], 1.0), writes=[vaugn[i][1]])

                    for g in range(2):
                        def kvcol(br, kv):
                            return OFF_NSA_KV + ((br * 2 + kv) * 2 + g) * 128
                        K.dma(sp, bn[:], _dram_ap(u_scr, (4 * g) * UL + 2048 - 127, [[1, 128], [UL, 4], [1, 256]]), reads=[r_uscr],
                              writes=[r_bn])
                        if g == 1:
                            K.barrier()
                            ps_rot[:] = list(ps_a) + [ps_b[2], ps_b[3]]
                        for i in range(2):
                            K.dma(pool, w1t[i][0], cmp_w1[i].rearrange("(l d) j -> d l j", d=128), writes=[w1t[i][1]])
                        if g == 0:
                            for i in range(2):
                                pt_, r_pt_ = ps_next()
                                for l in range(32):
                                    K.op(pe, lambda: PE.matmul(pt_[:, 0:1], lhsT=w1t[i][0][:, l, :], rhs=pet[i][0][:, l:l + 1],
                                                               start=(l == 0), stop=(l == 31)), reads=[w1t[i][1], pet[i][1]],
                                         writes=[r_pt_], inc=(l == 31))
                                K.op(dve, lambda: V.tensor_copy(out=cst[:, i:i + 1], in_=pt_[:, 0:1]), reads=[r_pt_], writes=[r_cst])
                        for i in range(2):
                            wz3, r_wz = ws_next(kvcol(0, i))
                            for tcn in range(4):
                                pj, r_pj = ps_next()
                                projT(pj[:], r_pj, wz3, r_wz, 0, tcn * 512, 512)
                                if tcn % 2 == 0:
                                    K.op(dve, lambda: V.tensor_copy(out=zT[i][0][:, tcn * 512:(tcn + 1) * 512], in_=pj[:]), reads=[r_pj],
                                         writes=[zT[i][1]])
                                else:
                                    K.op(act, lambda: A.copy(out=zT[i][0][:, tcn * 512:(tcn + 1) * 512], in_=pj[:]), reads=[r_pj],
                                         writes=[zT[i][1]])
                            if tcn == 3:
                                ws_release()
                        for i in range(2):
                            pj, r_pj = ps_next()
                            for l in range(32):
                                K.op(pe, lambda: PE.matmul(pj[:, 0:127], lhsT=w1t[i][0][:, l, :], rhs=zT[i][0][:, l:l + 16 * 126 + 1:16],
                                                           start=(l == 0), stop=(l == 31)), reads=[w1t[i][1], zT[i][1]], writes=[r_pj],
                                     inc=(l == 31))
                            K.op(dve, lambda: V.tensor_scalar(out=xg[:, 0:127], in0=pj[:, 0:127], scalar1=cst[:, i:i + 1], scalar2=None,
                                                              op0=ALU.add), reads=[r_pj, r_cst], writes=[r_xg])
                            K.op(dve, lambda: V.tensor_tensor(out=tg[:, 0:127], in0=xg[:, 0:127], in1=xg[:, 0:127], op=ALU.mult),
                                 reads=[r_xg], writes=[r_tg])
                            K.op(dve, lambda: V.tensor_scalar(out=tg[:, 0:127], in0=tg[:, 0:127], scalar1=0.044715, scalar2=1.0,
                                                              op0=ALU.mult, op1=ALU.add), reads=[r_tg], writes=[r_tg])
                            K.op(dve, lambda: V.tensor_tensor(out=tg[:, 0:127], in0=tg[:, 0:127], in1=xg[:, 0:127], op=ALU.mult),
                                 reads=[r_tg, r_xg], writes=[r_tg])
                            K.op(act, lambda: A.activation(out=tg[:, 0:127], in_=tg[:, 0:127], func=AF.Exp, scale=-1.5957691216),
                                 reads=[r_tg], writes=[r_tg])
                            K.op(dve, lambda: V.tensor_scalar(out=tg[:, 0:127], in0=tg[:, 0:127], scalar1=1.0, scalar2=None,
                                                              op0=ALU.add), reads=[r_tg], writes=[r_tg])
                            K.op(dve, lambda: V.reciprocal(out=tg[:, 0:127], in_=tg[:, 0:127]), reads=[r_tg], writes=[r_tg])
                            K.op(dve, lambda: V.tensor_tensor(out=glu[i][0][:, 0:127], in0=tg[:, 0:127], in1=xg[:, 0:127], op=ALU.mult),
                                 reads=[r_tg, r_xg], writes=[glu[i][1]])
                        pj, r_pj = ps_next()
                        K.op(pe, lambda: PE.matmul(pj[:, 0:127], lhsT=w2t[0][0][:], rhs=glu[0][0][:, 0:127], start=True, stop=True),
                             reads=[w2t[0][1], glu[0][1]], writes=[r_pj])
                        qknorm(pj[:, 0:127], r_pj, 127, qkg_t[:, 1:2], kcT[:, 0:127], r_kcT)
                        pj, r_pj = ps_next()
                        K.op(pe, lambda: PE.matmul(pj[0:127, 0:128], lhsT=glu[1][0][:, 0:127], rhs=w2t[1][0][:], start=True, stop=True),
                             reads=[w2t[1][1], glu[1][1]], writes=[r_pj])
                        K.op(dve, lambda: V.tensor_copy(out=vcaug[0:127, 0:128], in_=pj[0:127, 0:128]), reads=[r_pj], writes=[r_vcaug])
                        for bi, br in enumerate((1, 2)):
                            wz3, r_wz = ws_next(kvcol(br, 0))
                            for tcn in range(4):
                                pj, r_pj = ps_next()
                                projT(pj[:], r_pj, wz3, r_wz, 0, tcn * 512, 512)
                                qknorm(pj[:], r_pj, 512, qkg_t[:, 1 + br:2 + br], kTn[bi][0][:, tcn * 512:(tcn + 1) * 512], kTn[bi][1])
                            ws_release()
                            wz3, r_wz = ws_next(kvcol(br, 1))
                            for i in range(16):
                                pj, r_pj = ps_next()
                                for kc in range(16):
                                    K.op(pe, lambda: PE.matmul(pj[:, 0:128], lhsT=hT[:, kc, i * 128:(i + 1) * 128], rhs=wz3[:, kc, :],
                                                               start=(kc == 0), stop=(kc == 15)), reads=[r_hT, r_wz], writes=[r_pj],
                                         inc=(kc == 15))
                                if i % 2 == 0:
                                    K.op(dve, lambda: V.tensor_copy(out=vaugn[bi][0][:, i, 0:128], in_=pj[:, 0:128]), reads=[r_pj],
                                         writes=[vaugn[bi][1]])
                                else:
                                    K.op(act, lambda: A.copy(out=vaugn[bi][0][:, i, 0:128], in_=pj[:, 0:128]), reads=[r_pj],
                                         writes=[vaugn[bi][1]])
                            ws_release()
                        for hp in range(2):
                            wq3, r_wq = ws_next((4 * g + 2 * hp) * 128)
                            for hh in range(2):
                                hl = 2 * hp + hh
                                for (t0, n) in QCH:
                                    pj, r_pj = ps_next()
                                    projT(pj[:, 0:n], r_pj, wq3, r_wq, hh * 128, t0, n)
                                    q0 = (t0 - Q0) // 128
                                    qknorm(pj[:, 0:n], r_pj, n, qkg_t[:, 0:1], qTn[:, q0:q0 + n // 128, hl, :], r_qTn,
                                           view=lambda a: a.rearrange("p (a b) -> p a b", b=128))
                            ws_release()
                        K.barrier()
                        ps_rot[:] = list(ps_a) + [ps_b[4]]
                        def nsa_head(qi):
                            oacc_, r_oacc_ = oaccs[qi % 2]
                            qb = QT0 + qi
                            qrhs = qTn[:, qi].rearrange("p a b -> p (a b)")
                            cb, r_cb = cbt[qi % 2]
                            K.dma(sp, cb[0:127], _dram_ap(u_scr, (4 * g) * UL + 128 * qb + 1, [[16, 127], [UL, 4], [1, 128]]),
                                  reads=[r_uscr], writes=[r_cb])
                            pS, r_pS = ps_next()
                            K.op(pe, lambda: PE.matmul(pS[0:127, :], lhsT=kcT[:, 0:127], rhs=qrhs, start=True, stop=False),
                                 reads=[r_kcT, r_qTn], writes=[r_pS], inc=False)
                            K.op(pe, lambda: PE.matmul(pS[0:127, :], lhsT=j127[0:127, 0:127],
                                                       rhs=cb[0:127].rearrange("p a b -> p (a b)"), start=False, stop=True),
                                 reads=[r_j127, r_cb], writes=[r_pS])
                            K.op(act, lambda: A.activation(out=ptc[0:127, :], in_=pS[0:127, :], func=AF.Exp, bias=cm[0:127, 2:3]),
                                 reads=[r_pS, r_cm], writes=[r_ptc])
                            yield
                            pc_ = [ps_b[0], ps_b[1]]
                            for hl in range(4):
                                pcb, r_pcb = pc_[hl // 2]
                                c0 = (hl % 2) * 161
                                K.op(pe, lambda: PE.matmul(pcb[:, c0:c0 + 161], lhsT=ptc[0:127, hl * 128:(hl + 1) * 128],
                                                           rhs=vcaug[0:127, 0:161], start=True, stop=True), reads=[r_ptc, r_vcaug],
                                     writes=[r_pcb])
                            for b_ in range(2):
                                pcb, r_pcb = pc_[b_]
                                K.op(dve, lambda: V.tensor_scalar(out=smh[:, 2 * b_:2 * b_ + 2], in0=pcb[:, 128:290:161], scalar1=1e-20,
                                                                  scalar2=None, op0=ALU.max), reads=[r_pcb], writes=[r_smh])
                            yield
                            K.op(dve, lambda: V.reciprocal(out=smh[:, 0:4], in_=smh[:, 0:4]), reads=[r_smh], writes=[r_smh])
                            K.op(dve, lambda: V.tensor_tensor(out=smh[:, 4:8], in0=smh[:, 0:4], in1=gates[:, qi, 12 * g:12 * g + 10:3],
                                                              op=ALU.mult), reads=[r_smh, r_gates], writes=[r_smh])
                            for b_ in range(2):
                                pcb, r_pcb = pc_[b_]
                                pv = pcb[:, 0:322].rearrange("p (h c) -> p h c", c=161)
                                K.op(dve, lambda: V.tensor_tensor(out=oacc_[:, 2 * b_:2 * b_ + 2, :], in0=pv[:, :, 0:128],
                                                                  in1=smh[:, 4 + 2 * b_:6 + 2 * b_].unsqueeze(2).to_broadcast([128, 2, 128]),
                                                                  op=ALU.mult), reads=[r_pcb, r_smh], writes=[r_oacc_])
                                K.op(dve, lambda: V.tensor_tensor(out=impt[:, 2 * b_:2 * b_ + 2, :], in0=pv[:, :, 129:161],
                                                                  in1=smh[:, 2 * b_:2 * b_ + 2].unsqueeze(2).to_broadcast([128, 2, 32]),
                                                                  op=ALU.mult), reads=[r_pcb, r_smh], writes=[r_impt])
                            yield
                            K.op(dve, lambda: V.tensor_reduce(out=imp[:], in_=impt[:].rearrange("p h s -> p s h"), axis=AX.X, op=ALU.add),
                                 reads=[r_impt], writes=[r_imp])
                            for jj in range(5):
                                j = qb - 4 + jj
                                pS, r_pS = ps_next()
                                hasb = jj in (0, 3, 4)
                                K.op(pe, lambda: PE.matmul(pS[:], lhsT=kTn[1][0][:, j * 128:(j + 1) * 128], rhs=qrhs, start=True,
                                                           stop=not hasb), reads=[kTn[1][1], r_qTn], writes=[r_pS], inc=not hasb)
                                if jj == 0:
                                    K.op(pe, lambda: PE.matmul(pS[:], lhsT=jmat[:], rhs=w4t[:, 0:128].unsqueeze(1).to_broadcast([128, 4, 128]),
                                                               start=False, stop=True), reads=[r_jmat, r_w4t], writes=[r_pS])
                                elif jj >= 3:
                                    off = 0 if jj == 4 else 128
                                    K.op(pe, lambda: PE.matmul(pS[:], lhsT=jmat[:], rhs=bn[:, :, off:off + 128], start=False, stop=True),
                                         reads=[r_jmat, r_bn], writes=[r_pS])
                                bias_ = cm[:, 0:1] if j < 8 else 0.0
                                rd = [r_pS, r_cm] if j < 8 else [r_pS]
                                K.op(act, lambda: A.activation(out=ptw[:, jj, :], in_=pS[:], func=AF.Exp, bias=bias_), reads=rd,
                                     writes=[r_ptw])
                                yield
                            pw_ = [ps_b[0], ps_b[1]]
                            for hl in range(4):
                                pwb, r_pwb = pw_[hl // 2]
                                c0 = (hl % 2) * 129
                                for jj in range(5):
                                    j = qb - 4 + jj
                                    K.op(pe, lambda: PE.matmul(pwb[:, c0:c0 + 129], lhsT=ptw[:, jj, hl * 128:(hl + 1) * 128],
                                                               rhs=vaugn[1][0][:, j, 0:129], start=(jj == 0), stop=(jj == 4)),
                                         reads=[r_ptw, vaugn[1][1]], writes=[r_pwb], inc=(jj == 4))
                                yield
                            K.op(dve, lambda: V.tensor_tensor(out=scr_[:], in0=imp[:], in1=smk[:, qi, 0:32], op=ALU.mult),
                                 reads=[r_imp, r_smk], writes=[r_scr_])
                            K.op(dve, lambda: V.tensor_tensor(out=scr_[:], in0=scr_[:], in1=smk[:, qi, 32:64], op=ALU.add),
                                 reads=[r_scr_, r_smk], writes=[r_scr_])
                            yield
                            K.op(dve, lambda: V.max(out=m8[:, 0:8], in_=scr_[:]), reads=[r_scr_], writes=[r_m8])
                            K.op(dve, lambda: V.match_replace(out=wk_[:], in_to_replace=m8[:, 0:8], in_values=scr_[:], imm_value=-3e4),
                                 reads=[r_scr_, r_m8], writes=[r_wk_])
                            K.op(dve, lambda: V.max(out=m8[:, 8:16], in_=wk_[:]), reads=[r_wk_, r_m8], writes=[r_m8])
                            yield
                            K.op(dve, lambda: V.tensor_reduce(out=thr[:], in_=m8[:, 8:16], axis=AX.X, op=ALU.min), reads=[r_m8],
                                 writes=[r_thr])
                            K.op(dve, lambda: V.tensor_scalar(out=scr_[:], in0=scr_[:], scalar1=thr[:], scalar2=None, op0=ALU.is_ge),
                                 reads=[r_scr_, r_thr], writes=[r_scr_])
                            K.op(dve, lambda: V.tensor_scalar(out=negb[:], in0=scr_[:], scalar1=-NEGV, scalar2=NEGV, op0=ALU.mult,
                                                              op1=ALU.add), reads=[r_scr_], writes=[r_negb])
                            yield
                            ptnb, r_ptn = ps_t[:, 512:1024], r_pst_hi
                            K.op(pe, lambda: PE.transpose(out=ptnb[0:32, 0:128], in_=negb[:, 0:32], identity=ident[:]),
                                 reads=[r_negb, r_ident], writes=[r_ptn])
                            K.op(dve, lambda: V.tensor_copy(out=negT[:], in_=ptnb[0:32, 0:128]), reads=[r_ptn], writes=[r_negT])
                            yield
                            for b_ in range(2):
                                K.op(dve, lambda: V.tensor_scalar(out=smh[:, 8 + 2 * b_:10 + 2 * b_], in0=pw_[b_][0][:, 128:258:129],
                                                                  scalar1=1e-20, scalar2=None, op0=ALU.max), reads=[pw_[b_][1]],
                                     writes=[r_smh])
                            K.op(dve, lambda: V.reciprocal(out=smh[:, 8:12], in_=smh[:, 8:12]), reads=[r_smh], writes=[r_smh])
                            K.op(dve, lambda: V.tensor_tensor(out=smh[:, 8:12], in0=smh[:, 8:12], in1=gates[:, qi, 12 * g + 2:12 * g + 12:3],
                                                              op=ALU.mult), reads=[r_smh, r_gates], writes=[r_smh])
                            yield
                            for b_ in range(2):
                                pb_, r_pb_ = pw_[b_]
                                pv = pb_[:, 0:258].rearrange("p (h c) -> p h c", c=129)
                                t2, r_t2 = tmp2h[b_]
                                K.op(dve, lambda: V.tensor_tensor(out=t2[:], in0=pv[:, :, 0:128],
                                                                  in1=smh[:, 8 + 2 * b_:10 + 2 * b_].unsqueeze(2).to_broadcast([128, 2, 128]),
                                                                  op=ALU.mult), reads=[r_pb_, r_smh], writes=[r_t2])
                                K.op(dve, lambda: V.tensor_tensor(out=oacc_[:, 2 * b_:2 * b_ + 2, :], in0=oacc_[:, 2 * b_:2 * b_ + 2, :],
                                                                  in1=t2[:], op=ALU.add), reads=[r_oacc_, r_t2], writes=[r_oacc_])
                                yield
                            nj = qb + 1
                            for j in range(nj):
                                pS, r_pS = ps_next()
                                near = j >= qb - 1
                                K.op(pe, lambda: PE.matmul(pS[:], lhsT=kTn[0][0][:, j * 128:(j + 1) * 128], rhs=qrhs, start=True,
                                                           stop=False), reads=[kTn[0][1], r_qTn], writes=[r_pS], inc=False)
                                if j != qb:
                                    K.op(pe, lambda: PE.matmul(pS[:], lhsT=ebf[:, j * 128:(j + 1) * 128],
                                                               rhs=negT[:, 0:128].unsqueeze(1).to_broadcast([32, 4, 128]), start=False,
                                                               stop=not near), reads=[r_ebf, r_negT], writes=[r_pS], inc=not near)
                                if near:
                                    off = 0 if j == qb else 128
                                    K.op(pe, lambda: PE.matmul(pS[:], lhsT=jmat[:], rhs=bn[:, :, off:off + 128], start=False, stop=True),
                                         reads=[r_jmat, r_bn], writes=[r_pS])
                                bias_ = cm[:, 0:1] if j < 8 else 0.0
                                rd = [r_pS, r_cm] if j < 8 else [r_pS]
                                K.op(act, lambda: A.activation(out=pts[:, j, :], in_=pS[:], func=AF.Exp, bias=bias_), reads=rd,
                                     writes=[r_pts])
                                yield
                            px_ = [ps_b[2], ps_b[3]]
                            for hl in range(4):
                                pxb, r_pxb = px_[hl // 2]
                                c0 = (hl % 2) * 129
                                for j in range(nj):
                                    K.op(pe, lambda: PE.matmul(pxb[:, c0:c0 + 129], lhsT=pts[:, j, hl * 128:(hl + 1) * 128],
                                                               rhs=vaugn[0][0][:, j, 0:129], start=(j == 0), stop=(j == nj - 1)),
                                         reads=[r_pts, vaugn[0][1]], writes=[r_pxb], inc=(j == nj - 1))
                        def nsa_tail(qi):
                            oacc_, r_oacc_ = oaccs[qi % 2]
                            qb = QT0 + qi
                            px_ = [ps_b[2], ps_b[3]]
                            for b_ in range(2):
                                K.op(dve, lambda: V.tensor_scalar(out=smt[:, 12 + 2 * b_:14 + 2 * b_], in0=px_[b_][0][:, 128:258:129],
                                                                  scalar1=1e-20, scalar2=None, op0=ALU.max), reads=[px_[b_][1]],
                                     writes=[r_smt])
                            K.op(dve, lambda: V.reciprocal(out=smt[:, 12:16], in_=smt[:, 12:16]), reads=[r_smt], writes=[r_smt])
                            yield
                            K.op(dve, lambda: V.tensor_tensor(out=smt[:, 12:16], in0=smt[:, 12:16], in1=gates[:, qi, 12 * g + 1:12 * g + 11:3],
                                                              op=ALU.mult), reads=[r_smt, r_gates], writes=[r_smt])
                            yield
                            for bi, pp_ in ((1, px_),):
                                for b_ in range(2):
                                    pb_, r_pb_ = pp_[b_]
                                    pv = pb_[:, 0:258].rearrange("p (h c) -> p h c", c=129)
                                    t2, r_t2 = tmp2t[b_]
                                    cc0 = 8 + 4 * bi + 2 * b_
                                    K.op(dve, lambda: V.tensor_tensor(out=t2[:], in0=pv[:, :, 0:128],
                                                                      in1=smt[:, cc0:cc0 + 2].unsqueeze(2).to_broadcast([128, 2, 128]),
                                                                      op=ALU.mult), reads=[r_pb_, r_smt], writes=[r_t2])
                                    if bi == 0:
                                        K.op(dve, lambda: V.tensor_tensor(out=oacc_[:, 2 * b_:2 * b_ + 2, :], in0=oacc_[:, 2 * b_:2 * b_ + 2, :],
                                                                          in1=t2[:], op=ALU.add), reads=[r_oacc_, r_t2], writes=[r_oacc_])
                                    else:
                                        K.op(dve, lambda: V.tensor_tensor(out=obf[:, 2 * b_:2 * b_ + 2, :], in0=oacc_[:, 2 * b_:2 * b_ + 2, :],
                                                                          in1=t2[:], op=ALU.add), reads=[r_oacc_, r_t2], writes=[r_obf])
                                    yield
                            yield
                            ptrb, r_ptr = ps_t[:, 0:512], r_pst_lo
                            for hl in range(4):
                                K.op(pe, lambda: PE.transpose(out=ptrb[:, hl * 128:(hl + 1) * 128], in_=obf[:, hl, :], identity=ident[:]),
                                     reads=[r_obf, r_ident], writes=[r_ptr], inc=(hl == 3))
                            yield
                            K.op(act, lambda: A.copy(out=oT_nsa[:, 4 * g:4 * g + 4, qi * 128:(qi + 1) * 128],
                                                     in_=ptrb[:, 0:512].rearrange("p (a b) -> p a b", b=128)),
                                 reads=[r_ptr], writes=[r_oTn])
                        prev_tail = None
                        for qi in range(NQT):
                            hg = nsa_head(qi)
                            if prev_tail is not None:
                                t_alive = True
                                h_alive = True
                                while t_alive:
                                    if h_alive:
                                        try:
                                            next(hg)
                                        except StopIteration:
                                            h_alive = False
                                    try:
                                        next(prev_tail)
                                    except StopIteration:
                                        t_alive = False
                                if h_alive:
                                    for _ in hg:
                                        pass
                            else:
                                for _ in hg:
                                    pass
                            prev_tail = nsa_tail(qi)
                        for _ in prev_tail:
                            pass

                    K.barrier()
                K.barrier()

            if debug:
                of, r_of = sbuf(stB, "of", [128, 16 * NQ], F32)
                d = dbg_out("oT", [128, 16 * NQ])
                K.op(dve, lambda: V.tensor_copy(out=of[:, 0:8 * NQ], in_=oT_nsa[:].rearrange("p a b -> p (a b)")), reads=[r_oTn],
                     writes=[r_of])
                K.op(dve, lambda: V.tensor_copy(out=of[:, 8 * NQ:16 * NQ], in_=oT_diff[:].rearrange("p a b -> p (a b)")),
                     reads=[r_oTd], writes=[r_of])
                K.dma(sp, d, of[:], reads=[r_of], writes=[r_out], semres=r_of)
            if stage <= 3:
                K.barrier()
                return nc, dbg

            with ExitStack() as st:
                mergedT, r_mT = sbuf(st, "mergedT", [128, 16, NQ], BF16)
                wbufs = [sbuf(st, "wc%d" % i, [128, 4096], BF16) for i in range(5)]
                g1bc, r_g1bc = sbuf(st, "g1bc", [128, D], F32)
                sg = [sbuf(st, "sg%d" % i, [128, 512], F32) for i in range(4)]
                xts = [sbuf(st, "xts%d" % i, [128, 256], F32) for i in range(3)]
                x1s = [sbuf(st, "x1s%d" % i, [128, 256], F32) for i in range(3)]
                pss_ = [psum(st, "pcc%d" % i, [128, 512], F32) for i in range(8)]
                K.dma(sp, g1bc[:], _dram_ap(g_scr, 0, [[0, 128], [1, D]]), reads=[r_gscr], writes=[r_g1bc])
                wno_v = w_nsa_out.rearrange("(c p) n -> p c n", p=128)
                wdo_v = w_diff_out.rearrange("(c p) n -> p c n", p=128)
                TCH = [(0, 512), (512, 512), (1024, 128)]
                unit = 0
                wi = 0
                cload = {}

                def c_load(nch_):
                    wA_, r_wA_ = wbufs[(2 * nch_) % 5]
                    wa3_ = wA_[:, 0:1024].rearrange("p (c n) -> p c n", n=128)
                    wd3_ = wA_[:, 1024:2048].rearrange("p (c n) -> p c n", n=128)
                    K.dma(pool, wa3_, wno_v[:, :, nch_ * 128:(nch_ + 1) * 128], writes=[r_wA_])
                    K.dma(pool, wd3_, wdo_v[:, :, nch_ * 128:(nch_ + 1) * 128], writes=[r_wA_])
                    wB_, r_wB_ = wbufs[(2 * nch_ + 1) % 5]
                    wm0_ = wB_[:, 0:2048].rearrange("p (c n) -> p c n", n=128)
                    wm1_ = wB_[:, 2048:4096].rearrange("p (c n) -> p c n", n=128)
                    K.dma(pool, wm0_, w_in_v[:, :, OFF_MG + nch_ * 128:OFF_MG + (nch_ + 1) * 128], writes=[r_wB_])
                    K.dma(pool, wm1_, w_in_v[:, :, OFF_MG + D + nch_ * 128:OFF_MG + D + (nch_ + 1) * 128], writes=[r_wB_])
                    cload[nch_] = (wa3_, wd3_, r_wA_, wm0_, wm1_, r_wB_)
                c_load(0)
                for nch in range(16):
                    if nch + 1 < 16:
                        c_load(nch + 1)
                    wa3, wd3, r_wA, wm0, wm1, r_wB = cload.pop(nch)
                    for (t0, n) in TCH:
                        pset = pss_[(unit % 2) * 4:(unit % 2) * 4 + 4]
                        (pn, r_pn), (pd, r_pd), (p0, r_p0), (p1, r_p1) = pset
                        s0, r_s0 = sg[(unit % 2) * 2]
                        s1, r_s1 = sg[(unit % 2) * 2 + 1]
                        unit += 1
                        for c in range(8):
                            K.op(pe, lambda: PE.matmul(pn[:, 0:n], lhsT=wa3[:, c, :], rhs=oT_nsa[:, c, t0:t0 + n], start=(c == 0),
                                                       stop=(c == 7)), reads=[r_wA, r_oTn], writes=[r_pn], inc=(c == 7))
                        for c in range(8):
                            K.op(pe, lambda: PE.matmul(pd[:, 0:n], lhsT=wd3[:, c, :], rhs=oT_diff[:, c, t0:t0 + n], start=(c == 0),
                                                       stop=(c == 7)), reads=[r_wA, r_oTd], writes=[r_pd], inc=(c == 7))
                        for kc in range(16):
                            K.op(pe, lambda: PE.matmul(p0[:, 0:n], lhsT=wm0[:, kc, :], rhs=hT[:, kc, Q0 + t0:Q0 + t0 + n],
                                                       start=(kc == 0), stop=(kc == 15)), reads=[r_wB, r_hT], writes=[r_p0],
                                 inc=(kc == 15))
                        for kc in range(16):
                            K.op(pe, lambda: PE.matmul(p1[:, 0:n], lhsT=wm1[:, kc, :], rhs=hT[:, kc, Q0 + t0:Q0 + t0 + n],
                                                       start=(kc == 0), stop=(kc == 15)), reads=[r_wB, r_hT], writes=[r_p1],
                                 inc=(kc == 15))
                        K.op(act, lambda: A.activation(out=s0[:, 0:n], in_=p0[:, 0:n], func=AF.Sigmoid), reads=[r_p0], writes=[r_s0])
                        K.op(act, lambda: A.activation(out=s1[:, 0:n], in_=p1[:, 0:n], func=AF.Sigmoid), reads=[r_p1], writes=[r_s1])
                        K.op(dve, lambda: V.tensor_tensor(out=s0[:, 0:n], in0=pn[:, 0:n], in1=s0[:, 0:n], op=ALU.mult),
                             reads=[r_pn, r_s0], writes=[r_s0])
                        K.op(dve, lambda: V.tensor_tensor(out=s1[:, 0:n], in0=pd[:, 0:n], in1=s1[:, 0:n], op=ALU.mult),
                             reads=[r_pd, r_s1], writes=[r_s1])
                        K.op(dve, lambda: V.tensor_tensor(out=mergedT[:, nch, t0:t0 + n], in0=s0[:, 0:n], in1=s1[:, 0:n], op=ALU.add),
                             reads=[r_s0, r_s1], writes=[r_mT])
                w_o_v = w_o.rearrange("(kc p) n -> p kc n", p=128)
                u2 = 0
                oload = {}

                def o_load(cc_):
                    wO_, r_wO_ = wbufs[(32 + cc_) % 5]
                    wo3_ = wO_[:].rearrange("p (c n) -> p c n", n=256)
                    K.dma(pool, wo3_, w_o_v[:, :, cc_ * 256:(cc_ + 1) * 256], writes=[r_wO_])
                    oload[cc_] = (wo3_, r_wO_)
                o_load(0)
                o_load(1)
                for cc in range(8):
                    if cc + 2 < 8:
                        o_load(cc + 2)
                    wo3, r_wO = oload.pop(cc)
                    for qi in range(NQT):
                        xt, r_xt = xts[u2 % 3]
                        x1t, r_x1t = x1s[u2 % 3]
                        pq, r_pq = pss_[u2 % 8]
                        u2 += 1
                        K.dma(act, xt[:], xctx[Q0 + qi * 128:Q0 + (qi + 1) * 128, cc * 256:(cc + 1) * 256], writes=[r_xt])
                        for nch in range(16):
                            K.op(pe, lambda: PE.matmul(pq[:, 0:256], lhsT=mergedT[:, nch, qi * 128:(qi + 1) * 128], rhs=wo3[:, nch, :],
                                                       start=(nch == 0), stop=(nch == 15)), reads=[r_mT, r_wO], writes=[r_pq],
                                 inc=(nch == 15))
                        K.op(dve, lambda: V.tensor_tensor(out=x1t[:], in0=pq[:, 0:256], in1=g1bc[:, cc * 256:(cc + 1) * 256], op=ALU.mult),
                             reads=[r_pq, r_g1bc], writes=[r_x1t])
                        K.op(dve, lambda: V.tensor_tensor(out=x1t[:], in0=x1t[:], in1=xt[:], op=ALU.add), reads=[r_x1t, r_xt],
                             writes=[r_x1t])
                        K.dma(sp, x1_scr[qi * 128:(qi + 1) * 128, cc * 256:(cc + 1) * 256], x1t[:], reads=[r_x1t], writes=[r_x1scr],
                              semres=r_x1t)
                K.barrier()
        if stage <= 4:
            K.barrier()
            return nc, dbg

        with ExitStack() as stE:
            h2T, r_h2T = sbuf(stE, "h2T", [128, 16, NQ], BF16)
            K.op(dve, lambda: V.scalar_tensor_tensor(out=gcol[:, 16:32], in0=modcol[:, 64:80], scalar=1.0, in1=gl[:, 16:32],
                                                     op0=ALU.add, op1=ALU.mult), reads=[r_modcol, r_gl], writes=[r_gcol])
            with ExitStack() as st:
                norm_to_featmajor(st, lambda i: x1_scr[i * 128:(i + 1) * 128, :], NQT, h2T, r_h2T, 16, 48, 0, "n2",
                                  src_reads=[r_x1scr])
                K.barrier()
            with ExitStack() as st:
                acc, r_acc = sbuf(st, "acc", [128, 8, D], F32)
                g2bc, r_g2bc = sbuf(st, "g2bc", [128, D], F32)
                cvt, r_cvt = sbuf(st, "cvt", [128, 88, 4], F32)
                gT, r_gT = sbuf(st, "gT", [128, 11, 1024], BF16)
                wbufs = [sbuf(st, "we%d" % i, [128, 4096], BF16) for i in range(3)]
                uu = [sbuf(st, "uu%d" % i, [128, 1026], F32) for i in range(2)]
                cc_ = [sbuf(st, "cc%d" % i, [128, 1024], F32) for i in range(2)]
                sa, r_sa = sbuf(st, "sa", [128, 1024], F32)
                tmpd = [sbuf(st, "tmpd%d" % i, [128, 256], F32) for i in range(2)]
                pss_ = [psum(st, "pee%d" % i, [128, 512], F32) for i in range(8)]
                racc = [K.res("acc%d" % i) for i in range(8)]
                for i in range(8):
                    K.dma(sp, acc[:, i, :], x1_scr[128 + i * 128:128 + (i + 1) * 128, :], reads=[r_x1scr], writes=[racc[i]])
                K.dma(sp, g2bc[:], _dram_ap(g_scr, D, [[0, 128], [1, D]]), reads=[r_gscr], writes=[r_g2bc])
                K.dma(sp, cvt[:], conv_l, writes=[r_cvt])
                w_up_v = w_up.rearrange("(kc p) n -> p kc n", p=128)
                w_down_v = w_down.rearrange("(fc p) n -> p fc n", p=128)
                UCH = [(126, 342, 0), (468, 342, 342), (810, 342, 684)]
                pi = [0]
                estages = []
                for fp_ in range(4):
                    for fl_ in range(11):
                        estages.append(("up", fp_, fl_))
                    for cx_ in range(8):
                        estages.append(("down", fp_, cx_))
                eload = {}

                def e_load(si):
                    kind, fp_, x_ = estages[si]
                    wt_, r_wt_ = wbufs[si % 3]
                    if kind == "up":
                        fc_ = fp_ * 11 + x_
                        v0 = wt_[:, 0:2048].rearrange("p (c n) -> p c n", n=128)
                        v1 = wt_[:, 2048:4096].rearrange("p (c n) -> p c n", n=128)
                        K.dma(pool, v0, w_up_v[:, :, fc_ * 128:(fc_ + 1) * 128], writes=[r_wt_])
                        K.dma(pool, v1, w_up_v[:, :, DFF + fc_ * 128:DFF + (fc_ + 1) * 128], writes=[r_wt_])
                        eload[si] = ([v0, v1], r_wt_)
                    else:
                        v0 = wt_[:, 0:11 * 256].rearrange("p (c n) -> p c n", n=256)
                        K.dma(pool, v0, w_down_v[:, fp_ * 11:(fp_ + 1) * 11, x_ * 256:(x_ + 1) * 256], writes=[r_wt_])
                        eload[si] = (v0, r_wt_)

                def e_up(fp, fl, wu3, r_wU):
                    fc = fp * 11 + fl
                    for part in range(2):
                        u, r_u = uu[part]
                        cv_, r_cv = cc_[part]
                        ch = fc + 44 * part
                        for (ti, n, dc) in UCH:
                            pu, r_pu = pss_[pi[0] % 8]; pi[0] += 1
                            for kc in range(16):
                                K.op(pe, lambda: PE.matmul(pu[:, 0:n], lhsT=wu3[part][:, kc, :], rhs=h2T[:, kc, ti:ti + n],
                                                           start=(kc == 0), stop=(kc == 15)), reads=[r_wU, r_h2T], writes=[r_pu],
                                     inc=(kc == 15))
                            K.op(act, lambda: A.copy(out=u[:, dc:dc + n], in_=pu[:, 0:n]), reads=[r_pu], writes=[r_u])
                        K.op(dve, lambda: V.tensor_scalar(out=u[:, 0:2], in0=u[:, 0:2], scalar1=cm[:, 1:2], scalar2=None,
                                                          op0=ALU.mult), reads=[r_u, r_cm], writes=[r_u])
                        K.op(dve, lambda: V.tensor_scalar(out=cv_[:], in0=u[:, 2:1026], scalar1=cvt[:, ch, 2:3],
                                                          scalar2=cvt[:, ch, 3:4], op0=ALU.mult, op1=ALU.add),
                             reads=[r_u, r_cvt], writes=[r_cv])
                        K.op(dve, lambda: V.scalar_tensor_tensor(out=cv_[:], in0=u[:, 1:1025], scalar=cvt[:, ch, 1:2], in1=cv_[:],
                                                                 op0=ALU.mult, op1=ALU.add), reads=[r_u, r_cvt, r_cv],
                             writes=[r_cv])
                        K.op(dve, lambda: V.scalar_tensor_tensor(out=cv_[:], in0=u[:, 0:1024], scalar=cvt[:, ch, 0:1], in1=cv_[:],
                                                                 op0=ALU.mult, op1=ALU.add), reads=[r_u, r_cvt, r_cv],
                             writes=[r_cv])
                    K.op(act, lambda: A.activation(out=sa[:], in_=cc_[0][0][:], func=AF.Silu), reads=[cc_[0][1]], writes=[r_sa])
                    K.op(pool, lambda: G.tensor_tensor(out=gT[:, fl, :], in0=sa[:], in1=cc_[1][0][:], op=ALU.mult),
                         reads=[r_sa, cc_[1][1]], writes=[r_gT])

                def e_down(fp, cc, wd3, r_wD):
                    for i in range(8):
                        pq, r_pq = pss_[pi[0] % 8]; pi[0] += 1
                        td, r_td = tmpd[pi[0] % 2]
                        for fl in range(11):
                            K.op(pe, lambda: PE.matmul(pq[:, 0:256], lhsT=gT[:, fl, i * 128:(i + 1) * 128], rhs=wd3[:, fl, :],
                                                       start=(fl == 0), stop=(fl == 10)), reads=[r_gT, r_wD], writes=[r_pq],
                                 inc=(fl == 10))
                        K.op(dve, lambda: V.tensor_tensor(out=td[:], in0=pq[:, 0:256], in1=g2bc[:, cc * 256:(cc + 1) * 256],
                                                          op=ALU.mult), reads=[r_pq, r_g2bc], writes=[r_td])
                        K.op(dve, lambda: V.tensor_tensor(out=acc[:, i, cc * 256:(cc + 1) * 256],
                                                          in0=acc[:, i, cc * 256:(cc + 1) * 256], in1=td[:], op=ALU.add),
                             reads=[r_td, racc[i]], writes=[racc[i]])

                e_load(0)
                e_load(1)
                for si in range(len(estages)):
                    if si + 2 < len(estages):
                        e_load(si + 2)
                    kind, fp, x_ = estages[si]
                    wv_, r_wv_ = eload.pop(si)
                    if kind == "up":
                        e_up(fp, x_, wv_, r_wv_)
                    else:
                        e_down(fp, x_, wv_, r_wv_)
                for i in range(8):
                    K.dma(sp, out[i * 128:(i + 1) * 128, :], acc[:, i, :], reads=[racc[i]], writes=[r_out], semres=racc[i])
                K.barrier()
        print('ninst', K.ninst, 'nwaits', K.nwaits, 'nsem', K.nsem)
    return nc, dbg


def _t5_bucket_np(n):
    n = np.maximum(np.asarray(n, np.int64), 0)
    nf = np.maximum(n, 16).astype(np.float32)
    large = 16 + (np.log(nf / np.float32(16)) / np.float32(math.log(128 / 16)) * np.float32(16)).astype(np.int32)
    large = np.minimum(large, 31)
    return np.where(n < 16, n, large)


def _consts():
    d = np.arange(UL) - 2048
    oh = np.zeros((33, UL), np.float32)
    b = _t5_bucket_np(d)
    for i in range(UL):
        if d[i] < 0:
            oh[32, i] = 1.0
        else:
            oh[b[i], i] = 1.0
    e_mat = np.zeros((32, T), np.float32)
    for s in range(32):
        e_mat[s, s * 64:(s + 1) * 64] = 1.0
    starts = np.arange(127) * 16
    sel_start = np.arange(32) * 64
    ovl = np.clip(np.minimum(starts[:, None] + 32, sel_start[None, :] + 64) - np.maximum(starts[:, None], sel_start[None, :]),
                  0, None).astype(np.float32) / 16.0
    return oh, e_mat, ovl


def _core_masks(half):
    cm = np.zeros((128, 8), np.float32)
    cm[:, 0] = 0.0 if half == 1 else NEGV
    cm[:, 1] = 1.0 if half == 1 else 0.0
    if half == 0:
        cm[:64, 2] = NEGV
    shift = 0 if half == 1 else 1024
    t_loc = Q0 + np.arange(NQ)
    t_real = t_loc - shift
    cur = np.floor_divide(t_real, 64)
    j_real = np.arange(32)[None, :] - shift // 64
    valid = (j_real >= 0) & (j_real <= cur[:, None])
    forced = valid & ((j_real == 0) | (j_real == cur[:, None]) | (j_real == cur[:, None] - 1))
    mul = (valid & ~forced).astype(np.float32)
    add = np.where(forced, 1e4, np.where(valid, 0.0, -1e4)).astype(np.float32)
    sm = np.concatenate([mul, add], axis=1).reshape(NQT, 128, 64).transpose(1, 0, 2)
    return cm, np.ascontiguousarray(sm)


def _col_layout(v):
    return np.ascontiguousarray(np.asarray(v, np.float32).reshape(16, 128).T)


def make_in_maps(inp):
    x = np.asarray(inp["x"], np.float32)
    oh, e_mat, ovl = _consts()
    f = lambda k: np.ascontiguousarray(np.asarray(inp[k], np.float32)[0])
    qkg = np.zeros((128, 8), np.float32)
    qkg[:, 0] = f("nsa_q_gain")
    qkg[:, 1:4] = f("nsa_k_gain").T
    qkg[:, 4] = f("diff_q_gain")
    qkg[:, 5] = f("diff_k_gain")
    lam_qk = np.concatenate([f("diff_lambda_q").reshape(-1), f("diff_lambda_k").reshape(-1)])[None, :]
    cw = f("ffn_conv_w")
    cb = f("ffn_conv_b")
    conv = np.concatenate([cw, cb[None, :]], axis=0)
    conv_l = np.ascontiguousarray(conv.reshape(4, 88, 128).transpose(2, 1, 0))
    shared = {
        "w_ada": f("w_ada"), "b_ada": np.asarray(inp["b_ada"], np.float32).reshape(1, -1),
        "g1_l": _col_layout(f("norm1_gain")), "g2_l": _col_layout(f("norm2_gain")),
        "w_in": f("w_in"), "qkg": qkg, "cmp_pe": np.ascontiguousarray(f("cmp_pe").transpose(0, 2, 1)), "cmp_w1": f("cmp_w1"), "cmp_w2": f("cmp_w2"),
        "lam_qk": np.ascontiguousarray(lam_qk), "subln": f("diff_subln_gain")[None, :],
        "w_nsa_out": f("w_nsa_out"), "w_diff_out": f("w_diff_out"), "w_o": f("w_o"), "w_up": f("w_ffn_up"),
        "conv_l": conv_l, "w_down": f("w_ffn_down"), "rel_bias": np.asarray(inp["rel_bias"], np.float32),
        "oh_tab": oh, "e_mat": e_mat, "ovl": ovl,
    }
    maps = []
    for core in range(8):
        b, half = core // 2, core % 2
        if half == 1:
            xc = np.ascontiguousarray(x[b])
        else:
            xc = np.concatenate([np.zeros((1024, D), np.float32), x[b, :1024]], axis=0)
        cm, sm = _core_masks(half)
        m = dict(shared)
        m.update({"xctx": xc, "c_l": _col_layout(np.asarray(inp["c"], np.float32)[b]), "cmasks": cm, "selmask": sm})
        maps.append(m)
    return maps


_PROG = {}


def kernel(**inputs):
    if "p" not in _PROG:
        _PROG["p"] = build_program()[0]
    nc = _PROG["p"]
    maps = make_in_maps(inputs)
    res = run_bass_kernel_spmd(nc, maps, core_ids=list(range(8)))
    outp = np.zeros((4, T, D), np.float32)
    for core in range(8):
        b, half = core // 2, core % 2
        outp[b, half * 1024:(half + 1) * 1024] = res.results[core]["out"]
    return outp
```
